# Optimizing a Trainium2 kernel written in Bass

```python
import math
import jax, jax.numpy as jnp
from jax import lax
import numpy as np

D_MODEL = 1024
BATCH = 16
SEQ = 4096
DEPTH = 2
DEC_BATCH = 16
DEC_SEQ = 2048
PAST_LEN = 128

D_MIX = D_MODEL
N_GROUPS = 4
W_GROUP = D_MIX // N_GROUPS
H_A = 4
NOPE_A = 64
ROPE_A = 32
V_A = W_GROUP // H_A
Q_LORA = 192
KV_LORA = 128
H_B = 4
DK_B = 32
DV_B = W_GROUP // H_B
W_C = W_GROUP
H_C = 4
BW_C = W_C // H_C
CONV_W = 4
CONV_PAD_L = CONV_W // 2
LRU_C = 8.0
H_D = 4
KV_H_D = 2
HD_D = W_GROUP // H_D
GRID_W = 64
ROPE_THETA = 500000.0
ROPE_FRAC = 4
MLA_ROPE_THETA = 10000.0
AXIAL_THETA = 10000.0
Q_BLOCK = 128
EPS = 1e-6
IN_SIZES = (Q_LORA, KV_LORA, ROPE_A, W_GROUP,
            H_B * 2 * DK_B, H_B * 2 * DK_B, H_B * DV_B, W_GROUP,
            W_C, W_GROUP,
            H_D * HD_D, KV_H_D * HD_D, KV_H_D * HD_D, W_GROUP)
N_IN = sum(IN_SIZES)

kernel_name = 'hybrid_hymba_encoder_two_groups'


def rms_norm(x, g):
    xf = x.astype(jnp.float32)
    y = xf * lax.rsqrt(jnp.mean(xf * xf, axis=-1, keepdims=True) + EPS)
    return (y * g.astype(jnp.float32)).astype(x.dtype)


def rope(x, pos, theta):
    half = x.shape[-1] // 2
    inv = jnp.power(jnp.float32(theta), -jnp.arange(half, dtype=jnp.float32) / half)
    ang = pos[:, None] * inv[None, :]
    cos = jnp.cos(ang)[None, :, None, :]
    sin = jnp.sin(ang)[None, :, None, :]
    xf = x.astype(jnp.float32)
    x1, x2 = xf[..., :half], xf[..., half:]
    return jnp.concatenate([x1 * cos - x2 * sin, x2 * cos + x1 * sin], axis=-1).astype(x.dtype)


def partial_rope(x, pos):
    r = x.shape[-1] // ROPE_FRAC
    return jnp.concatenate([rope(x[..., :r], pos, ROPE_THETA), x[..., r:]], axis=-1)


def axial_rope(x, row, col):
    half = x.shape[-1] // 2
    return jnp.concatenate([rope(x[..., :half], row, AXIAL_THETA),
                            rope(x[..., half:], col, AXIAL_THETA)], axis=-1)


def blocked_attention(q, k, v, scale):
    b, s, h, dk = q.shape
    g = k.shape[2]
    rep = h // g
    dv = v.shape[-1]
    nb = s // Q_BLOCK
    qb = q.reshape(b, nb, Q_BLOCK, g, rep, dk).transpose(1, 0, 2, 3, 4, 5)

    def one_block(qblk):
        sc = jnp.einsum('bqgrd,bkgd->bgrqk', qblk, k, preferred_element_type=jnp.float32) * scale
        p = jax.nn.softmax(sc, axis=-1).astype(v.dtype)
        return jnp.einsum('bgrqk,bkgv->bqgrv', p, v)

    out = lax.map(one_block, qb)
    return out.transpose(1, 0, 2, 3, 4, 5).reshape(b, s, h, dv)


def mla_branch(q_lat, kv_lat, k_rope, pos, q_norm, w_uq, kv_norm, w_ukv):
    b, s, _ = q_lat.shape
    q = (rms_norm(q_lat, q_norm) @ w_uq).reshape(b, s, H_A, NOPE_A + ROPE_A)
    kv = (rms_norm(kv_lat, kv_norm) @ w_ukv).reshape(b, s, H_A, NOPE_A + V_A)
    q_nope, q_pe = q[..., :NOPE_A], q[..., NOPE_A:]
    k_nope, v = kv[..., :NOPE_A], kv[..., NOPE_A:]
    q_pe = rope(q_pe, pos, MLA_ROPE_THETA)
    k_pe = rope(k_rope[:, :, None, :], pos, MLA_ROPE_THETA)
    qf = jnp.concatenate([q_nope, q_pe], axis=-1)
    kf = jnp.concatenate([k_nope, jnp.broadcast_to(k_pe, (b, s, H_A, ROPE_A))], axis=-1)
    o = blocked_attention(qf, kf, v, (NOPE_A + ROPE_A) ** -0.5)
    return o.reshape(b, s, H_A * V_A)


def diff_branch(q, k, v, pos, lam, subln, layer_idx):
    b, s, _ = q.shape
    q = partial_rope(q.reshape(b, s, 2 * H_B, DK_B), pos).reshape(b, s, H_B, 2, DK_B)
    k = partial_rope(k.reshape(b, s, 2 * H_B, DK_B), pos).reshape(b, s, H_B, 2, DK_B)
    v = v.reshape(b, s, H_B, DV_B)
    lam_init = 0.8 - 0.6 * math.exp(-0.3 * layer_idx)
    lf = lam.astype(jnp.float32)
    lam_full = jnp.exp(jnp.sum(lf[0] * lf[1])) - jnp.exp(jnp.sum(lf[2] * lf[3])) + lam_init
    o1 = blocked_attention(q[..., 0, :], k[..., 0, :], v, DK_B ** -0.5)
    o2 = blocked_attention(q[..., 1, :], k[..., 1, :], v, DK_B ** -0.5)
    o = o1 - lam_full.astype(o1.dtype) * o2
    o = rms_norm(o, subln) * (1.0 - lam_init)
    return o.reshape(b, s, H_B * DV_B)


def blockdiag(x, w):
    b, s, _ = x.shape
    return jnp.einsum('bshi,hij->bshj', x.reshape(b, s, H_C, BW_C), w).reshape(b, s, W_C)


def scan_combine(e1, e2):
    a1, b1 = e1
    a2, b2 = e2
    return a1 * a2, a2 * b1 + b2


def rglru_branch(xc, conv_w, conv_b, wa, ba, wx, bx, lam):
    b, s, _ = xc.shape
    xp = jnp.pad(xc, ((0, 0), (CONV_PAD_L, CONV_W - 1 - CONV_PAD_L), (0, 0)))
    xconv = conv_b + xp[:, 0:s] * conv_w[0]
    for j in range(1, CONV_W):
        xconv = xconv + xp[:, j:j + s] * conv_w[j]

    def direction(d, reverse):
        r = jax.nn.sigmoid(blockdiag(xconv, wa[d]) + ba[d]).astype(jnp.float32)
        i = jax.nn.sigmoid(blockdiag(xconv, wx[d]) + bx[d])
        log_a = -LRU_C * r * jax.nn.softplus(-lam[d].astype(jnp.float32))
        a = jnp.exp(log_a)
        gx = jnp.sqrt(-jnp.expm1(2.0 * log_a)) * (i * xconv).astype(jnp.float32)
        _, h = lax.associative_scan(scan_combine, (a, gx), axis=1, reverse=reverse)
        return h

    h = direction(0, False) + direction(1, True)
    return h.astype(xc.dtype)


def gqa_branch(q, k, v, row, col, q_norm, k_norm):
    b, s, _ = q.shape
    q = axial_rope(rms_norm(q.reshape(b, s, H_D, HD_D), q_norm), row, col)
    k = axial_rope(rms_norm(k.reshape(b, s, KV_H_D, HD_D), k_norm), row, col)
    v = v.reshape(b, s, KV_H_D, HD_D)
    o = blocked_attention(q, k, v, HD_D ** -0.5)
    return o.reshape(b, s, H_D * HD_D)


def layer(x, c, l, p, pos, row, col):
    mod = jax.nn.silu(c) @ p['ada_w'][l] + p['ada_b'][l]
    shift, scale, gate = jnp.split(mod, 3, axis=-1)
    h = rms_norm(x, p['norm_g'][l]) * (1.0 + scale[:, None, :]) + shift[:, None, :]
    z = h @ p['w_in'][l]
    points = np.cumsum(IN_SIZES)[:-1].tolist()
    (qa, kva, kra, ga, qb, kb, vb, gb, xc, gc, qd, kd, vd, gd) = jnp.split(z, points, axis=-1)
    oa = mla_branch(qa, kva, kra, pos, p['mla_q_norm'][l], p['mla_w_uq'][l],
                    p['mla_kv_norm'][l], p['mla_w_ukv'][l])
    ob = diff_branch(qb, kb, vb, pos, p['diff_lambda'][l], p['diff_subln'][l], l)
    oc = rglru_branch(xc, p['lru_conv_w'][l], p['lru_conv_b'][l], p['lru_wa'][l], p['lru_ba'][l],
                      p['lru_wx'][l], p['lru_bx'][l], p['lru_lambda'][l])
    od = gqa_branch(qd, kd, vd, row, col, p['gqa_q_norm'][l], p['gqa_k_norm'][l])
    o = jnp.concatenate([oa * jax.nn.silu(ga), ob * jax.nn.silu(gb),
                         oc * jax.nn.silu(gc), od * jax.nn.silu(gd)], axis=-1)
    return x + gate[:, None, :] * (o @ p['w_out'][l])


def trunk(x, c, p):
    s = x.shape[1]
    rows = s // GRID_W
    pos = jnp.arange(s, dtype=jnp.float32)
    row = jnp.repeat(jnp.arange(rows, dtype=jnp.float32), GRID_W)
    col = jnp.tile(jnp.arange(GRID_W, dtype=jnp.float32), rows)
    for l in range(DEPTH):
        x = layer(x, c, l, p, pos, row, col)
    return rms_norm(x, p['final_norm'])


def setup_inputs(seed: int = 0) -> dict:
    key = jax.random.key(seed)
    ks = jax.random.split(key, 25)
    f32 = jnp.float32

    def nrm(k, shape, scale):
        return jax.random.normal(k, shape, f32) * scale

    def gain(k, shape):
        return 1.0 + 0.02 * jax.random.normal(k, shape, f32)

    u = jax.random.uniform(ks[20], (DEPTH, 2, W_C), f32, 0.9, 0.999)
    a_base = u ** (1.0 / LRU_C)
    lru_lambda = jnp.log(a_base) - jnp.log1p(-a_base)
    return {
        'x_prompt': nrm(ks[0], (BATCH, SEQ, D_MODEL), 1.0),
        'x_sample': nrm(ks[1], (DEC_BATCH, DEC_SEQ, D_MODEL), 1.0),
        'c_prompt': nrm(ks[2], (BATCH, D_MODEL), 1.0),
        'c_sample': nrm(ks[3], (DEC_BATCH, D_MODEL), 1.0),
        'ada_w': nrm(ks[4], (DEPTH, D_MODEL, 3 * D_MODEL), 0.5 * D_MODEL ** -0.5),
        'ada_b': nrm(ks[5], (DEPTH, 3 * D_MODEL), 0.02),
        'norm_g': gain(ks[6], (DEPTH, D_MODEL)),
        'w_in': nrm(ks[7], (DEPTH, D_MODEL, N_IN), D_MODEL ** -0.5),
        'mla_q_norm': gain(ks[8], (DEPTH, Q_LORA)),
        'mla_w_uq': nrm(ks[9], (DEPTH, Q_LORA, H_A * (NOPE_A + ROPE_A)), Q_LORA ** -0.5),
        'mla_kv_norm': gain(ks[10], (DEPTH, KV_LORA)),
        'mla_w_ukv': nrm(ks[11], (DEPTH, KV_LORA, H_A * (NOPE_A + V_A)), KV_LORA ** -0.5),
        'diff_lambda': nrm(ks[12], (DEPTH, 4, DK_B), 0.1),
        'diff_subln': gain(ks[13], (DEPTH, DV_B)),
        'lru_conv_w': nrm(ks[14], (DEPTH, CONV_W, W_C), CONV_W ** -0.5),
        'lru_conv_b': nrm(ks[15], (DEPTH, W_C), 0.02),
        'lru_wa': nrm(ks[16], (DEPTH, 2, H_C, BW_C, BW_C), BW_C ** -0.5),
        'lru_ba': nrm(ks[17], (DEPTH, 2, W_C), 0.02),
        'lru_wx': nrm(ks[18], (DEPTH, 2, H_C, BW_C, BW_C), BW_C ** -0.5),
        'lru_bx': nrm(ks[19], (DEPTH, 2, W_C), 0.02),
        'lru_lambda': lru_lambda,
        'gqa_q_norm': gain(ks[21], (DEPTH, HD_D)),
        'gqa_k_norm': gain(ks[22], (DEPTH, HD_D)),
        'w_out': nrm(ks[23], (DEPTH, D_MIX, D_MODEL), D_MIX ** -0.5),
        'final_norm': gain(ks[24], (D_MODEL,)),
    }


def reference(x_prompt, x_sample, c_prompt, c_sample, ada_w, ada_b, norm_g, w_in,
              mla_q_norm, mla_w_uq, mla_kv_norm, mla_w_ukv, diff_lambda, diff_subln,
              lru_conv_w, lru_conv_b, lru_wa, lru_ba, lru_wx, lru_bx, lru_lambda,
              gqa_q_norm, gqa_k_norm, w_out, final_norm):
    p = dict(ada_w=ada_w, ada_b=ada_b, norm_g=norm_g, w_in=w_in,
             mla_q_norm=mla_q_norm, mla_w_uq=mla_w_uq, mla_kv_norm=mla_kv_norm, mla_w_ukv=mla_w_ukv,
             diff_lambda=diff_lambda, diff_subln=diff_subln,
             lru_conv_w=lru_conv_w, lru_conv_b=lru_conv_b, lru_wa=lru_wa, lru_ba=lru_ba,
             lru_wx=lru_wx, lru_bx=lru_bx, lru_lambda=lru_lambda,
             gqa_q_norm=gqa_q_norm, gqa_k_norm=gqa_k_norm, w_out=w_out, final_norm=final_norm)
    y_prompt = trunk(x_prompt, c_prompt, p)
    y_sample = trunk(x_sample, c_sample, p)
    return (y_prompt, y_sample)
```

```python
import contextlib
import math
import numpy as np
import ml_dtypes
import concourse.bass as bass
import concourse.mybir as mybir
from concourse.bass_utils import run_bass_kernel_spmd

F32 = mybir.dt.float32
BF16 = mybir.dt.bfloat16
AF = mybir.ActivationFunctionType
ALU = mybir.AluOpType
AX = mybir.AxisListType

D = 1024
NIN = 2912
EPS = 1e-6
OFF = dict(qa=0, kva=192, kra=320, ga=352, qb=608, kb=864, vb=1120, gb=1376,
           xc=1632, gc=1888, qd=2144, kd=2400, vd=2528, gd=2656)
NPC = 64
PC_NG = 0
PC_ADAB = 8
PC_QN = 32
PC_KVN = 34
PC_GQ = 35
PC_GK = 36
PC_GQS = 37
PC_GKS = 38
PC_CB = 39
PC_CW = 41
PC_BA = 49
PC_BX = 53
PC_LAM = 57
PR_LAM = 0
PR_SUB = 128
NPR = 192


class Res:
    __slots__ = ("name", "w", "r", "excl")

    def __init__(self, name, excl=False):
        self.name = name
        self.w = {}
        self.r = {}
        self.excl = excl


class Prog:
    ENGS = ("pe", "act", "dve", "pool", "sp")

    def __init__(self, nc):
        self.nc = nc
        self.count = {e: 0 for e in self.ENGS}
        self.waited = {e: {} for e in self.ENGS}
        self.stream = {e: [] for e in self.ENGS}
        self.dma_count = {}
        self.n_ins = 0

    def _deps(self, eng, reads, writes):
        need = {}

        def add(s, v, same_ok):
            if s == eng and eng in ("pe", "sp"):
                return
            if s in self.dma_count:
                v = self.dma_count[s]
            if need.get(s, 0) < v:
                need[s] = v
        for r in reads:
            for s, v in r.w.items():
                add(s, v, False)
            if r.excl:
                for s, v in r.r.items():
                    if s != eng:
                        add(s, v, False)
        for w in writes:
            for s, v in w.w.items():
                add(s, v, True)
            for s, v in w.r.items():
                add(s, v, True)
        out = []
        for s, v in need.items():
            if self.waited[eng].get(s, 0) >= v:
                continue
            self.waited[eng][s] = v
            out.append((s, v))
        return out

    def _commit(self, tok, reads, writes):
        s, v = tok
        for r in reads:
            if r.r.get(s, 0) < v:
                r.r[s] = v
        for w in writes:
            w.w[s] = v
            w.r = {}

    def op(self, eng, fn, reads=(), writes=(), signal=True):
        waits = self._deps(eng, reads, writes)
        if signal:
            self.count[eng] += 1
            tok = (eng, self.count[eng])
        else:
            tok = (eng, self.count[eng] + 1)
        self._commit(tok, reads, writes)
        self.stream[eng].append((waits, fn, eng if signal else None, 1))
        self.n_ins += 1

    def dma(self, queue, key, out, in_, reads=(), writes=(), **kw):
        waits = self._deps(queue, reads, writes)
        self.dma_count[key] = self.dma_count.get(key, 0) + 16
        tok = (key, self.dma_count[key])
        self._commit(tok, reads, writes)

        def fn(e, out=out, in_=in_, kw=kw):
            return e.dma_start(out=out, in_=in_, **kw)
        self.stream[queue].append((waits, fn, key, 16))
        self.n_ins += 1

    def wait_all(self, eng, res_list):
        waits = self._deps(eng, (), res_list)
        self.stream[eng].append((waits, None, None, 0))

    def emit(self):
        nc = self.nc
        with contextlib.ExitStack() as st:
            sems = {}
            for e in self.ENGS:
                sems[e] = st.enter_context(nc.semaphore("s_" + e))
            for k in self.dma_count:
                sems[k] = st.enter_context(nc.semaphore("d_" + k))
            block = st.enter_context(nc.Block())

            def replay(ename):
                def body(e):
                    for waits, fn, sig, inc in self.stream[ename]:
                        for s, v in waits:
                            e.wait_ge(sems[s], v)
                        if fn is None:
                            continue
                        ins = fn(e)
                        if sig is not None:
                            ins.then_inc(sems[sig], inc)
                return body

            block.tensor(replay("pe"))
            block.scalar(replay("act"))
            block.vector(replay("dve"))
            block.gpsimd(replay("pool"))
            block.sync(replay("sp"))


class Ring:
    def __init__(self, items):
        self.items = items
        self.i = 0

    def next(self):
        it = self.items[self.i % len(self.items)]
        self.i += 1
        return it


def build(seq_lens, depth, mixers="ABCD"):
    nc = bass.Bass("TRN2", target_bir_lowering=False)
    nseq = len(seq_lens)
    Smax = max(seq_lens)
    NTmax = Smax // 128

    def din(name, shape, dt=F32):
        return nc.dram_tensor(name, list(shape), dt, kind="ExternalInput").ap()

    x_in = [din(f"x{i}", [S, D]) for i, S in enumerate(seq_lens)]
    y_out = [nc.dram_tensor(f"y{i}", [S, D], F32, kind="ExternalOutput").ap() for i, S in enumerate(seq_lens)]
    cfm = din("cfm", [nseq, 128, 8])
    ada_w = din("ada_w", [depth, D, 3 * D])
    ada_b = din("ada_b", [depth, 3 * D])
    w_in = din("w_in", [depth, D, NIN])
    w_out = din("w_out", [depth, D, D])
    w_uq = din("mla_w_uq", [depth, 192, 384])
    w_ukv = din("mla_w_ukv", [depth, 128, 512])
    lru_w = din("lru_w", [depth, 2, 2, 4, 64, 64])
    pcol = din("pcol", [depth, 128, NPC])
    prow = din("prow", [depth, NPR])
    fnorm = din("final_norm", [1, D])
    ident_b_d = din("ident_b", [128, 128], BF16)
    ident_f_d = din("ident_f", [128, 128])
    tab_mla = din("tab_mla", [2, 32, Smax])
    tab_diff = din("tab_diff", [2, 128, Smax])
    tab_gqa = din("tab_gqa", [2, 128, Smax])

    xres = [nc.dram_tensor(f"xres{i}", [S, D], F32).ap() for i, S in enumerate(seq_lens)]
    oscr = [nc.dram_tensor(f"oscr{i}", [S, 768], BF16).ap() for i, S in enumerate(seq_lens)]
    ocscr = [nc.dram_tensor(f"ocscr{i}", [256, S], BF16).ap() for i, S in enumerate(seq_lens)]

    P = Prog(nc)
    st = contextlib.ExitStack()
    with st:
        sb_bytes = [0]

        def sb(name, shape, dt=F32):
            n = 1
            for d_ in shape[1:]:
                n *= d_
            sb_bytes[0] += n * (2 if dt == BF16 else 4)
            return st.enter_context(nc.sbuf_tensor(name, list(shape), dt))

        ident_b = sb("ident_b_s", [128, 128], BF16)
        ident_f = sb("ident_f_s", [128, 65])
        ones_b = sb("ones_b", [128, 128], BF16)
        bones_b = sb("bones_b", [128, 128], BF16)
        epsc = sb("epsc", [128, 1])
        h_fm = sb("h_fm", [128, 8, Smax], BF16)
        wmix = sb("wmix", [128, 8, 1536], BF16)
        wst = [sb(f"wst{i}", [128, 1024]) for i in range(2)]
        qbuf = sb("qbuf", [128, 2, Smax], BF16)
        kbuf = sb("kbuf", [128, 2, Smax], BF16)
        lat = sb("lat", [128, Smax], BF16)
        vaug = sb("vaug", [128, NTmax, 260], BF16)
        xt = [sb(f"xt{i}", [128, 1024]) for i in range(2)]
        tabc = [sb(f"tabc{i}", [128, 512]) for i in range(1)]
        tabs = [sb(f"tabs{i}", [128, 512]) for i in range(1)]
        ptbuf = sb("ptbuf", [128, 4, 512], BF16)
        qz = sb("qz", [128, 4, 512], BF16)
        pts = [ptbuf[:, i, :] for i in range(4)]
        xn = ptbuf[:, 0:2, :].rearrange("p a n -> p (a n)")
        tmpf = [sb(f"tmpf{i}", [128, 512]) for i in range(4)]
        tmpb = [sb(f"tmpb{i}", [128, 512], BF16) for i in range(3)]
        tmpo = [sb(f"tmpo{i}", [128, 512]) for i in range(2)]
        tmpg = [sb(f"tmpg{i}", [128, 256]) for i in range(2)]
        ogs = [sb(f"og{i}", [128, 256], BF16) for i in range(2)]
        pc = sb("pc", [128, NPC])
        pr = sb("pr", [128, NPR])
        sc = sb("sc", [128, 8])
        sc_bc = sb("sc_bc", [128, 8, 128])
        fn_bc = sc_bc[:].rearrange("p k n -> p (k n)")
        shiftcol = sb("shiftcol", [128, 8])
        gscol = sb("gscol", [128, 8])
        gate_bc = sb("gate_bc", [128, 1024])
        small = sb("small", [128, 64])
        wuq = sb("wuq", [128, 2, 384], BF16)
        wuqs = sb("wuqs", [128, 2, 384], BF16)
        wukv = sb("wukv", [128, 512], BF16)
        lamc = sb("lamc", [128, 4])
        subln_bc = sb("subln_bc", [128, 4, 64])
        lruw = sb("lruw", [128, 2, 2, 2, 128], BF16)
        lsp = sb("lsp", [128, 12])

        import os as _os0
        if _os0.environ.get("KDEBUG"):
            print("SBUF bytes/partition:", sb_bytes[0])
        ps = [st.enter_context(nc.psum_tensor(f"ps{i}", [128, 512], F32)) for i in range(8)]
        r_ps = [Res(f"ps{i}", excl=True) for i in range(8)]
        SB_BANKS = (0, 1, 2, 3)
        O_BANKS = (4, 5)
        O_ALL = (4, 5, 6, 7)
        misc = Ring([6, 7])

        r_const = Res("const")
        r_h = [Res(f"h{t}") for t in range(NTmax)]
        r_wmix = Res("wmix")
        r_wst = [Res("wst0"), Res("wst1")]
        wst_ring = Ring([0, 1])
        NCmax = Smax // 512 if Smax >= 512 else 1
        r_q = [[Res(f"q{s}_{c}") for c in range(NTmax)] for s in range(2)]
        r_k = [[Res(f"k{s}_{c}") for c in range(NTmax)] for s in range(2)]
        r_lat = [Res(f"lat{c}") for c in range(NTmax)]
        r_v = [Res(f"v{c}") for c in range(NTmax)]
        r_xt = [Res("xt0"), Res("xt1")]
        xt_ring = Ring([0, 1])
        r_xn = Res("xn")
        r_tab = [Res("tab0"), Res("tab1")]
        tab_ring = Ring([0])
        r_pt = [Res(f"pt{i}") for i in range(4)]
        r_qz = [Res(f"qz{i}") for i in range(4)]
        r_tmpf = [Res(f"tmpf{i}") for i in range(4)]
        tmpf_ring = Ring([0, 1, 2, 3])
        r_tmpb = [Res(f"tmpb{i}") for i in range(3)]
        r_tmpo = [Res("tmpo0"), Res("tmpo1")]
        tmpo_ring = Ring([0, 1])
        r_tmpg = [Res("tmpg0"), Res("tmpg1")]
        tmpg_ring = Ring([0, 1])
        tmpb_ring = Ring([0, 1, 2])
        r_og = [Res("og0"), Res("og1")]
        og_ring = Ring([0, 1])
        r_pc = Res("pc")
        r_mod = Res("mod")
        r_gate = Res("gate")
        r_small = Res("small")
        r_wa = Res("wa")
        r_ot = Res("ot")
        r_ofm = Res("ofm")
        r_oscr = [[Res(f"oscr{i}_{t}") for t in range(S // 128)] for i, S in enumerate(seq_lens)]
        r_ocscr = [Res(f"ocscr{i}") for i in range(nseq)]
        r_xres = [[Res(f"xres{i}_{t}") for t in range(S // 128)] for i, S in enumerate(seq_lens)]
        r_y = Res("y")

        def MM(out, lhsT, rhs, start, stop, rd, wr, sig=True, **kw):
            P.op("pe", lambda e: e.matmul(out, lhsT=lhsT, rhs=rhs, start=start, stop=stop, **kw),
                 reads=rd, writes=wr, signal=sig)

        def TR(out, in_, ident, rd, wr, sig=True):
            P.op("pe", lambda e: e.transpose(out, in_, ident), reads=rd, writes=wr, signal=sig)

        def ACT(out, in_, func, rd, wr, scale=1.0, bias=None, accum=None):
            def fn(e):
                kw = {}
                if bias is not None:
                    kw["bias"] = bias
                if accum is not None:
                    kw["accum_out"] = accum
                return e.activation(out=out, in_=in_, func=func, scale=scale, **kw)
            P.op("act", fn, reads=rd, writes=wr)

        def eng_of(P_, name):
            return name

        def TT(eng, out, in0, in1, op, rd, wr):
            P.op(eng, lambda e: e.tensor_tensor(out=out, in0=in0, in1=in1, op=op), reads=rd, writes=wr)

        def TS(eng, out, in0, s1, s2, op0, op1, rd, wr):
            if op1 is None:
                P.op(eng, lambda e: e.tensor_scalar(out=out, in0=in0, scalar1=s1, scalar2=None, op0=op0), reads=rd, writes=wr)
            else:
                P.op(eng, lambda e: e.tensor_scalar(out=out, in0=in0, scalar1=s1, scalar2=s2, op0=op0, op1=op1), reads=rd, writes=wr)

        def STT(out, in0, scalar, in1, op0, op1, rd, wr):
            P.op("dve", lambda e: e.scalar_tensor_tensor(out=out, in0=in0, scalar=scalar, in1=in1, op0=op0, op1=op1),
                 reads=rd, writes=wr)

        def CP(eng, out, in_, rd, wr):
            P.op(eng, lambda e: e.tensor_copy(out=out, in_=in_), reads=rd, writes=wr)

        def RCP(out, in_, rd, wr):
            P.op("dve", lambda e: e.reciprocal(out=out, in_=in_), reads=rd, writes=wr)

        def MEMSET(eng, ap, val, wr):
            P.op(eng, lambda e: e.memset(ap, val), writes=wr)

        dma_i = [0]
        import os as _os
        ROPE_ENG = _os.environ.get("ROPE_ENG", "dve")

        def LOAD(key, out, in_, rd, wr, **kw):
            P.dma("sp", key, out, in_, reads=rd, writes=wr, **kw)

        def rstd_act(out, in_, n, rd, wr):
            ACT(out, in_, AF.Ln, list(rd) + [r_const], wr, scale=1.0 / n, bias=epsc[0:out.shape[0], 0:1])
            ACT(out, out, AF.Exp, wr, wr, scale=-0.5)

        LOAD("c0", ident_b[:], ident_b_d, [], [r_const])
        LOAD("c0", ident_f[:], ident_f_d[:, 0:65], [], [r_const])
        MEMSET("pool", ones_b[:], 1.0, [r_const])
        MEMSET("pool", bones_b[:], 0.0, [r_const])
        MEMSET("pool", bones_b[0:64, 0:64], 1.0, [r_const])
        MEMSET("pool", bones_b[64:128, 64:128], 1.0, [r_const])
        MEMSET("pool", epsc[:], EPS, [r_const])
        MEMSET("pool", vaug[:], 1.0, r_v)
        MEMSET("pool", qz[:], 0.0, r_qz)

        if len(mixers) < 4:
            zt = sb("zt", [128, 768], BF16)
            r_zt = Res("zt")
            MEMSET("pool", zt[:], 0.0, [r_zt])
            for i_, S_ in enumerate(seq_lens):
                for t_ in range(S_ // 128):
                    P.dma("sp", "zst", oscr[i_][t_ * 128:(t_ + 1) * 128, :], zt[:], reads=[r_zt], writes=[r_oscr[i_][t_]])
                    for cc_ in range(2):
                        P.dma("sp", "zst", ocscr[i_][cc_ * 128:(cc_ + 1) * 128, t_ * 128:(t_ + 1) * 128], zt[:, 0:128], reads=[r_zt], writes=[r_ocscr[i_]])

        def load_w(src2d, ncols, dst_col, swap=None, rows=D, dst=None, gaincol=None):
            nk = rows // 128
            i = wst_ring.next()
            stg = wst[i][:, 0:nk * ncols].rearrange("p (k n) -> p k n", k=nk)
            LOAD(f"wst{i}", stg, src2d.rearrange("(k p) n -> p k n", p=128), [], [r_wst[i]])
            return i, stg

        cast_i = [0]

        def cast_cols(stg_ap, i, dst_ap, eng=None):
            cast_i[0] += 1
            if cast_i[0] % 2 == 0:
                ACT(dst_ap, stg_ap, AF.Copy, [r_wst[i]], [r_wmix])
            else:
                CP("dve", dst_ap, stg_ap, [r_wst[i]], [r_wmix])

        def load_cast(l, src_c0, n, dst_c0, extra=None):
            for p0 in range(0, n, 128):
                pn = min(128, n - p0)
                i, stg = load_w(w_in[l][:, src_c0 + p0:src_c0 + p0 + pn], pn, 0)
                cast_cols(stg, i, wmix[:, :, dst_c0 + p0:dst_c0 + p0 + pn])
                if extra is not None:
                    extra(i, stg, p0, pn)

        def swapped(ap3, half):
            return ap3.rearrange("p k (m b j) -> p k m b j", b=2, j=half)

        def layer_prep(l):
            LOAD("c1", pc[:], pcol[l], [r_pc, r_mod, r_wa], [r_pc])
            LOAD("c1", pr[:], prow[l:l + 1, :].partition_broadcast(128), [r_pc], [r_pc])
            if "A" in mixers:
                i = wst_ring.next()
                LOAD(f"wst{i}", wst[i][:, 0:384], w_uq[l][0:128, :], [], [r_wst[i]])
                LOAD(f"wst{i}", wst[i][0:64, 384:768], w_uq[l][128:192, :], [], [r_wst[i]])
                for kk, rows in ((0, 128), (1, 64)):
                    TS("pool", wuq[0:rows, kk, :], wst[i][0:rows, kk * 384:(kk + 1) * 384], pc[0:rows, PC_QN + kk:PC_QN + kk + 1], None,
                       ALU.mult, None, [r_wst[i], r_pc], [r_wa])
                    CP("pool", wuqs[0:rows, kk, :], wuq[0:rows, kk, :], [r_wa], [r_wa])
                    v = wuq[0:rows, kk, :].rearrange("p (h c) -> p h c", c=96)
                    vs = wuqs[0:rows, kk, :].rearrange("p (h c) -> p h c", c=96)
                    CP("pool", vs[:, :, 64:80], v[:, :, 80:96], [r_wa], [r_wa])
                    CP("pool", vs[:, :, 80:96], v[:, :, 64:80], [r_wa], [r_wa])
                i = wst_ring.next()
                LOAD(f"wst{i}", wst[i][:, 0:512], w_ukv[l], [], [r_wst[i]])
                sv = wst[i][:, 0:512].rearrange("p (h t c) -> p h t c", h=4, t=2)
                dv_ = wukv[:, :].rearrange("p (t h c) -> p t h c", t=2, h=4)
                for t_ in range(2):
                    TS("pool", dv_[:, t_], sv[:, :, t_, :], pc[:, PC_KVN:PC_KVN + 1], None, ALU.mult, None,
                       [r_wst[i], r_pc], [r_wa])
            if "B" in mixers:
                lam_init = 0.8 - 0.6 * math.exp(-0.3 * l)
                lv = pr[:, PR_LAM:PR_LAM + 128].rearrange("p (a d) -> p a d", a=4)
                TT("dve", small[:, 0:32], lv[:, 0, :], lv[:, 1, :], ALU.mult, [r_pc], [r_small])
                TT("dve", small[:, 32:64], lv[:, 2, :], lv[:, 3, :], ALU.mult, [r_pc], [r_small])
                P.op("dve", lambda e: e.tensor_reduce(out=lamc[:, 0:2], in_=small[:, 0:64].rearrange("p (a d) -> p a d", a=2),
                                                     axis=AX.X, op=ALU.add), reads=[r_small], writes=[r_wa])
                ACT(lamc[:, 0:2], lamc[:, 0:2], AF.Exp, [r_wa], [r_wa])
                TT("dve", lamc[:, 2:3], lamc[:, 0:1], lamc[:, 1:2], ALU.subtract, [r_wa], [r_wa])
                TS("dve", lamc[:, 3:4], lamc[:, 2:3], -1.0, -lam_init, ALU.mult, ALU.add, [r_wa], [r_wa])
                for qb in range(4):
                    TS("dve", subln_bc[:, qb, :], pr[:, PR_SUB:PR_SUB + 64], 1.0 - lam_init, None, ALU.mult, None,
                       [r_pc], [r_wa])
            if "C" in mixers:
                i = wst_ring.next()
                MEMSET("pool", wst[i][:, 0:1024], 0.0, [r_wst[i]])
                for m in range(2):
                    for d_ in range(2):
                        for hh in range(4):
                            cc, hb = hh // 2, (hh % 2) * 64
                            col = ((m * 2 + d_) * 2 + cc) * 128 + hb
                            LOAD(f"wst{i}", wst[i][hb:hb + 64, col:col + 64], lru_w[l, m, d_, hh], [], [r_wst[i]])
                CP("pool", lruw[:].rearrange("p m d c n -> p (m d c n)"), wst[i][:, 0:1024], [r_wst[i]], [r_wa])
                ACT(lsp[:, 0:4], pc[:, PC_LAM:PC_LAM + 4], AF.Exp, [r_pc], [r_wa], scale=-1.0)
                ACT(lsp[:, 0:4], lsp[:, 0:4], AF.Ln, [r_wa], [r_wa], bias=1.0)
                TS("dve", lsp[:, 0:4], lsp[:, 0:4], -8.0, None, ALU.mult, None, [r_wa], [r_wa])
                TS("dve", lsp[:, 4:8], pc[:, PC_BA:PC_BA + 4], -1.0, None, ALU.mult, None, [r_pc], [r_wa])
                TS("dve", lsp[:, 8:12], pc[:, PC_BX:PC_BX + 4], -1.0, None, ALU.mult, None, [r_pc], [r_wa])

        def phase0(l, s):
            LOAD("c2", sc[:], cfm[s], [r_mod], [r_mod])
            ACT(small[:, 8:16], sc[:], AF.Exp, [r_mod], [r_small], scale=-1.0)
            TS("dve", small[:, 8:16], small[:, 8:16], 1.0, None, ALU.add, None, [r_small], [r_small])
            RCP(small[:, 8:16], small[:, 8:16], [r_small], [r_small])
            TT("dve", sc[:], sc[:], small[:, 8:16], ALU.mult, [r_small, r_mod], [r_mod])
            CP("dve", sc_bc[:], sc[:, :].unsqueeze(2).broadcast_to([128, 8, 128]), [r_mod], [r_mod])
            mb = misc.next()
            for j in range(24):
                i, stg = load_w(ada_w[l][:, j * 128:(j + 1) * 128], 128, 0)
                if j < 16:
                    for k in range(8):
                        MM(ps[mb][:, j * 2:j * 2 + 2], stg[:, k, :], sc_bc[:, k, 0:2],
                           k == 0, k == 7, [r_wst[i], r_mod], [r_ps[mb]], sig=(k == 7))
                    if j == 15:
                        pv = ps[mb][:, 0:32].rearrange("p (j t) -> p j t", t=2)
                        TT("dve", shiftcol[:], pv[:, 0:8, 0], pc[:, PC_ADAB:PC_ADAB + 8], ALU.add, [r_ps[mb], r_pc], [r_mod])
                        TT("dve", gscol[:], pv[:, 8:16, 0], pc[:, PC_ADAB + 8:PC_ADAB + 16], ALU.add, [r_ps[mb], r_pc], [r_mod])
                        STT(gscol[:], gscol[:], 1.0, pc[:, PC_NG:PC_NG + 8], ALU.add, ALU.mult, [r_mod, r_pc], [r_mod])
                else:
                    gb = misc.next()
                    c0 = (j - 16) * 128
                    a = tmpf_ring.next()
                    LOAD(f"abg{a}", tmpf[a][:, 0:128], ada_b[l:l + 1, 2 * D + c0:2 * D + c0 + 128].partition_broadcast(128),
                         [], [r_tmpf[a]])
                    for k in range(8):
                        MM(ps[gb][:, 0:128], sc_bc[:, k, :], stg[:, k, :], k == 0, k == 7, [r_wst[i], r_mod], [r_ps[gb]], sig=(k == 7))
                    TT("dve", gate_bc[:, c0:c0 + 128], ps[gb][:, 0:128], tmpf[a][:, 0:128], ALU.add,
                       [r_ps[gb], r_tmpf[a]], [r_gate])

        def phase1(l, s):
            S = seq_lens[s]
            src = x_in[s] if l == 0 else xres[s]
            for t in range(S // 128):
                xi = xt_ring.next()
                rd = [] if l == 0 else [r_xres[s][t]]
                LOAD(f"xt{xi}", xt[xi][:], src[t * 128:(t + 1) * 128, :], rd, [r_xt[xi]])
                ACT(tmpf[0][:].bitcast(BF16), xt[xi][:], AF.Square, [r_xt[xi]], [r_tmpf[0], r_small], accum=small[:, 16:17])
                rstd_act(small[:, 16:17], small[:, 16:17], float(D), [r_small], [r_small])
                TS("dve", xn[:], xt[xi][:], small[:, 16:17], None, ALU.mult, None, [r_xt[xi], r_small], [r_pt[0], r_pt[1]])
                mb = misc.next()
                pT = ps[mb][:].bitcast(BF16)
                for k in range(8):
                    TR(pT[:, k * 128:(k + 1) * 128], xn[:, k * 128:(k + 1) * 128], ident_b[:], [r_pt[0], r_pt[1], r_const], [r_ps[mb]], sig=(k == 7))
                for k in range(8):
                    if t % 2 == 0:
                        TS("dve", h_fm[:, k, t * 128:(t + 1) * 128], pT[:, k * 128:(k + 1) * 128], gscol[:, k:k + 1], shiftcol[:, k:k + 1],
                           ALU.mult, ALU.add, [r_ps[mb], r_mod], [r_h[t]])
                    else:
                        ACT(h_fm[:, k, t * 128:(t + 1) * 128], pT[:, k * 128:(k + 1) * 128], AF.Identity, [r_ps[mb], r_mod], [r_h[t]],
                            scale=gscol[:, k:k + 1], bias=shiftcol[:, k:k + 1])

        def proj_fm(S, c, wcol, rows=128, wrows=None):
            cw = min(512, S)
            mb = misc.next()
            tl = [r_h[t] for t in range(c * cw // 128, (c + 1) * cw // 128)]
            for k in range(8):
                MM(ps[mb][0:rows, 0:cw], wmix[:, k, wcol:wcol + rows], h_fm[:, k, c * cw:(c + 1) * cw], k == 0, k == 7,
                   [r_wmix] + tl, [r_ps[mb]], sig=(k == 7))
            return mb

        def load_tab(tab, c, cw, rows=slice(0, 128), trows=None):
            ti = tab_ring.next()
            nrow = rows.stop - rows.start
            LOAD(f"tab{ti}", tabc[ti][rows, 0:cw], tab[0, 0:nrow, c * cw:(c + 1) * cw], [], [r_tab[ti]])
            LOAD(f"tab{ti}", tabs[ti][rows, 0:cw], tab[1, 0:nrow, c * cw:(c + 1) * cw], [], [r_tab[ti]])
            return ti

        def rope_evac(mb, mbs, ti, rows, cw, out_ap, wr, gcol=None, gscol_=None, rstd=None, rstd_rd=()):
            a = tmpf_ring.next()
            b = tmpf_ring.next()
            if gcol is None:
                TT("dve", tmpf[a][rows, 0:cw], ps[mb][rows, 0:cw], tabc[ti][rows, 0:cw], ALU.mult, [r_ps[mb], r_tab[ti]], [r_tmpf[a]])
                TT("dve", tmpf[b][rows, 0:cw], ps[mbs][rows, 0:cw], tabs[ti][rows, 0:cw], ALU.mult, [r_ps[mbs], r_tab[ti]], [r_tmpf[b]])
            else:
                TS("dve", tmpf[a][rows, 0:cw], ps[mb][rows, 0:cw], gcol, None, ALU.mult, None, [r_ps[mb], r_pc], [r_tmpf[a]])
                TT("dve", tmpf[a][rows, 0:cw], tmpf[a][rows, 0:cw], tabc[ti][rows, 0:cw], ALU.mult, [r_tmpf[a], r_tab[ti]], [r_tmpf[a]])
                TS("dve", tmpf[b][rows, 0:cw], ps[mbs][rows, 0:cw], gscol_, None, ALU.mult, None, [r_ps[mbs], r_pc], [r_tmpf[b]])
                TT("dve", tmpf[b][rows, 0:cw], tmpf[b][rows, 0:cw], tabs[ti][rows, 0:cw], ALU.mult, [r_tmpf[b], r_tab[ti]], [r_tmpf[b]])
            if rstd is None:
                TT(ROPE_ENG, out_ap, tmpf[a][rows, 0:cw], tmpf[b][rows, 0:cw], ALU.add, [r_tmpf[a], r_tmpf[b]], wr)
            else:
                TT(ROPE_ENG, tmpf[a][rows, 0:cw], tmpf[a][rows, 0:cw], tmpf[b][rows, 0:cw], ALU.add, [r_tmpf[a], r_tmpf[b]], [r_tmpf[a]])
                TT(ROPE_ENG, out_ap, tmpf[a][rows, 0:cw], rstd, ALU.mult, [r_tmpf[a]] + list(rstd_rd), wr)

        def run_attention(S, groups):
            cw = min(512, S)
            NC = S // cw
            NKT = S // 128
            steps = []
            oset = [0]
            for g in groups:
                nu = len(g["units"])
                for c in range(NC):
                    if nu == 1:
                        bs = (O_BANKS[oset[0] % 2],)
                        oset[0] += 1
                    else:
                        bs = O_BANKS
                    for ui, u in enumerate(g["units"]):
                        for kt in range(0, NKT, 2):
                            kts = [kt] if kt + 1 >= NKT else [kt, kt + 1]
                            steps.append(dict(g=g, c=c, lanes=[(u, k_, bs[ui]) for k_ in kts],
                                              first=(ui == 0 and kt == 0 and c == 0), cfirst=(ui == 0 and kt == 0),
                                              last=(ui == nu - 1 and kts[-1] == NKT - 1), obs=bs))

            def emit_S(i):
                sp_ = steps[i]
                if sp_["first"] and sp_["g"].get("pre") is not None:
                    sp_["g"]["pre"]()
                if sp_["cfirst"] and sp_["g"].get("prec") is not None:
                    sp_["g"]["prec"](sp_["c"])
                for j, (u, kt, ob) in enumerate(sp_["lanes"]):
                    qap, qres = u["q"](sp_["c"])
                    bank = SB_BANKS[(2 * i + j) % 4]
                    kap, kres = u["k"](kt)
                    MM(ps[bank][:, 0:cw], kap, qap, True, True, list(kres) + list(qres), [r_ps[bank]])

            emit_S(0)
            for i, sp_ in enumerate(steps):
                if i + 1 < len(steps):
                    emit_S(i + 1)
                nl = len(sp_["lanes"])
                for j, (u, kt, ob) in enumerate(sp_["lanes"]):
                    bank = SB_BANKS[(2 * i + j) % 4]
                    pi = (2 * i + j) % 4
                    ACT(pts[pi][:, 0:cw], ps[bank][:, 0:cw], AF.Exp, [r_ps[bank]], [r_pt[pi]], scale=u["scale"])
                for j, (u, kt, ob) in enumerate(sp_["lanes"]):
                    pi = (2 * i + j) % 4
                    vap, vres = u["v"](kt)
                    MM(ps[ob][0:65, 0:cw], vap, pts[pi][:, 0:cw], kt == 0, kt == NKT - 1, [r_pt[pi]] + list(vres),
                       [r_ps[ob]], sig=(kt == NKT - 1 or j == nl - 1))
                if sp_["last"]:
                    sp_["g"]["epilogue"](sp_["c"], sp_["obs"])

        def o_copy(ob, cw):
            a = tmpo_ring.next()
            CP("dve", tmpo[a][0:65, 0:cw], ps[ob][0:65, 0:cw], [r_ps[ob]], [r_tmpo[a]])
            return a

        def o_transposed(a, cw):
            mb = misc.next()
            nqb = cw // 128
            for qb in range(nqb):
                TR(ps[mb][:, qb * 65:(qb + 1) * 65], tmpo[a][0:65, qb * 128:(qb + 1) * 128], ident_f[0:65, 0:65],
                   [r_tmpo[a], r_const], [r_ps[mb]], sig=(qb == nqb - 1))
            return mb, ps[mb][:, 0:nqb * 65].rearrange("p (q d) -> p q d", d=65)

        def gate_mm(S, c, gcol, width):
            cw = min(512, S)
            nqb = cw // 128
            gb = misc.next()
            for qb in range(nqb):
                t = c * nqb + qb
                for k in range(8):
                    MM(ps[gb][:, qb * width:(qb + 1) * width], h_fm[:, k, t * 128:(t + 1) * 128], wmix[:, k, gcol:gcol + width],
                       k == 0, k == 7, [r_h[t], r_wmix], [r_ps[gb]], sig=(qb == nqb - 1 and k == 7))
            return gb

        def gate_fin(gb, n):
            a = tmpg_ring.next()
            ACT(tmpg[a][:, 0:n], ps[gb][:, 0:n], AF.Exp, [r_ps[gb]], [r_tmpg[a]], scale=-1.0)
            TS("dve", tmpg[a][:, 0:n], tmpg[a][:, 0:n], 1.0, None, ALU.add, None, [r_tmpg[a]], [r_tmpg[a]])
            RCP(tmpg[a][:, 0:n], tmpg[a][:, 0:n], [r_tmpg[a]], [r_tmpg[a]])
            TT("dve", tmpg[a][:, 0:n], ps[gb][:, 0:n], tmpg[a][:, 0:n], ALU.mult, [r_ps[gb], r_tmpg[a]], [r_tmpg[a]])
            return a

        def store_o(s, c, cw, col, og_i, width=64):
            nqb = cw // 128
            dst = oscr[s][c * cw:(c + 1) * cw, col:col + width].rearrange("(q p) d -> p q d", p=128)
            src = ogs[og_i][:, 0:nqb * width].rearrange("p (q d) -> p q d", d=width)
            tl = [r_oscr[s][c * nqb + q] for q in range(nqb)]
            P.dma("sp", f"ost{og_i}", dst, src, reads=[r_og[og_i]], writes=tl)

        def std_epilogue(s, S, col0, gcol0, h):
            def ep(c, obs):
                cw = min(512, S)
                nqb = cw // 128
                gb = gate_mm(S, c, gcol0 + h * 64, 64)
                oc_ = o_copy(obs[0], cw)
                mb, ov = o_transposed(oc_, cw)
                RCP(small[:, 20:20 + nqb], ov[:, :, 64], [r_ps[mb]], [r_small])
                a = tmpf_ring.next()
                n = nqb * 64
                av = tmpf[a][:, 0:n].rearrange("p (q d) -> p q d", d=64)
                TT("dve", av, ov[:, :, 0:64], small[:, 20:20 + nqb].unsqueeze(2).broadcast_to([128, nqb, 64]), ALU.mult,
                   [r_ps[mb], r_small], [r_tmpf[a]])
                sg = gate_fin(gb, n)
                oi = og_ring.next()
                TT("pool", ogs[oi][:, 0:n], tmpf[a][:, 0:n], tmpg[sg][:, 0:n], ALU.mult, [r_tmpf[a], r_tmpg[sg]], [r_og[oi]])
                store_o(s, c, cw, col0 + h * 64, oi)
            return ep

        def mixer_D(l, s):
            S = seq_lens[s]
            cw = min(512, S)
            NC = S // cw
            P.wait_all("pool", [r_wmix])
            def q_extra(i, stg, p0, pn):
                sv = swapped(stg, 16)
                dvw = swapped(wmix[:, :, 256 + p0:256 + p0 + pn], 16)
                CP("pool", dvw[:, :, :, 0, :], sv[:, :, :, 1, :], [r_wst[i]], [r_wmix])
                CP("pool", dvw[:, :, :, 1, :], sv[:, :, :, 0, :], [r_wst[i]], [r_wmix])
            load_cast(l, OFF["qd"], 256, 0, extra=q_extra)
            i, stg = load_w(w_in[l][:, OFF["kd"]:OFF["kd"] + 128], 128, 0)
            for g in range(2):
                for r in range(2):
                    c0 = 512 + g * 128 + r * 64
                    CP("pool", wmix[:, :, c0:c0 + 64], stg[:, :, g * 64:(g + 1) * 64], [r_wst[i]], [r_wmix])
                    sv = swapped(stg[:, :, g * 64:(g + 1) * 64], 16)
                    dvw = swapped(wmix[:, :, c0 + 256:c0 + 256 + 64], 16)
                    CP("pool", dvw[:, :, :, 0, :], sv[:, :, :, 1, :], [r_wst[i]], [r_wmix])
                    CP("pool", dvw[:, :, :, 1, :], sv[:, :, :, 0, :], [r_wst[i]], [r_wmix])
            load_cast(l, OFF["vd"], 128, 1024)
            load_cast(l, OFF["gd"], 256, 1152)
            for c in range(NC):
                ti = load_tab(tab_gqa, c, cw)
                for fc in range(4):
                    isq = fc < 2
                    wc = fc * 128 if isq else 512 + (fc - 2) * 128
                    mb = proj_fm(S, c, wc)
                    mbs = proj_fm(S, c, wc + 256)
                    sq = tmpb_ring.next()
                    ACT(tmpb[sq][:, 0:cw], ps[mb][:, 0:cw], AF.Square, [r_ps[mb]], [r_tmpb[sq]])
                    sb_ = SB_BANKS[fc % 2]
                    MM(ps[sb_][:, 0:cw], bones_b[:], tmpb[sq][:, 0:cw], True, True, [r_const, r_tmpb[sq]], [r_ps[sb_]])
                    rs = tmpf_ring.next()
                    rstd_act(tmpf[rs][:, 0:cw], ps[sb_][:, 0:cw], 64.0, [r_ps[sb_]], [r_tmpf[rs]])
                    if isq:
                        out_ap, wr = qbuf[:, fc, c * cw:(c + 1) * cw], [r_q[fc][c]]
                        g1, g2 = pc[:, PC_GQ:PC_GQ + 1], pc[:, PC_GQS:PC_GQS + 1]
                    else:
                        out_ap, wr = kbuf[:, fc - 2, c * cw:(c + 1) * cw], [r_k[fc - 2][c]]
                        g1, g2 = pc[:, PC_GK:PC_GK + 1], pc[:, PC_GKS:PC_GKS + 1]
                    rope_evac(mb, mbs, ti, slice(0, 128), cw, out_ap, wr, gcol=g1, gscol_=g2,
                              rstd=tmpf[rs][:, 0:cw], rstd_rd=[r_tmpf[rs]])
                for qb in range(cw // 128):
                    t = c * (cw // 128) + qb
                    mb = misc.next()
                    for k in range(8):
                        MM(ps[mb][:, 0:128], h_fm[:, k, t * 128:(t + 1) * 128], wmix[:, k, 1024:1152], k == 0, k == 7,
                           [r_h[t], r_wmix], [r_ps[mb]], sig=(k == 7))
                    CP("dve", vaug[:, t, 0:130].rearrange("p (h d) -> p h d", d=65)[:, :, 0:64],
                       ps[mb][:, 0:128].rearrange("p (h d) -> p h d", d=64), [r_ps[mb]], [r_v[t]])
            groups = []
            for fc in range(2):
                units = []
                for hh in range(2):
                    def qf(c, hh=hh):
                        return qz[:, hh, 0:cw], [r_qz[hh]]

                    def kf(kt, fc=fc):
                        return kbuf[:, fc, kt * 128:(kt + 1) * 128], [r_k[fc][kt * 128 // cw]]

                    def vf(kt, fc=fc):
                        return vaug[:, kt, fc * 65:(fc + 1) * 65], [r_v[kt]]
                    units.append(dict(q=qf, k=kf, v=vf, scale=64 ** -0.5))

                def prec(c, fc=fc):
                    CP("pool", qz[0:64, 0, 0:cw], qbuf[0:64, fc, c * cw:(c + 1) * cw], [r_q[fc][c]], [r_qz[0]])
                    CP("dve", qz[64:128, 1, 0:cw], qbuf[64:128, fc, c * cw:(c + 1) * cw], [r_q[fc][c]], [r_qz[1]])
                e0 = std_epilogue(s, S, 512, 1152, 2 * fc)
                e1 = std_epilogue(s, S, 512, 1152, 2 * fc + 1)

                def ep2(c, obs, e0=e0, e1=e1):
                    e0(c, (obs[0],))
                    e1(c, (obs[1],))
                groups.append(dict(units=units, epilogue=ep2, prec=prec))
            MEMSET("pool", qz[64:128, 0, :], 0.0, [r_qz[0]])
            MEMSET("pool", qz[0:64, 1, :], 0.0, [r_qz[1]])
            run_attention(S, groups)

        def mixer_B(l, s):
            S = seq_lens[s]
            cw = min(512, S)
            NC = S // cw
            P.wait_all("pool", [r_wmix])
            for nm, c0 in (("qb", 0), ("kb", 512)):
                def sw_extra(i, stg, p0, pn, c0=c0):
                    d0 = c0 + 256 + p0
                    CP("pool", wmix[:, :, d0:d0 + pn], stg, [r_wst[i]], [r_wmix])
                    sv = stg.rearrange("p k (m j) -> p k m j", j=32)
                    dvw = wmix[:, :, d0:d0 + pn].rearrange("p k (m j) -> p k m j", j=32)
                    CP("pool", dvw[:, :, :, 0:4], sv[:, :, :, 4:8], [r_wst[i]], [r_wmix])
                    CP("pool", dvw[:, :, :, 4:8], sv[:, :, :, 0:4], [r_wst[i]], [r_wmix])
                load_cast(l, OFF[nm], 256, c0, extra=sw_extra)
            load_cast(l, OFF["vb"], 256, 1024)
            load_cast(l, OFF["gb"], 256, 1280)
            for c in range(NC):
                ti = load_tab(tab_diff, c, cw)
                for fc in range(4):
                    isq = fc < 2
                    wc = fc * 128 if isq else 512 + (fc - 2) * 128
                    mb = proj_fm(S, c, wc)
                    mbs = proj_fm(S, c, wc + 256)
                    if isq:
                        out_ap, wr = qbuf[:, fc, c * cw:(c + 1) * cw], [r_q[fc][c]]
                    else:
                        out_ap, wr = kbuf[:, fc - 2, c * cw:(c + 1) * cw], [r_k[fc - 2][c]]
                    rope_evac(mb, mbs, ti, slice(0, 128), cw, out_ap, wr)
                for qb in range(cw // 128):
                    t = c * (cw // 128) + qb
                    mb = misc.next()
                    for k in range(8):
                        MM(ps[mb][:, 0:256], h_fm[:, k, t * 128:(t + 1) * 128], wmix[:, k, 1024:1280], k == 0, k == 7,
                           [r_h[t], r_wmix], [r_ps[mb]], sig=(k == 7))
                    CP("dve", vaug[:, t, 0:260].rearrange("p (h d) -> p h d", d=65)[:, :, 0:64],
                       ps[mb][:, 0:256].rearrange("p (h d) -> p h d", d=64), [r_ps[mb]], [r_v[t]])
            groups = []
            for h in range(4):
                fc = h // 2
                units = []
                for j in range(2):
                    b_ = (h % 2) * 2 + j

                    def qf(c, b_=b_):
                        return qz[:, b_, 0:cw], [r_qz[b_]]

                    def kf(kt, fc=fc):
                        return kbuf[:, fc, kt * 128:(kt + 1) * 128], [r_k[fc][kt * 128 // cw]]

                    def vf(kt, h=h):
                        return vaug[:, kt, h * 65:(h + 1) * 65], [r_v[kt]]
                    units.append(dict(q=qf, k=kf, v=vf, scale=32 ** -0.5))

                def prec(c, h=h, fc=fc):
                    for j in range(2):
                        b_ = (h % 2) * 2 + j
                        rs_ = slice(b_ * 32, (b_ + 1) * 32)
                        CP("pool" if j == 0 else "dve", qz[rs_, b_, 0:cw], qbuf[rs_, fc, c * cw:(c + 1) * cw], [r_q[fc][c]], [r_qz[b_]])

                def ep(c, obs, h=h):
                    nqb = cw // 128
                    n = nqb * 64
                    gb = gate_mm(S, c, 1280 + h * 64, 64)
                    oc1 = o_copy(obs[0], cw)
                    oc2 = o_copy(obs[1], cw)
                    ge = tmpg_ring.next()
                    gc_ = tmpf_ring.next()
                    ACT(tmpg[ge][:, 0:n], ps[gb][:, 0:n], AF.Exp, [r_ps[gb]], [r_tmpg[ge]], scale=-1.0)
                    ACT(tmpf[gc_][:, 0:n], ps[gb][:, 0:n], AF.Copy, [r_ps[gb]], [r_tmpf[gc_]])
                    mb1, ov1 = o_transposed(oc1, cw)
                    mb2, ov2 = o_transposed(oc2, cw)
                    RCP(small[:, 24:24 + nqb], ov1[:, :, 64], [r_ps[mb1]], [r_small])
                    RCP(small[:, 28:28 + nqb], ov2[:, :, 64], [r_ps[mb2]], [r_small])
                    TS("dve", small[:, 28:28 + nqb], small[:, 28:28 + nqb], lamc[:, 3:4], None, ALU.mult, None, [r_small, r_wa], [r_small])
                    a = tmpf_ring.next()
                    b = tmpf_ring.next()
                    av = tmpf[a][:, 0:n].rearrange("p (q d) -> p q d", d=64)
                    bv = tmpf[b][:, 0:n].rearrange("p (q d) -> p q d", d=64)
                    TT("dve", av, ov1[:, :, 0:64], small[:, 24:24 + nqb].unsqueeze(2).broadcast_to([128, nqb, 64]), ALU.mult,
                       [r_ps[mb1], r_small], [r_tmpf[a]])
                    TT("dve", bv, ov2[:, :, 0:64], small[:, 28:28 + nqb].unsqueeze(2).broadcast_to([128, nqb, 64]), ALU.mult,
                       [r_ps[mb2], r_small], [r_tmpf[b]])
                    TT("pool", tmpf[a][:, 0:n], tmpf[a][:, 0:n], tmpf[b][:, 0:n], ALU.add, [r_tmpf[a], r_tmpf[b]], [r_tmpf[a]])
                    TT("pool", tmpf[b][:, 0:n], tmpf[a][:, 0:n], tmpf[a][:, 0:n], ALU.mult, [r_tmpf[a]], [r_tmpf[b]])
                    P.op("dve", lambda e: e.tensor_reduce(out=small[:, 32:32 + nqb], in_=bv, axis=AX.X, op=ALU.add),
                         reads=[r_tmpf[b]], writes=[r_small])
                    rstd_act(small[:, 32:32 + nqb], small[:, 32:32 + nqb], 64.0, [r_small], [r_small])
                    TT("dve", av, av, small[:, 32:32 + nqb].unsqueeze(2).broadcast_to([128, nqb, 64]), ALU.mult,
                       [r_tmpf[a], r_small], [r_tmpf[a]])
                    TT("pool", av, av, subln_bc[:, 0:nqb, :], ALU.mult, [r_tmpf[a], r_wa], [r_tmpf[a]])
                    TS("dve", tmpg[ge][:, 0:n], tmpg[ge][:, 0:n], 1.0, None, ALU.add, None, [r_tmpg[ge]], [r_tmpg[ge]])
                    RCP(tmpg[ge][:, 0:n], tmpg[ge][:, 0:n], [r_tmpg[ge]], [r_tmpg[ge]])
                    TT("dve", tmpg[ge][:, 0:n], tmpf[gc_][:, 0:n], tmpg[ge][:, 0:n], ALU.mult, [r_tmpf[gc_], r_tmpg[ge]], [r_tmpg[ge]])
                    sg = ge
                    oi = og_ring.next()
                    TT("pool", ogs[oi][:, 0:n], tmpf[a][:, 0:n], tmpg[sg][:, 0:n], ALU.mult, [r_tmpf[a], r_tmpg[sg]], [r_og[oi]])
                    store_o(s, c, cw, 256 + h * 64, oi)
                groups.append(dict(units=units, epilogue=ep, prec=prec))
            for b_ in range(4):
                MEMSET("pool", qz[:, b_, :], 0.0, [r_qz[b_]])
            run_attention(S, groups)

        def mixer_A(l, s):
            S = seq_lens[s]
            cw = min(512, S)
            NC = S // cw
            nslots = 2 if 2 * S <= Smax else 1
            P.wait_all("pool", [r_wmix])

            def kr_extra(i, stg, p0, pn):
                if p0 == 256:
                    CP("pool", wmix[:, :, 352:368], stg[:, :, 80:96], [r_wst[i]], [r_wmix])
                    CP("pool", wmix[:, :, 368:384], stg[:, :, 64:80], [r_wst[i]], [r_wmix])
            load_cast(l, 0, 352, 0, extra=kr_extra)
            load_cast(l, OFF["ga"], 256, 384)
            R64 = slice(64, 96)
            qn0 = qbuf[:, 1, :]
            qn1 = kbuf[:, 1, :]

            def qs_(slot, c):
                return qbuf[:, 0, slot * S + c * cw:slot * S + (c + 1) * cw], r_q[0][slot * NC + c]

            def ks_(slot, c):
                return kbuf[:, 0, slot * S + c * cw:slot * S + (c + 1) * cw], r_k[0][slot * NC + c]

            for c in range(NC):
                tl = [r_h[t] for t in range(c * cw // 128, (c + 1) * cw // 128)]
                csl = slice(c * cw, (c + 1) * cw)
                mbq = []
                sqs = []
                for fc, rows in ((0, 128), (1, 64)):
                    mb = proj_fm(S, c, fc * 128, rows=rows)
                    sq = tmpb_ring.next()
                    ACT(tmpb[sq][0:rows, 0:cw], ps[mb][0:rows, 0:cw], AF.Square, [r_ps[mb]], [r_tmpb[sq]])
                    mbq.append(mb)
                    sqs.append(sq)
                sb_ = SB_BANKS[0]
                MM(ps[sb_][:, 0:cw], ones_b[:, :], tmpb[sqs[0]][:, 0:cw], True, False, [r_const, r_tmpb[sqs[0]]], [r_ps[sb_]], sig=False)
                MM(ps[sb_][:, 0:cw], ones_b[0:64, :], tmpb[sqs[1]][0:64, 0:cw], False, True, [r_const, r_tmpb[sqs[1]]], [r_ps[sb_]])
                rs = tmpf_ring.next()
                rstd_act(tmpf[rs][:, 0:cw], ps[sb_][:, 0:cw], 192.0, [r_ps[sb_]], [r_tmpf[rs]])
                TT("dve", qn0[:, csl], ps[mbq[0]][:, 0:cw], tmpf[rs][:, 0:cw], ALU.mult,
                   [r_ps[mbq[0]], r_tmpf[rs]], [r_q[1][c]])
                TT("dve", qn1[0:64, csl], ps[mbq[1]][0:64, 0:cw], tmpf[rs][0:64, 0:cw], ALU.mult,
                   [r_ps[mbq[1]], r_tmpf[rs]], [r_k[1][c]])
                mb = proj_fm(S, c, 192)
                sq = tmpb_ring.next()
                ACT(tmpb[sq][:, 0:cw], ps[mb][:, 0:cw], AF.Square, [r_ps[mb]], [r_tmpb[sq]])
                sb_ = SB_BANKS[1]
                MM(ps[sb_][:, 0:cw], ones_b[:, :], tmpb[sq][:, 0:cw], True, True, [r_const, r_tmpb[sq]], [r_ps[sb_]])
                rs = tmpf_ring.next()
                rstd_act(tmpf[rs][:, 0:cw], ps[sb_][:, 0:cw], 128.0, [r_ps[sb_]], [r_tmpf[rs]])
                TT("dve", lat[:, csl], ps[mb][:, 0:cw], tmpf[rs][:, 0:cw], ALU.mult,
                   [r_ps[mb], r_tmpf[rs]], [r_lat[c]])
                for qb in range(cw // 128):
                    t = c * (cw // 128) + qb
                    vb_ = misc.next()
                    MM(ps[vb_][:, 0:256], lat[:, t * 128:(t + 1) * 128], wukv[:, 256:512], True, True, [r_lat[c], r_wa], [r_ps[vb_]])
                    CP("dve", vaug[:, t, 0:260].rearrange("p (h d) -> p h d", d=65)[:, :, 0:64],
                       ps[vb_][:, 0:256].rearrange("p (h d) -> p h d", d=64), [r_ps[vb_]], [r_v[t]])
                ti = load_tab(tab_mla, c, cw, rows=R64)
                mbk = misc.next()
                mbks = misc.next()
                for k in range(8):
                    MM(ps[mbk][64:96, 0:cw], wmix[:, k, 320:352], h_fm[:, k, c * cw:(c + 1) * cw], k == 0, k == 7,
                       [r_wmix] + tl, [r_ps[mbk]], sig=(k == 7))
                for k in range(8):
                    MM(ps[mbks][64:96, 0:cw], wmix[:, k, 352:384], h_fm[:, k, c * cw:(c + 1) * cw], k == 0, k == 7,
                       [r_wmix] + tl, [r_ps[mbks]], sig=(k == 7))
                k0, rk0 = ks_(0, c)
                rope_evac(mbk, mbks, ti, R64, cw, k0[64:96, :], [rk0])
                if nslots == 2:
                    k1, rk1 = ks_(1, c)
                    CP("pool", k1[64:96, :], k0[64:96, :], [rk0], [rk1])

            def head_proj(h):
                slot = h % nslots
                for c in range(NC):
                    csl = slice(c * cw, (c + 1) * cw)
                    qa_, rq_ = qs_(slot, c)
                    ka_, rk_ = ks_(slot, c)
                    mb = misc.next()
                    MM(ps[mb][0:96, 0:cw], wuq[:, 0, h * 96:(h + 1) * 96], qn0[:, csl], True, False,
                       [r_wa, r_q[1][c]], [r_ps[mb]], sig=False)
                    MM(ps[mb][0:96, 0:cw], wuq[0:64, 1, h * 96:(h + 1) * 96], qn1[0:64, csl], False, True,
                       [r_wa, r_k[1][c]], [r_ps[mb]])
                    mbs = misc.next()
                    MM(ps[mbs][0:96, 0:cw], wuqs[:, 0, h * 96:(h + 1) * 96], qn0[:, csl], True, False,
                       [r_wa, r_q[1][c]], [r_ps[mbs]], sig=False)
                    MM(ps[mbs][0:96, 0:cw], wuqs[0:64, 1, h * 96:(h + 1) * 96], qn1[0:64, csl], False, True,
                       [r_wa, r_k[1][c]], [r_ps[mbs]])
                    CP("dve", qa_[0:64, :], ps[mb][0:64, 0:cw], [r_ps[mb]], [rq_])
                    ti = load_tab(tab_mla, c, cw, rows=R64)
                    rope_evac(mb, mbs, ti, R64, cw, qa_[64:96, :], [rq_])
                    mbk = misc.next()
                    MM(ps[mbk][0:64, 0:cw], wukv[:, h * 64:(h + 1) * 64], lat[:, csl], True, True,
                       [r_wa, r_lat[c]], [r_ps[mbk]])
                    CP("dve", ka_[0:64, :], ps[mbk][0:64, 0:cw], [r_ps[mbk]], [rk_])

            groups = []
            for h in range(4):
                slot = h % nslots

                def qf(c, slot=slot):
                    a_, r_ = qs_(slot, c)
                    return a_[0:96, :], [r_]

                def kf(kt, slot=slot):
                    a_, r_ = ks_(slot, kt * 128 // cw)
                    o_ = (kt * 128) % cw
                    return a_[0:96, o_:o_ + 128], [r_]

                def vf(kt, h=h):
                    return vaug[:, kt, h * 65:(h + 1) * 65], [r_v[kt]]
                u = dict(q=qf, k=kf, v=vf, scale=96 ** -0.5)
                pre = None
                if nslots == 2:
                    if h + 1 < 4:
                        pre = (lambda hh=h + 1: head_proj(hh))
                elif h >= 1:
                    pre = (lambda hh=h: head_proj(hh))
                groups.append(dict(units=[u], epilogue=std_epilogue(s, S, 0, 384, h), pre=pre))
            head_proj(0)
            run_attention(S, groups)

        def mixer_C(l, s):
            S = seq_lens[s]
            cw = min(512, S)
            NC = S // cw
            P.wait_all("pool", [r_wmix])
            load_cast(l, OFF["xc"], 256, 0)
            load_cast(l, OFF["gc"], 256, 256)
            xcv = vaug[:].rearrange("p t d -> p (t d)").bitcast(F32)
            xconv = kbuf[:].rearrange("p a s -> p (a s)").bitcast(F32)
            hsum = qbuf[:].rearrange("p a s -> p (a s)").bitcast(F32)
            xcb = lat[:, :]
            allr = [r for rr in (r_lat, r_v) for r in rr] + [r for sl in (r_q, r_k) for rr in sl for r in rr]
            r_xc, r_xconv, r_hsum, r_xcb = Res("xc"), Res("xconv"), Res("hsum"), Res("xcb")
            for e_ in ("dve", "pool", "act", "pe"):
                P.wait_all(e_, allr)
            for cc in range(2):
                MEMSET("dve", xcv[:, 0:2], 0.0, [r_xc])
                MEMSET("dve", xcv[:, S + 2:S + 4], 0.0, [r_xc])
                for c in range(NC):
                    mb = proj_fm(S, c, cc * 128)
                    if c % 2 == 0:
                        CP("dve", xcv[:, 2 + c * cw:2 + (c + 1) * cw], ps[mb][:, 0:cw], [r_ps[mb]], [r_xc])
                    else:
                        ACT(xcv[:, 2 + c * cw:2 + (c + 1) * cw], ps[mb][:, 0:cw], AF.Copy, [r_ps[mb]], [r_xc])
                cwc = lambda j: pc[:, PC_CW + cc * 4 + j:PC_CW + cc * 4 + j + 1]
                for c in range(NC):
                    sl = slice(c * cw, (c + 1) * cw)
                    TS("dve", xconv[:, sl], xcv[:, c * cw:(c + 1) * cw], cwc(0), pc[:, PC_CB + cc:PC_CB + cc + 1], ALU.mult, ALU.add,
                       [r_xc, r_pc], [r_xconv])
                    for j in range(1, 4):
                        STT(xconv[:, sl], xcv[:, j + c * cw:j + (c + 1) * cw], cwc(j), xconv[:, sl], ALU.mult, ALU.add,
                            [r_xc, r_pc, r_xconv], [r_xconv])
                    ACT(xcb[:, sl], xconv[:, sl], AF.Copy, [r_xconv], [r_xcb])
                for d_ in range(2):
                    order = range(NC) if d_ == 0 else range(NC - 1, -1, -1)
                    prev = None
                    for c in order:
                        sl = slice(c * cw, (c + 1) * cw)
                        mba = misc.next()
                        MM(ps[mba][:, 0:cw], lruw[:, 0, d_, cc, :], xcb[:, sl], True, True, [r_wa, r_xcb], [r_ps[mba]])
                        mbx = misc.next()
                        MM(ps[mbx][:, 0:cw], lruw[:, 1, d_, cc, :], xcb[:, sl], True, True, [r_wa, r_xcb], [r_ps[mbx]])
                        ra, ri = tmpf_ring.next(), tmpf_ring.next()
                        ci = d_ * 2 + cc
                        ACT(tmpf[ra][:, 0:cw], ps[mba][:, 0:cw], AF.Exp, [r_ps[mba], r_wa], [r_tmpf[ra]], scale=-1.0, bias=lsp[:, 4 + ci:5 + ci])
                        ACT(tmpf[ri][:, 0:cw], ps[mbx][:, 0:cw], AF.Exp, [r_ps[mbx], r_wa], [r_tmpf[ri]], scale=-1.0, bias=lsp[:, 8 + ci:9 + ci])
                        ACT(tmpf[ra][:, 0:cw], tmpf[ra][:, 0:cw], AF.Ln, [r_tmpf[ra]], [r_tmpf[ra]], bias=1.0)
                        ACT(tmpf[ri][:, 0:cw], tmpf[ri][:, 0:cw], AF.Ln, [r_tmpf[ri]], [r_tmpf[ri]], bias=1.0)
                        ACT(tmpf[ra][:, 0:cw], tmpf[ra][:, 0:cw], AF.Exp, [r_tmpf[ra]], [r_tmpf[ra]], scale=-1.0)
                        ACT(tmpf[ri][:, 0:cw], tmpf[ri][:, 0:cw], AF.Exp, [r_tmpf[ri]], [r_tmpf[ri]], scale=-1.0)
                        ACT(tmpf[ra][:, 0:cw], tmpf[ra][:, 0:cw], AF.Exp, [r_tmpf[ra], r_wa], [r_tmpf[ra]], scale=lsp[:, ci:ci + 1])
                        rg = tmpf_ring.next()
                        TT("dve", tmpf[rg][:, 0:cw], tmpf[ra][:, 0:cw], tmpf[ra][:, 0:cw], ALU.mult, [r_tmpf[ra]], [r_tmpf[rg]])
                        TS("dve", tmpf[rg][:, 0:cw], tmpf[rg][:, 0:cw], -1.0, 1.0, ALU.mult, ALU.add, [r_tmpf[rg]], [r_tmpf[rg]])
                        ACT(tmpf[rg][:, 0:cw], tmpf[rg][:, 0:cw], AF.Ln, [r_tmpf[rg]], [r_tmpf[rg]])
                        ACT(tmpf[rg][:, 0:cw], tmpf[rg][:, 0:cw], AF.Exp, [r_tmpf[rg]], [r_tmpf[rg]], scale=0.5)
                        TT("pool", tmpf[ri][:, 0:cw], tmpf[ri][:, 0:cw], xconv[:, sl], ALU.mult, [r_tmpf[ri], r_xconv], [r_tmpf[ri]])
                        TT("dve", tmpf[rg][:, 0:cw], tmpf[rg][:, 0:cw], tmpf[ri][:, 0:cw], ALU.mult, [r_tmpf[rg], r_tmpf[ri]], [r_tmpf[rg]])
                        init = 0.0 if prev is None else prev
                        if d_ == 0:
                            P.op("dve", lambda e, ra=ra, rg=rg, init=init, ri=ri: e.tensor_tensor_scan(
                                out=tmpf[ri][:, 0:cw], data0=tmpf[ra][:, 0:cw], data1=tmpf[rg][:, 0:cw], initial=init,
                                op0=ALU.mult, op1=ALU.add), reads=[r_tmpf[ra], r_tmpf[rg], r_small], writes=[r_tmpf[ri]])
                            CP("dve", small[:, 40:41], tmpf[ri][:, cw - 1:cw], [r_tmpf[ri]], [r_small])
                            prev = small[:, 40:41]
                            CP("pool", hsum[:, sl], tmpf[ri][:, 0:cw], [r_tmpf[ri]], [r_hsum])
                        else:
                            P.op("dve", lambda e, ra=ra, rg=rg, init=init, ri=ri: e.tensor_tensor_scan(
                                out=tmpf[ri][:, 0:cw][:, ::-1], data0=tmpf[ra][:, 0:cw][:, ::-1], data1=tmpf[rg][:, 0:cw][:, ::-1],
                                initial=init, op0=ALU.mult, op1=ALU.add), reads=[r_tmpf[ra], r_tmpf[rg], r_small], writes=[r_tmpf[ri]])
                            CP("dve", small[:, 41:42], tmpf[ri][:, 0:1], [r_tmpf[ri]], [r_small])
                            prev = small[:, 41:42]
                            TT("pool", hsum[:, sl], hsum[:, sl], tmpf[ri][:, 0:cw], ALU.add, [r_hsum, r_tmpf[ri]], [r_hsum])
                for c in range(NC):
                    sl = slice(c * cw, (c + 1) * cw)
                    gb = proj_fm(S, c, 256 + cc * 128)
                    a = tmpf_ring.next()
                    ACT(tmpf[a][:, 0:cw], ps[gb][:, 0:cw], AF.Exp, [r_ps[gb]], [r_tmpf[a]], scale=-1.0)
                    ACT(tmpf[a][:, 0:cw], tmpf[a][:, 0:cw], AF.Ln, [r_tmpf[a]], [r_tmpf[a]], bias=1.0)
                    ACT(tmpf[a][:, 0:cw], tmpf[a][:, 0:cw], AF.Exp, [r_tmpf[a]], [r_tmpf[a]], scale=-1.0)
                    TT("dve", tmpf[a][:, 0:cw], ps[gb][:, 0:cw], tmpf[a][:, 0:cw], ALU.mult, [r_ps[gb], r_tmpf[a]], [r_tmpf[a]])
                    b = tmpb_ring.next()
                    TT("dve", tmpb[b][:, 0:cw], tmpf[a][:, 0:cw], hsum[:, sl], ALU.mult, [r_tmpf[a], r_hsum], [r_tmpb[b]])
                    P.dma("sp", f"oc{b}", ocscr[s][cc * 128:(cc + 1) * 128, sl], tmpb[b][:, 0:cw], reads=[r_tmpb[b]], writes=[r_ocscr[s]])
            MEMSET("pool", vaug[:], 1.0, [r_xcb, r_xc, r_xconv, r_hsum])
            for e_ in ("dve", "pool", "act", "pe"):
                P.wait_all(e_, [r_xcb, r_xc, r_xconv, r_hsum])

        def phase3(l, s, last):
            S = seq_lens[s]
            P.wait_all("pool", [r_wmix])
            for k in range(8):
                i = wst_ring.next()
                LOAD(f"wst{i}", wst[i][:, 0:1024], w_out[l][k * 128:(k + 1) * 128, :], [], [r_wst[i]])
                TT("dve", wmix[:, k, 0:1024], wst[i][:, 0:1024], gate_bc[:, :], ALU.mult, [r_wst[i], r_gate], [r_wmix])
            src = x_in[s] if l == 0 else xres[s]
            if last:
                LOAD("c3", fn_bc, fnorm.partition_broadcast(128), [], [r_mod])
            for t in range(S // 128):
                jb = t % 2
                ot = tmpf[jb][:].bitcast(BF16)[:, 0:768]
                ofm = tmpf[2 + jb][:].bitcast(BF16).rearrange("p (k n) -> p k n", n=128)
                r_ot, r_ofm = r_tmpf[jb], r_tmpf[2 + jb]
                P.dma("sp", f"otl{jb}", ot, oscr[s][t * 128:(t + 1) * 128, :], reads=[r_oscr[s][t]], writes=[r_ot])
                P.dma("sp", f"ofl{jb}", ofm[:, 4:6, :], ocscr[s][:, t * 128:(t + 1) * 128].rearrange("(c p) n -> p c n", p=128),
                      reads=[r_ocscr[s]], writes=[r_ofm])
                mb = misc.next()
                pT = ps[mb][:].bitcast(BF16)
                for j in range(6):
                    TR(pT[:, j * 128:(j + 1) * 128], ot[:, j * 128:(j + 1) * 128], ident_b[:], [r_ot, r_const], [r_ps[mb]], sig=(j == 5))
                CP("dve", ofm[:, 0:4, :], pT[:, 0:512].rearrange("p (j n) -> p j n", n=128), [r_ps[mb]], [r_ofm])
                CP("dve", ofm[:, 6:8, :], pT[:, 512:768].rearrange("p (j n) -> p j n", n=128), [r_ps[mb]], [r_ofm])
                xi = xt_ring.next()
                rd = [] if l == 0 else [r_xres[s][t]]
                LOAD(f"xt{xi}", xt[xi][:], src[t * 128:(t + 1) * 128, :], rd, [r_xt[xi]])
                for n in range(2):
                    yb = misc.next()
                    for k in range(8):
                        MM(ps[yb][:, :], ofm[:, k, :], wmix[:, k, n * 512:(n + 1) * 512], k == 0, k == 7, [r_ofm, r_wmix], [r_ps[yb]], sig=(k == 7))
                    TT("dve", xt[xi][:, n * 512:(n + 1) * 512], xt[xi][:, n * 512:(n + 1) * 512], ps[yb][:, :], ALU.add,
                       [r_ps[yb], r_xt[xi]], [r_xt[xi]])
                if not last:
                    P.dma("sp", f"xst{xi}", xres[s][t * 128:(t + 1) * 128, :], xt[xi][:], reads=[r_xt[xi]], writes=[r_xres[s][t]])
                else:
                    ACT(xn[:], xt[xi][:], AF.Square, [r_xt[xi]], [r_pt[0], r_pt[1], r_small], accum=small[:, 17:18])
                    rstd_act(small[:, 17:18], small[:, 17:18], float(D), [r_small], [r_small])
                    STT(xt[xi][:], xt[xi][:], small[:, 17:18], fn_bc, ALU.mult, ALU.mult, [r_xt[xi], r_small, r_mod], [r_xt[xi]])
                    P.dma("sp", f"xst{xi}", y_out[s][t * 128:(t + 1) * 128, :], xt[xi][:], reads=[r_xt[xi]], writes=[r_y])

        try:
          for l in range(depth):
            layer_prep(l)
            for s in range(nseq):
                phase0(l, s)
                phase1(l, s)
                if _os.environ.get("KSTOP") == "p1":
                    raise StopIteration
                if "A" in mixers:
                    mixer_A(l, s)
                if "B" in mixers:
                    mixer_B(l, s)
                if "C" in mixers:
                    mixer_C(l, s)
                if "D" in mixers:
                    mixer_D(l, s)
                phase3(l, s, l == depth - 1)
        except StopIteration:
            pass
        allres = [r_y] + [r for rr in r_xres for r in rr] + [r for rr in r_oscr for r in rr] + r_ocscr
        P.wait_all("sp", allres)
        P.emit()
    return nc, P


def _rope_tables(Smax):
    pos = np.arange(Smax, dtype=np.float32)

    def cs(p, theta, half):
        inv = np.power(np.float32(theta), -np.arange(half, dtype=np.float32) / np.float32(half)).astype(np.float32)
        ang = (p[:, None] * inv[None, :]).astype(np.float32)
        return np.cos(ang).astype(np.float32).T, np.sin(ang).astype(np.float32).T

    c, s_ = cs(pos, 10000.0, 16)
    tab_mla = np.zeros((2, 32, Smax), np.float32)
    tab_mla[0, 0:16] = c
    tab_mla[0, 16:32] = c
    tab_mla[1, 0:16] = -s_
    tab_mla[1, 16:32] = s_
    c, s_ = cs(pos, 500000.0, 4)
    blk_c = np.ones((32, Smax), np.float32)
    blk_s = np.zeros((32, Smax), np.float32)
    blk_c[0:4] = c
    blk_c[4:8] = c
    blk_s[0:4] = -s_
    blk_s[4:8] = s_
    tab_diff = np.stack([np.tile(blk_c, (4, 1)), np.tile(blk_s, (4, 1))])
    row = np.floor(pos / 64.0).astype(np.float32)
    col = (pos - row * 64.0).astype(np.float32)
    cr, sr = cs(row, 10000.0, 16)
    cc, sc_ = cs(col, 10000.0, 16)
    blk_c = np.concatenate([cr, cr, cc, cc], 0)
    blk_s = np.concatenate([-sr, sr, -sc_, sc_], 0)
    tab_gqa = np.stack([np.tile(blk_c, (2, 1)), np.tile(blk_s, (2, 1))])
    return tab_mla, np.ascontiguousarray(tab_diff), np.ascontiguousarray(tab_gqa)


def _host_layout(inp, depth):
    f = np.float32
    pcol = np.zeros((depth, 128, NPC), f)
    prow = np.zeros((depth, NPR), f)
    p = np.arange(128)
    partner = np.where((p % 32) < 16, p + 16, p - 16) % 64
    for l in range(depth):
        pcol[l, :, PC_NG:PC_NG + 8] = inp["norm_g"][l].reshape(8, 128).T
        pcol[l, :, PC_ADAB:PC_ADAB + 24] = inp["ada_b"][l].reshape(24, 128).T
        pcol[l, :, PC_QN] = inp["mla_q_norm"][l][0:128]
        pcol[l, 0:64, PC_QN + 1] = inp["mla_q_norm"][l][128:192]
        pcol[l, :, PC_KVN] = inp["mla_kv_norm"][l]
        pcol[l, :, PC_GQ] = inp["gqa_q_norm"][l][p % 64]
        pcol[l, :, PC_GK] = inp["gqa_k_norm"][l][p % 64]
        pcol[l, :, PC_GQS] = inp["gqa_q_norm"][l][partner]
        pcol[l, :, PC_GKS] = inp["gqa_k_norm"][l][partner]
        pcol[l, :, PC_CB:PC_CB + 2] = inp["lru_conv_b"][l].reshape(2, 128).T
        pcol[l, :, PC_CW:PC_CW + 8] = inp["lru_conv_w"][l].reshape(4, 2, 128).transpose(2, 1, 0).reshape(128, 8)
        pcol[l, :, PC_BA:PC_BA + 4] = inp["lru_ba"][l].reshape(2, 2, 128).transpose(2, 0, 1).reshape(128, 4)
        pcol[l, :, PC_BX:PC_BX + 4] = inp["lru_bx"][l].reshape(2, 2, 128).transpose(2, 0, 1).reshape(128, 4)
        pcol[l, :, PC_LAM:PC_LAM + 4] = inp["lru_lambda"][l].reshape(2, 2, 128).transpose(2, 0, 1).reshape(128, 4)
        prow[l, PR_LAM:PR_LAM + 128] = inp["diff_lambda"][l].reshape(-1)
        prow[l, PR_SUB:PR_SUB + 64] = inp["diff_subln"][l]
    lru_w = np.ascontiguousarray(np.stack([inp["lru_wa"], inp["lru_wx"]], axis=1)).astype(f)
    return pcol, prow, lru_w


_CACHE = {}


def run(inputs, seqs_per_core, n_cores, mixers="ABCD"):
    depth = inputs["w_in"].shape[0]
    f = np.float32
    xs = {"p": np.asarray(inputs["x_prompt"], f), "s": np.asarray(inputs["x_sample"], f)}
    cs_ = {"p": np.asarray(inputs["c_prompt"], f), "s": np.asarray(inputs["c_sample"], f)}
    seq_lens = [xs[w].shape[1] for (w, _) in seqs_per_core[0]]
    Smax = max(seq_lens)
    key = (tuple(seq_lens), depth, mixers)
    if key not in _CACHE:
        _CACHE[key] = build(seq_lens, depth, mixers)
    nc, _ = _CACHE[key]
    pcol, prow, lru_w = _host_layout({k: np.asarray(v, f) for k, v in inputs.items()}, depth)
    tab_mla, tab_diff, tab_gqa = _rope_tables(Smax)
    shared = {
        "ada_w": np.asarray(inputs["ada_w"], f), "ada_b": np.asarray(inputs["ada_b"], f),
        "w_in": np.asarray(inputs["w_in"], f), "w_out": np.asarray(inputs["w_out"], f),
        "mla_w_uq": np.asarray(inputs["mla_w_uq"], f), "mla_w_ukv": np.asarray(inputs["mla_w_ukv"], f),
        "lru_w": lru_w, "pcol": pcol, "prow": prow, "final_norm": np.asarray(inputs["final_norm"], f).reshape(1, -1),
        "ident_b": np.eye(128, dtype=f).astype(ml_dtypes.bfloat16), "ident_f": np.eye(128, dtype=f),
        "tab_mla": tab_mla, "tab_diff": tab_diff, "tab_gqa": tab_gqa,
    }
    in_maps = []
    for core in range(n_cores):
        m = dict(shared)
        cf = np.zeros((len(seq_lens), 128, 8), f)
        for i, (w, idx) in enumerate(seqs_per_core[core]):
            m[f"x{i}"] = np.ascontiguousarray(xs[w][idx])
            cf[i] = cs_[w][idx].reshape(8, 128).T
        m["cfm"] = cf
        in_maps.append(m)
    res = run_bass_kernel_spmd(nc, in_maps, core_ids=list(range(n_cores)))
    yp = np.zeros_like(xs["p"])
    ys = np.zeros_like(xs["s"])
    out = {"p": yp, "s": ys}
    for core in range(n_cores):
        for i, (w, idx) in enumerate(seqs_per_core[core]):
            out[w][idx] = res.results[core][f"y{i}"]
    return yp, ys


def kernel(x_prompt, x_sample, c_prompt, c_sample, **weights):
    inputs = dict(x_prompt=x_prompt, x_sample=x_sample, c_prompt=c_prompt, c_sample=c_sample, **weights)
    n_cores = 8
    bp = x_prompt.shape[0] // n_cores
    bs = x_sample.shape[0] // n_cores
    spc = [[("p", c * bp + j) for j in range(bp)] + [("s", c * bs + j) for j in range(bs)] for c in range(n_cores)]
    return run(inputs, spc, n_cores)
```

```python
import contextlib
import math
import numpy as np
import ml_dtypes
import concourse.bass as bass
import concourse.mybir as mybir
from concourse.bass_utils import run_bass_kernel_spmd

F32 = mybir.dt.float32
BF16 = mybir.dt.bfloat16
AF = mybir.ActivationFunctionType
ALU = mybir.AluOpType
AX = mybir.AxisListType

D = 1024
NIN = 2912
EPS = 1e-6
OFF = dict(qa=0, kva=192, kra=320, ga=352, qb=608, kb=864, vb=1120, gb=1376,
           xc=1632, gc=1888, qd=2144, kd=2400, vd=2528, gd=2656)
NPC = 64
PC_NG = 0
PC_ADAB = 8
PC_QN = 32
PC_KVN = 34
PC_GQ = 35
PC_GK = 36
PC_GQS = 37
PC_GKS = 38
PC_CB = 39
PC_CW = 41
PC_BA = 49
PC_BX = 53
PC_LAM = 57
PR_LAM = 0
PR_SUB = 128
NPR = 192


class Res:
    __slots__ = ("name", "w", "r", "excl")

    def __init__(self, name, excl=False):
        self.name = name
        self.w = {}
        self.r = {}
        self.excl = excl


class Prog:
    ENGS = ("pe", "act", "dve", "pool", "sp")

    def __init__(self, nc):
        self.nc = nc
        self.count = {e: 0 for e in self.ENGS}
        self.waited = {e: {} for e in self.ENGS}
        self.stream = {e: [] for e in self.ENGS}
        self.dma_count = {}
        self.n_ins = 0

    def _deps(self, eng, reads, writes):
        need = {}

        def add(s, v, same_ok):
            if s == eng and eng in ("pe", "sp"):
                return
            if s in self.dma_count:
                v = self.dma_count[s]
            if need.get(s, 0) < v:
                need[s] = v
        for r in reads:
            for s, v in r.w.items():
                add(s, v, False)
            if r.excl:
                for s, v in r.r.items():
                    if s != eng:
                        add(s, v, False)
        for w in writes:
            for s, v in w.w.items():
                add(s, v, True)
            for s, v in w.r.items():
                add(s, v, True)
        out = []
        for s, v in need.items():
            if self.waited[eng].get(s, 0) >= v:
                continue
            self.waited[eng][s] = v
            out.append((s, v))
        return out

    def _commit(self, tok, reads, writes):
        s, v = tok
        for r in reads:
            if r.r.get(s, 0) < v:
                r.r[s] = v
        for w in writes:
            w.w[s] = v
            w.r = {}

    def op(self, eng, fn, reads=(), writes=(), signal=True):
        waits = self._deps(eng, reads, writes)
        if signal:
            self.count[eng] += 1
            tok = (eng, self.count[eng])
        else:
            tok = (eng, self.count[eng] + 1)
        self._commit(tok, reads, writes)
        self.stream[eng].append((waits, fn, eng if signal else None, 1))
        self.n_ins += 1

    def dma(self, queue, key, out, in_, reads=(), writes=(), **kw):
        waits = self._deps(queue, reads, writes)
        self.dma_count[key] = self.dma_count.get(key, 0) + 16
        tok = (key, self.dma_count[key])
        self._commit(tok, reads, writes)

        def fn(e, out=out, in_=in_, kw=kw):
            return e.dma_start(out=out, in_=in_, **kw)
        self.stream[queue].append((waits, fn, key, 16))
        self.n_ins += 1

    def wait_all(self, eng, res_list):
        waits = self._deps(eng, (), res_list)
        self.stream[eng].append((waits, None, None, 0))

    def emit(self):
        nc = self.nc
        with contextlib.ExitStack() as st:
            sems = {}
            for e in self.ENGS:
                sems[e] = st.enter_context(nc.semaphore("s_" + e))
            for k in self.dma_count:
                sems[k] = st.enter_context(nc.semaphore("d_" + k))
            block = st.enter_context(nc.Block())

            def replay(ename):
                def body(e):
                    for waits, fn, sig, inc in self.stream[ename]:
                        for s, v in waits:
                            e.wait_ge(sems[s], v)
                        if fn is None:
                            continue
                        ins = fn(e)
                        if sig is not None:
                            ins.then_inc(sems[sig], inc)
                return body

            block.tensor(replay("pe"))
            block.scalar(replay("act"))
            block.vector(replay("dve"))
            block.gpsimd(replay("pool"))
            block.sync(replay("sp"))


class Ring:
    def __init__(self, items):
        self.items = items
        self.i = 0

    def next(self):
        it = self.items[self.i % len(self.items)]
        self.i += 1
        return it


def build(seq_lens, depth, mixers="ABCD"):
    nc = bass.Bass("TRN2", target_bir_lowering=False)
    nseq = len(seq_lens)
    Smax = max(seq_lens)
    NTmax = Smax // 128

    def din(name, shape, dt=F32):
        return nc.dram_tensor(name, list(shape), dt, kind="ExternalInput").ap()

    x_in = [din(f"x{i}", [S, D]) for i, S in enumerate(seq_lens)]
    y_out = [nc.dram_tensor(f"y{i}", [S, D], F32, kind="ExternalOutput").ap() for i, S in enumerate(seq_lens)]
    cfm = din("cfm", [nseq, 128, 8])
    ada_w = din("ada_w", [depth, D, 3 * D])
    ada_b = din("ada_b", [depth, 3 * D])
    w_in = din("w_in", [depth, D, NIN])
    w_out = din("w_out", [depth, D, D])
    w_uq = din("mla_w_uq", [depth, 192, 384])
    w_ukv = din("mla_w_ukv", [depth, 128, 512])
    lru_w = din("lru_w", [depth, 2, 2, 4, 64, 64])
    pcol = din("pcol", [depth, 128, NPC])
    prow = din("prow", [depth, NPR])
    fnorm = din("final_norm", [1, D])
    ident_b_d = din("ident_b", [128, 128], BF16)
    ident_f_d = din("ident_f", [128, 128])
    tab_mla = din("tab_mla", [2, 32, Smax])
    tab_diff = din("tab_diff", [2, 128, Smax])
    tab_gqa = din("tab_gqa", [2, 128, Smax])

    xres = [nc.dram_tensor(f"xres{i}", [S, D], F32).ap() for i, S in enumerate(seq_lens)]
    oscr = [nc.dram_tensor(f"oscr{i}", [S, 768], BF16).ap() for i, S in enumerate(seq_lens)]
    ocscr = [nc.dram_tensor(f"ocscr{i}", [256, S], BF16).ap() for i, S in enumerate(seq_lens)]

    P = Prog(nc)
    st = contextlib.ExitStack()
    with st:
        sb_bytes = [0]

        def sb(name, shape, dt=F32):
            n = 1
            for d_ in shape[1:]:
                n *= d_
            sb_bytes[0] += n * (2 if dt == BF16 else 4)
            return st.enter_context(nc.sbuf_tensor(name, list(shape), dt))

        ident_b = sb("ident_b_s", [128, 128], BF16)
        ident_f = sb("ident_f_s", [128, 65])
        ones_b = sb("ones_b", [128, 128], BF16)
        bones_b = sb("bones_b", [128, 128], BF16)
        epsc = sb("epsc", [128, 1])
        h_fm = sb("h_fm", [128, 8, Smax], BF16)
        wmix = sb("wmix", [128, 8, 1536], BF16)
        wst = [sb(f"wst{i}", [128, 1024]) for i in range(2)]
        qbuf = sb("qbuf", [128, 2, Smax], BF16)
        kbuf = sb("kbuf", [128, 2, Smax], BF16)
        lat = sb("lat", [128, Smax], BF16)
        vaug = sb("vaug", [128, NTmax, 260], BF16)
        xt = [sb(f"xt{i}", [128, 1024]) for i in range(2)]
        tabc = [sb(f"tabc{i}", [128, 512]) for i in range(1)]
        tabs = [sb(f"tabs{i}", [128, 512]) for i in range(1)]
        ptbuf = sb("ptbuf", [128, 4, 512], BF16)
        qz = sb("qz", [128, 4, 512], BF16)
        pts = [ptbuf[:, i, :] for i in range(4)]
        xn = ptbuf[:, 0:2, :].rearrange("p a n -> p (a n)")
        tmpf = [sb(f"tmpf{i}", [128, 512]) for i in range(4)]
        tmpb = [sb(f"tmpb{i}", [128, 512], BF16) for i in range(3)]
        tmpo = [sb(f"tmpo{i}", [128, 512]) for i in range(2)]
        tmpg = [sb(f"tmpg{i}", [128, 256]) for i in range(2)]
        ogs = [sb(f"og{i}", [128, 256], BF16) for i in range(2)]
        pc = sb("pc", [128, NPC])
        pr = sb("pr", [128, NPR])
        sc = sb("sc", [128, 8])
        sc_bc = sb("sc_bc", [128, 8, 128])
        fn_bc = sc_bc[:].rearrange("p k n -> p (k n)")
        shiftcol = sb("shiftcol", [128, 8])
        gscol = sb("gscol", [128, 8])
        gate_bc = sb("gate_bc", [128, 1024])
        small = sb("small", [128, 64])
        wuq = sb("wuq", [128, 2, 384], BF16)
        wuqs = sb("wuqs", [128, 2, 384], BF16)
        wukv = sb("wukv", [128, 512], BF16)
        lamc = sb("lamc", [128, 4])
        subln_bc = sb("subln_bc", [128, 4, 64])
        lruw = sb("lruw", [128, 2, 2, 2, 128], BF16)
        lsp = sb("lsp", [128, 12])

        import os as _os0
        if _os0.environ.get("KDEBUG"):
            print("SBUF bytes/partition:", sb_bytes[0])
        ps = [st.enter_context(nc.psum_tensor(f"ps{i}", [128, 512], F32)) for i in range(8)]
        r_ps = [Res(f"ps{i}", excl=True) for i in range(8)]
        SB_BANKS = (0, 1, 2, 3)
        O_BANKS = (4, 5)
        O_ALL = (4, 5, 6, 7)
        misc = Ring([6, 7])

        r_const = Res("const")
        r_h = [Res(f"h{t}") for t in range(NTmax)]
        r_wmix = Res("wmix")
        r_wst = [Res("wst0"), Res("wst1")]
        wst_ring = Ring([0, 1])
        NCmax = Smax // 512 if Smax >= 512 else 1
        r_q = [[Res(f"q{s}_{c}") for c in range(NTmax)] for s in range(2)]
        r_k = [[Res(f"k{s}_{c}") for c in range(NTmax)] for s in range(2)]
        r_lat = [Res(f"lat{c}") for c in range(NTmax)]
        r_v = [Res(f"v{c}") for c in range(NTmax)]
        r_xt = [Res("xt0"), Res("xt1")]
        xt_ring = Ring([0, 1])
        r_xn = Res("xn")
        r_tab = [Res("tab0"), Res("tab1")]
        tab_ring = Ring([0])
        r_pt = [Res(f"pt{i}") for i in range(4)]
        r_qz = [Res(f"qz{i}") for i in range(4)]
        r_tmpf = [Res(f"tmpf{i}") for i in range(4)]
        tmpf_ring = Ring([0, 1, 2, 3])
        r_tmpb = [Res(f"tmpb{i}") for i in range(3)]
        r_tmpo = [Res("tmpo0"), Res("tmpo1")]
        tmpo_ring = Ring([0, 1])
        r_tmpg = [Res("tmpg0"), Res("tmpg1")]
        tmpg_ring = Ring([0, 1])
        tmpb_ring = Ring([0, 1, 2])
        r_og = [Res("og0"), Res("og1")]
        og_ring = Ring([0, 1])
        r_pc = Res("pc")
        r_mod = Res("mod")
        r_gate = Res("gate")
        r_small = Res("small")
        rcol = [small[:, 44:45], small[:, 46:47]]
        r_rcol = [Res("rcol0"), Res("rcol1")]
        r_wa = Res("wa")
        r_ot = Res("ot")
        r_ofm = Res("ofm")
        r_oscr = [[Res(f"oscr{i}_{t}") for t in range(S // 128)] for i, S in enumerate(seq_lens)]
        r_ocscr = [Res(f"ocscr{i}") for i in range(nseq)]
        r_xres = [[Res(f"xres{i}_{t}") for t in range(S // 128)] for i, S in enumerate(seq_lens)]
        r_y = Res("y")

        def MM(out, lhsT, rhs, start, stop, rd, wr, sig=True, **kw):
            P.op("pe", lambda e: e.matmul(out, lhsT=lhsT, rhs=rhs, start=start, stop=stop, **kw),
                 reads=rd, writes=wr, signal=sig)

        def TR(out, in_, ident, rd, wr, sig=True):
            P.op("pe", lambda e: e.transpose(out, in_, ident), reads=rd, writes=wr, signal=sig)

        def ACT(out, in_, func, rd, wr, scale=1.0, bias=None, accum=None):
            def fn(e):
                kw = {}
                if bias is not None:
                    kw["bias"] = bias
                if accum is not None:
                    kw["accum_out"] = accum
                return e.activation(out=out, in_=in_, func=func, scale=scale, **kw)
            P.op("act", fn, reads=rd, writes=wr)

        def eng_of(P_, name):
            return name

        def TT(eng, out, in0, in1, op, rd, wr):
            P.op(eng, lambda e: e.tensor_tensor(out=out, in0=in0, in1=in1, op=op), reads=rd, writes=wr)

        def TS(eng, out, in0, s1, s2, op0, op1, rd, wr):
            if op1 is None:
                P.op(eng, lambda e: e.tensor_scalar(out=out, in0=in0, scalar1=s1, scalar2=None, op0=op0), reads=rd, writes=wr)
            else:
                P.op(eng, lambda e: e.tensor_scalar(out=out, in0=in0, scalar1=s1, scalar2=s2, op0=op0, op1=op1), reads=rd, writes=wr)

        def STT(out, in0, scalar, in1, op0, op1, rd, wr):
            P.op("dve", lambda e: e.scalar_tensor_tensor(out=out, in0=in0, scalar=scalar, in1=in1, op0=op0, op1=op1),
                 reads=rd, writes=wr)

        def CP(eng, out, in_, rd, wr):
            P.op(eng, lambda e: e.tensor_copy(out=out, in_=in_), reads=rd, writes=wr)

        def RCP(out, in_, rd, wr):
            P.op("dve", lambda e: e.reciprocal(out=out, in_=in_), reads=rd, writes=wr)

        def MEMSET(eng, ap, val, wr):
            P.op(eng, lambda e: e.memset(ap, val), writes=wr)

        dma_i = [0]
        import os as _os
        ROPE_ENG = _os.environ.get("ROPE_ENG", "dve")

        def LOAD(key, out, in_, rd, wr, **kw):
            P.dma("sp", key, out, in_, reads=rd, writes=wr, **kw)

        def rstd_act(out, in_, n, rd, wr):
            ACT(out, in_, AF.Ln, list(rd) + [r_const], wr, scale=1.0 / n, bias=epsc[0:out.shape[0], 0:1])
            ACT(out, out, AF.Exp, wr, wr, scale=-0.5)

        LOAD("c0", ident_b[:], ident_b_d, [], [r_const])
        LOAD("c0", ident_f[:], ident_f_d[:, 0:65], [], [r_const])
        MEMSET("pool", ones_b[:], 1.0, [r_const])
        MEMSET("pool", bones_b[:], 0.0, [r_const])
        MEMSET("pool", bones_b[0:64, 0:64], 1.0, [r_const])
        MEMSET("pool", bones_b[64:128, 64:128], 1.0, [r_const])
        MEMSET("pool", epsc[:], EPS, [r_const])
        MEMSET("pool", vaug[:], 1.0, r_v)
        MEMSET("pool", qz[:], 0.0, r_qz)

        if len(mixers) < 4:
            zt = sb("zt", [128, 768], BF16)
            r_zt = Res("zt")
            MEMSET("pool", zt[:], 0.0, [r_zt])
            for i_, S_ in enumerate(seq_lens):
                for t_ in range(S_ // 128):
                    P.dma("sp", "zst", oscr[i_][t_ * 128:(t_ + 1) * 128, :], zt[:], reads=[r_zt], writes=[r_oscr[i_][t_]])
                    for cc_ in range(2):
                        P.dma("sp", "zst", ocscr[i_][cc_ * 128:(cc_ + 1) * 128, t_ * 128:(t_ + 1) * 128], zt[:, 0:128], reads=[r_zt], writes=[r_ocscr[i_]])

        def load_w(src2d, ncols, dst_col, swap=None, rows=D, dst=None, gaincol=None):
            nk = rows // 128
            i = wst_ring.next()
            stg = wst[i][:, 0:nk * ncols].rearrange("p (k n) -> p k n", k=nk)
            LOAD(f"wst{i}", stg, src2d.rearrange("(k p) n -> p k n", p=128), [], [r_wst[i]])
            return i, stg

        cast_i = [0]

        def cast_cols(stg_ap, i, dst_ap, eng=None):
            cast_i[0] += 1
            if cast_i[0] % 2 == 0:
                ACT(dst_ap, stg_ap, AF.Copy, [r_wst[i]], [r_wmix])
            else:
                CP("dve", dst_ap, stg_ap, [r_wst[i]], [r_wmix])

        def load_cast(l, src_c0, n, dst_c0, extra=None):
            for p0 in range(0, n, 128):
                pn = min(128, n - p0)
                i, stg = load_w(w_in[l][:, src_c0 + p0:src_c0 + p0 + pn], pn, 0)
                cast_cols(stg, i, wmix[:, :, dst_c0 + p0:dst_c0 + p0 + pn])
                if extra is not None:
                    extra(i, stg, p0, pn)

        def swapped(ap3, half):
            return ap3.rearrange("p k (m b j) -> p k m b j", b=2, j=half)

        def layer_prep(l):
            LOAD("c1", pc[:], pcol[l], [r_pc, r_mod, r_wa], [r_pc])
            LOAD("c1", pr[:], prow[l:l + 1, :].partition_broadcast(128), [r_pc], [r_pc])
            if "A" in mixers:
                i = wst_ring.next()
                LOAD(f"wst{i}", wst[i][:, 0:384], w_uq[l][0:128, :], [], [r_wst[i]])
                LOAD(f"wst{i}", wst[i][0:64, 384:768], w_uq[l][128:192, :], [], [r_wst[i]])
                for kk, rows in ((0, 128), (1, 64)):
                    TS("pool", wuq[0:rows, kk, :], wst[i][0:rows, kk * 384:(kk + 1) * 384], pc[0:rows, PC_QN + kk:PC_QN + kk + 1], None,
                       ALU.mult, None, [r_wst[i], r_pc], [r_wa])
                    CP("pool", wuqs[0:rows, kk, :], wuq[0:rows, kk, :], [r_wa], [r_wa])
                    v = wuq[0:rows, kk, :].rearrange("p (h c) -> p h c", c=96)
                    vs = wuqs[0:rows, kk, :].rearrange("p (h c) -> p h c", c=96)
                    CP("pool", vs[:, :, 64:80], v[:, :, 80:96], [r_wa], [r_wa])
                    CP("pool", vs[:, :, 80:96], v[:, :, 64:80], [r_wa], [r_wa])
                i = wst_ring.next()
                LOAD(f"wst{i}", wst[i][:, 0:512], w_ukv[l], [], [r_wst[i]])
                sv = wst[i][:, 0:512].rearrange("p (h t c) -> p h t c", h=4, t=2)
                dv_ = wukv[:, :].rearrange("p (t h c) -> p t h c", t=2, h=4)
                for t_ in range(2):
                    TS("pool", dv_[:, t_], sv[:, :, t_, :], pc[:, PC_KVN:PC_KVN + 1], None, ALU.mult, None,
                       [r_wst[i], r_pc], [r_wa])
            if "B" in mixers:
                lam_init = 0.8 - 0.6 * math.exp(-0.3 * l)
                lv = pr[:, PR_LAM:PR_LAM + 128].rearrange("p (a d) -> p a d", a=4)
                TT("dve", small[:, 0:32], lv[:, 0, :], lv[:, 1, :], ALU.mult, [r_pc], [r_small])
                TT("dve", small[:, 32:64], lv[:, 2, :], lv[:, 3, :], ALU.mult, [r_pc], [r_small])
                P.op("dve", lambda e: e.tensor_reduce(out=lamc[:, 0:2], in_=small[:, 0:64].rearrange("p (a d) -> p a d", a=2),
                                                     axis=AX.X, op=ALU.add), reads=[r_small], writes=[r_wa])
                ACT(lamc[:, 0:2], lamc[:, 0:2], AF.Exp, [r_wa], [r_wa])
                TT("dve", lamc[:, 2:3], lamc[:, 0:1], lamc[:, 1:2], ALU.subtract, [r_wa], [r_wa])
                TS("dve", lamc[:, 3:4], lamc[:, 2:3], -1.0, -lam_init, ALU.mult, ALU.add, [r_wa], [r_wa])
                for qb in range(4):
                    TS("dve", subln_bc[:, qb, :], pr[:, PR_SUB:PR_SUB + 64], 1.0 - lam_init, None, ALU.mult, None,
                       [r_pc], [r_wa])
            if "C" in mixers:
                i = wst_ring.next()
                MEMSET("pool", wst[i][:, 0:1024], 0.0, [r_wst[i]])
                for m in range(2):
                    for d_ in range(2):
                        for hh in range(4):
                            cc, hb = hh // 2, (hh % 2) * 64
                            col = ((m * 2 + d_) * 2 + cc) * 128 + hb
                            LOAD(f"wst{i}", wst[i][hb:hb + 64, col:col + 64], lru_w[l, m, d_, hh], [], [r_wst[i]])
                CP("pool", lruw[:].rearrange("p m d c n -> p (m d c n)"), wst[i][:, 0:1024], [r_wst[i]], [r_wa])
                ACT(lsp[:, 0:4], pc[:, PC_LAM:PC_LAM + 4], AF.Exp, [r_pc], [r_wa], scale=-1.0)
                ACT(lsp[:, 0:4], lsp[:, 0:4], AF.Ln, [r_wa], [r_wa], bias=1.0)
                TS("dve", lsp[:, 0:4], lsp[:, 0:4], -8.0, None, ALU.mult, None, [r_wa], [r_wa])
                TS("dve", lsp[:, 4:8], pc[:, PC_BA:PC_BA + 4], -1.0, None, ALU.mult, None, [r_pc], [r_wa])
                TS("dve", lsp[:, 8:12], pc[:, PC_BX:PC_BX + 4], -1.0, None, ALU.mult, None, [r_pc], [r_wa])

        def phase0(l, s):
            LOAD("c2", sc[:], cfm[s], [r_mod], [r_mod])
            ACT(small[:, 8:16], sc[:], AF.Exp, [r_mod], [r_small], scale=-1.0)
            TS("dve", small[:, 8:16], small[:, 8:16], 1.0, None, ALU.add, None, [r_small], [r_small])
            RCP(small[:, 8:16], small[:, 8:16], [r_small], [r_small])
            TT("dve", sc[:], sc[:], small[:, 8:16], ALU.mult, [r_small, r_mod], [r_mod])
            CP("dve", sc_bc[:], sc[:, :].unsqueeze(2).broadcast_to([128, 8, 128]), [r_mod], [r_mod])
            mb = misc.next()
            for j in range(24):
                i, stg = load_w(ada_w[l][:, j * 128:(j + 1) * 128], 128, 0)
                if j < 16:
                    for k in range(8):
                        MM(ps[mb][:, j * 2:j * 2 + 2], stg[:, k, :], sc_bc[:, k, 0:2],
                           k == 0, k == 7, [r_wst[i], r_mod], [r_ps[mb]], sig=(k == 7))
                    if j == 15:
                        pv = ps[mb][:, 0:32].rearrange("p (j t) -> p j t", t=2)
                        TT("dve", shiftcol[:], pv[:, 0:8, 0], pc[:, PC_ADAB:PC_ADAB + 8], ALU.add, [r_ps[mb], r_pc], [r_mod])
                        TT("dve", gscol[:], pv[:, 8:16, 0], pc[:, PC_ADAB + 8:PC_ADAB + 16], ALU.add, [r_ps[mb], r_pc], [r_mod])
                        STT(gscol[:], gscol[:], 1.0, pc[:, PC_NG:PC_NG + 8], ALU.add, ALU.mult, [r_mod, r_pc], [r_mod])
                else:
                    gb = misc.next()
                    c0 = (j - 16) * 128
                    a = tmpf_ring.next()
                    LOAD(f"abg{a}", tmpf[a][:, 0:128], ada_b[l:l + 1, 2 * D + c0:2 * D + c0 + 128].partition_broadcast(128),
                         [], [r_tmpf[a]])
                    for k in range(8):
                        MM(ps[gb][:, 0:128], sc_bc[:, k, :], stg[:, k, :], k == 0, k == 7, [r_wst[i], r_mod], [r_ps[gb]], sig=(k == 7))
                    TT("dve", gate_bc[:, c0:c0 + 128], ps[gb][:, 0:128], tmpf[a][:, 0:128], ALU.add,
                       [r_ps[gb], r_tmpf[a]], [r_gate])

        def phase1(l, s):
            S = seq_lens[s]
            src = x_in[s] if l == 0 else xres[s]
            misc.items = [6, 7, 0, 1, 2, 3]
            for t in range(S // 128):
                xi = xt_ring.next()
                rd = [] if l == 0 else [r_xres[s][t]]
                LOAD(f"xt{xi}", xt[xi][:], src[t * 128:(t + 1) * 128, :], rd, [r_xt[xi]])
                rc, r_rc = rcol[t % 2], r_rcol[t % 2]
                ACT(tmpf[0][:].bitcast(BF16), xt[xi][:], AF.Square, [r_xt[xi]], [r_tmpf[0], r_rc], accum=rc)
                rstd_act(rc, rc, float(D), [r_rc], [r_rc])
                TS("dve", xn[:], xt[xi][:], rc, None, ALU.mult, None, [r_xt[xi], r_rc], [r_pt[0], r_pt[1]])
                mb = misc.next()
                pT = ps[mb][:].bitcast(BF16)
                for k in range(8):
                    TR(pT[:, k * 128:(k + 1) * 128], xn[:, k * 128:(k + 1) * 128], ident_b[:], [r_pt[0], r_pt[1], r_const], [r_ps[mb]], sig=(k == 7))
                for k in range(8):
                    if t % 2 == 0:
                        TS("dve", h_fm[:, k, t * 128:(t + 1) * 128], pT[:, k * 128:(k + 1) * 128], gscol[:, k:k + 1], shiftcol[:, k:k + 1],
                           ALU.mult, ALU.add, [r_ps[mb], r_mod], [r_h[t]])
                    else:
                        ACT(h_fm[:, k, t * 128:(t + 1) * 128], pT[:, k * 128:(k + 1) * 128], AF.Identity, [r_ps[mb], r_mod], [r_h[t]],
                            scale=gscol[:, k:k + 1], bias=shiftcol[:, k:k + 1])
            misc.items = [6, 7]

        def proj_fm(S, c, wcol, rows=128, wrows=None):
            cw = min(512, S)
            mb = misc.next()
            tl = [r_h[t] for t in range(c * cw // 128, (c + 1) * cw // 128)]
            for k in range(8):
                MM(ps[mb][0:rows, 0:cw], wmix[:, k, wcol:wcol + rows], h_fm[:, k, c * cw:(c + 1) * cw], k == 0, k == 7,
                   [r_wmix] + tl, [r_ps[mb]], sig=(k == 7))
            return mb

        def load_tab(tab, c, cw, rows=slice(0, 128), trows=None):
            ti = tab_ring.next()
            nrow = rows.stop - rows.start
            LOAD(f"tab{ti}", tabc[ti][rows, 0:cw], tab[0, 0:nrow, c * cw:(c + 1) * cw], [], [r_tab[ti]])
            LOAD(f"tab{ti}", tabs[ti][rows, 0:cw], tab[1, 0:nrow, c * cw:(c + 1) * cw], [], [r_tab[ti]])
            return ti

        def rope_evac(mb, mbs, ti, rows, cw, out_ap, wr, gcol=None, gscol_=None, rstd=None, rstd_rd=()):
            a = tmpf_ring.next()
            b = tmpf_ring.next()
            if gcol is None:
                TT("dve", tmpf[a][rows, 0:cw], ps[mb][rows, 0:cw], tabc[ti][rows, 0:cw], ALU.mult, [r_ps[mb], r_tab[ti]], [r_tmpf[a]])
                TT("dve", tmpf[b][rows, 0:cw], ps[mbs][rows, 0:cw], tabs[ti][rows, 0:cw], ALU.mult, [r_ps[mbs], r_tab[ti]], [r_tmpf[b]])
            else:
                TS("dve", tmpf[a][rows, 0:cw], ps[mb][rows, 0:cw], gcol, None, ALU.mult, None, [r_ps[mb], r_pc], [r_tmpf[a]])
                TT("dve", tmpf[a][rows, 0:cw], tmpf[a][rows, 0:cw], tabc[ti][rows, 0:cw], ALU.mult, [r_tmpf[a], r_tab[ti]], [r_tmpf[a]])
                TS("dve", tmpf[b][rows, 0:cw], ps[mbs][rows, 0:cw], gscol_, None, ALU.mult, None, [r_ps[mbs], r_pc], [r_tmpf[b]])
                TT("dve", tmpf[b][rows, 0:cw], tmpf[b][rows, 0:cw], tabs[ti][rows, 0:cw], ALU.mult, [r_tmpf[b], r_tab[ti]], [r_tmpf[b]])
            if rstd is None:
                TT(ROPE_ENG, out_ap, tmpf[a][rows, 0:cw], tmpf[b][rows, 0:cw], ALU.add, [r_tmpf[a], r_tmpf[b]], wr)
            else:
                TT(ROPE_ENG, tmpf[a][rows, 0:cw], tmpf[a][rows, 0:cw], tmpf[b][rows, 0:cw], ALU.add, [r_tmpf[a], r_tmpf[b]], [r_tmpf[a]])
                TT(ROPE_ENG, out_ap, tmpf[a][rows, 0:cw], rstd, ALU.mult, [r_tmpf[a]] + list(rstd_rd), wr)

        def run_attention(S, groups):
            cw = min(512, S)
            NC = S // cw
            NKT = S // 128
            steps = []
            oset = [0]
            for g in groups:
                nu = len(g["units"])
                for c in range(NC):
                    if nu == 1:
                        bs = (O_BANKS[oset[0] % 2],)
                        oset[0] += 1
                    else:
                        bs = O_BANKS
                    for ui, u in enumerate(g["units"]):
                        for kt in range(0, NKT, 2):
                            kts = [kt] if kt + 1 >= NKT else [kt, kt + 1]
                            steps.append(dict(g=g, c=c, lanes=[(u, k_, bs[ui]) for k_ in kts],
                                              first=(ui == 0 and kt == 0 and c == 0), cfirst=(ui == 0 and kt == 0),
                                              last=(ui == nu - 1 and kts[-1] == NKT - 1), obs=bs))

            def emit_S(i):
                sp_ = steps[i]
                if sp_["first"] and sp_["g"].get("pre") is not None:
                    sp_["g"]["pre"]()
                if sp_["cfirst"] and sp_["g"].get("prec") is not None:
                    sp_["g"]["prec"](sp_["c"])
                for j, (u, kt, ob) in enumerate(sp_["lanes"]):
                    qap, qres = u["q"](sp_["c"])
                    bank = SB_BANKS[(2 * i + j) % 4]
                    kap, kres = u["k"](kt)
                    MM(ps[bank][:, 0:cw], kap, qap, True, True, list(kres) + list(qres), [r_ps[bank]])

            emit_S(0)
            for i, sp_ in enumerate(steps):
                if i + 1 < len(steps):
                    emit_S(i + 1)
                nl = len(sp_["lanes"])
                for j, (u, kt, ob) in enumerate(sp_["lanes"]):
                    bank = SB_BANKS[(2 * i + j) % 4]
                    pi = (2 * i + j) % 4
                    ACT(pts[pi][:, 0:cw], ps[bank][:, 0:cw], AF.Exp, [r_ps[bank]], [r_pt[pi]], scale=u["scale"])
                for j, (u, kt, ob) in enumerate(sp_["lanes"]):
                    pi = (2 * i + j) % 4
                    vap, vres = u["v"](kt)
                    MM(ps[ob][0:65, 0:cw], vap, pts[pi][:, 0:cw], kt == 0, kt == NKT - 1, [r_pt[pi]] + list(vres),
                       [r_ps[ob]], sig=(kt == NKT - 1 or j == nl - 1))
                if sp_["last"]:
                    sp_["g"]["epilogue"](sp_["c"], sp_["obs"])

        def o_copy(ob, cw):
            a = tmpo_ring.next()
            CP("dve", tmpo[a][0:65, 0:cw], ps[ob][0:65, 0:cw], [r_ps[ob]], [r_tmpo[a]])
            return a

        def o_transposed(a, cw):
            mb = misc.next()
            nqb = cw // 128
            for qb in range(nqb):
                TR(ps[mb][:, qb * 65:(qb + 1) * 65], tmpo[a][0:65, qb * 128:(qb + 1) * 128], ident_f[0:65, 0:65],
                   [r_tmpo[a], r_const], [r_ps[mb]], sig=(qb == nqb - 1))
            return mb, ps[mb][:, 0:nqb * 65].rearrange("p (q d) -> p q d", d=65)

        def gate_mm(S, c, gcol, width):
            cw = min(512, S)
            nqb = cw // 128
            gb = misc.next()
            for qb in range(nqb):
                t = c * nqb + qb
                for k in range(8):
                    MM(ps[gb][:, qb * width:(qb + 1) * width], h_fm[:, k, t * 128:(t + 1) * 128], wmix[:, k, gcol:gcol + width],
                       k == 0, k == 7, [r_h[t], r_wmix], [r_ps[gb]], sig=(qb == nqb - 1 and k == 7))
            return gb

        def gate_fin(gb, n):
            a = tmpg_ring.next()
            ACT(tmpg[a][:, 0:n], ps[gb][:, 0:n], AF.Exp, [r_ps[gb]], [r_tmpg[a]], scale=-1.0)
            TS("dve", tmpg[a][:, 0:n], tmpg[a][:, 0:n], 1.0, None, ALU.add, None, [r_tmpg[a]], [r_tmpg[a]])
            RCP(tmpg[a][:, 0:n], tmpg[a][:, 0:n], [r_tmpg[a]], [r_tmpg[a]])
            TT("dve", tmpg[a][:, 0:n], ps[gb][:, 0:n], tmpg[a][:, 0:n], ALU.mult, [r_ps[gb], r_tmpg[a]], [r_tmpg[a]])
            return a

        def store_o(s, c, cw, col, og_i, width=64):
            nqb = cw // 128
            dst = oscr[s][c * cw:(c + 1) * cw, col:col + width].rearrange("(q p) d -> p q d", p=128)
            src = ogs[og_i][:, 0:nqb * width].rearrange("p (q d) -> p q d", d=width)
            tl = [r_oscr[s][c * nqb + q] for q in range(nqb)]
            P.dma("pool", f"ost{og_i}", dst, src, reads=[r_og[og_i]], writes=tl)

        def std_epilogue(s, S, col0, gcol0, h):
            def ep(c, obs):
                cw = min(512, S)
                nqb = cw // 128
                gb = gate_mm(S, c, gcol0 + h * 64, 64)
                oc_ = o_copy(obs[0], cw)
                mb, ov = o_transposed(oc_, cw)
                RCP(small[:, 20:20 + nqb], ov[:, :, 64], [r_ps[mb]], [r_small])
                a = tmpf_ring.next()
                n = nqb * 64
                av = tmpf[a][:, 0:n].rearrange("p (q d) -> p q d", d=64)
                TT("dve", av, ov[:, :, 0:64], small[:, 20:20 + nqb].unsqueeze(2).broadcast_to([128, nqb, 64]), ALU.mult,
                   [r_ps[mb], r_small], [r_tmpf[a]])
                sg = gate_fin(gb, n)
                oi = og_ring.next()
                TT("pool", ogs[oi][:, 0:n], tmpf[a][:, 0:n], tmpg[sg][:, 0:n], ALU.mult, [r_tmpf[a], r_tmpg[sg]], [r_og[oi]])
                store_o(s, c, cw, col0 + h * 64, oi)
            return ep

        def mixer_D(l, s):
            S = seq_lens[s]
            cw = min(512, S)
            NC = S // cw
            P.wait_all("pool", [r_wmix])
            def q_extra(i, stg, p0, pn):
                sv = swapped(stg, 16)
                dvw = swapped(wmix[:, :, 256 + p0:256 + p0 + pn], 16)
                CP("pool", dvw[:, :, :, 0, :], sv[:, :, :, 1, :], [r_wst[i]], [r_wmix])
                CP("pool", dvw[:, :, :, 1, :], sv[:, :, :, 0, :], [r_wst[i]], [r_wmix])
            load_cast(l, OFF["qd"], 256, 0, extra=q_extra)
            i, stg = load_w(w_in[l][:, OFF["kd"]:OFF["kd"] + 128], 128, 0)
            for g in range(2):
                for r in range(2):
                    c0 = 512 + g * 128 + r * 64
                    CP("pool", wmix[:, :, c0:c0 + 64], stg[:, :, g * 64:(g + 1) * 64], [r_wst[i]], [r_wmix])
                    sv = swapped(stg[:, :, g * 64:(g + 1) * 64], 16)
                    dvw = swapped(wmix[:, :, c0 + 256:c0 + 256 + 64], 16)
                    CP("pool", dvw[:, :, :, 0, :], sv[:, :, :, 1, :], [r_wst[i]], [r_wmix])
                    CP("pool", dvw[:, :, :, 1, :], sv[:, :, :, 0, :], [r_wst[i]], [r_wmix])
            load_cast(l, OFF["vd"], 128, 1024)
            load_cast(l, OFF["gd"], 256, 1152)
            for c in range(NC):
                ti = load_tab(tab_gqa, c, cw)
                for fc in range(4):
                    isq = fc < 2
                    wc = fc * 128 if isq else 512 + (fc - 2) * 128
                    mb = proj_fm(S, c, wc)
                    mbs = proj_fm(S, c, wc + 256)
                    sq = tmpb_ring.next()
                    ACT(tmpb[sq][:, 0:cw], ps[mb][:, 0:cw], AF.Square, [r_ps[mb]], [r_tmpb[sq]])
                    sb_ = SB_BANKS[fc % 2]
                    MM(ps[sb_][:, 0:cw], bones_b[:], tmpb[sq][:, 0:cw], True, True, [r_const, r_tmpb[sq]], [r_ps[sb_]])
                    rs = tmpf_ring.next()
                    rstd_act(tmpf[rs][:, 0:cw], ps[sb_][:, 0:cw], 64.0, [r_ps[sb_]], [r_tmpf[rs]])
                    if isq:
                        out_ap, wr = qbuf[:, fc, c * cw:(c + 1) * cw], [r_q[fc][c]]
                        g1, g2 = pc[:, PC_GQ:PC_GQ + 1], pc[:, PC_GQS:PC_GQS + 1]
                    else:
                        out_ap, wr = kbuf[:, fc - 2, c * cw:(c + 1) * cw], [r_k[fc - 2][c]]
                        g1, g2 = pc[:, PC_GK:PC_GK + 1], pc[:, PC_GKS:PC_GKS + 1]
                    rope_evac(mb, mbs, ti, slice(0, 128), cw, out_ap, wr, gcol=g1, gscol_=g2,
                              rstd=tmpf[rs][:, 0:cw], rstd_rd=[r_tmpf[rs]])
                for qb in range(cw // 128):
                    t = c * (cw // 128) + qb
                    mb = misc.next()
                    for k in range(8):
                        MM(ps[mb][:, 0:128], h_fm[:, k, t * 128:(t + 1) * 128], wmix[:, k, 1024:1152], k == 0, k == 7,
                           [r_h[t], r_wmix], [r_ps[mb]], sig=(k == 7))
                    CP("dve", vaug[:, t, 0:130].rearrange("p (h d) -> p h d", d=65)[:, :, 0:64],
                       ps[mb][:, 0:128].rearrange("p (h d) -> p h d", d=64), [r_ps[mb]], [r_v[t]])
            groups = []
            for fc in range(2):
                units = []
                for hh in range(2):
                    def qf(c, hh=hh):
                        return qz[:, hh, 0:cw], [r_qz[hh]]

                    def kf(kt, fc=fc):
                        return kbuf[:, fc, kt * 128:(kt + 1) * 128], [r_k[fc][kt * 128 // cw]]

                    def vf(kt, fc=fc):
                        return vaug[:, kt, fc * 65:(fc + 1) * 65], [r_v[kt]]
                    units.append(dict(q=qf, k=kf, v=vf, scale=64 ** -0.5))

                def prec(c, fc=fc):
                    CP("pool", qz[0:64, 0, 0:cw], qbuf[0:64, fc, c * cw:(c + 1) * cw], [r_q[fc][c]], [r_qz[0]])
                    CP("dve", qz[64:128, 1, 0:cw], qbuf[64:128, fc, c * cw:(c + 1) * cw], [r_q[fc][c]], [r_qz[1]])
                e0 = std_epilogue(s, S, 512, 1152, 2 * fc)
                e1 = std_epilogue(s, S, 512, 1152, 2 * fc + 1)

                def ep2(c, obs, e0=e0, e1=e1):
                    e0(c, (obs[0],))
                    e1(c, (obs[1],))
                groups.append(dict(units=units, epilogue=ep2, prec=prec))
            MEMSET("pool", qz[64:128, 0, :], 0.0, [r_qz[0]])
            MEMSET("pool", qz[0:64, 1, :], 0.0, [r_qz[1]])
            run_attention(S, groups)

        def mixer_B(l, s):
            S = seq_lens[s]
            cw = min(512, S)
            NC = S // cw
            P.wait_all("pool", [r_wmix])
            for nm, c0 in (("qb", 0), ("kb", 512)):
                def sw_extra(i, stg, p0, pn, c0=c0):
                    d0 = c0 + 256 + p0
                    CP("pool", wmix[:, :, d0:d0 + pn], stg, [r_wst[i]], [r_wmix])
                    sv = stg.rearrange("p k (m j) -> p k m j", j=32)
                    dvw = wmix[:, :, d0:d0 + pn].rearrange("p k (m j) -> p k m j", j=32)
                    CP("pool", dvw[:, :, :, 0:4], sv[:, :, :, 4:8], [r_wst[i]], [r_wmix])
                    CP("pool", dvw[:, :, :, 4:8], sv[:, :, :, 0:4], [r_wst[i]], [r_wmix])
                load_cast(l, OFF[nm], 256, c0, extra=sw_extra)
            load_cast(l, OFF["vb"], 256, 1024)
            load_cast(l, OFF["gb"], 256, 1280)
            for c in range(NC):
                ti = load_tab(tab_diff, c, cw)
                for fc in range(4):
                    isq = fc < 2
                    wc = fc * 128 if isq else 512 + (fc - 2) * 128
                    mb = proj_fm(S, c, wc)
                    mbs = proj_fm(S, c, wc + 256)
                    if isq:
                        out_ap, wr = qbuf[:, fc, c * cw:(c + 1) * cw], [r_q[fc][c]]
                    else:
                        out_ap, wr = kbuf[:, fc - 2, c * cw:(c + 1) * cw], [r_k[fc - 2][c]]
                    rope_evac(mb, mbs, ti, slice(0, 128), cw, out_ap, wr)
                for qb in range(cw // 128):
                    t = c * (cw // 128) + qb
                    mb = misc.next()
                    for k in range(8):
                        MM(ps[mb][:, 0:256], h_fm[:, k, t * 128:(t + 1) * 128], wmix[:, k, 1024:1280], k == 0, k == 7,
                           [r_h[t], r_wmix], [r_ps[mb]], sig=(k == 7))
                    CP("dve", vaug[:, t, 0:260].rearrange("p (h d) -> p h d", d=65)[:, :, 0:64],
                       ps[mb][:, 0:256].rearrange("p (h d) -> p h d", d=64), [r_ps[mb]], [r_v[t]])
            groups = []
            for h in range(4):
                fc = h // 2
                units = []
                for j in range(2):
                    b_ = (h % 2) * 2 + j

                    def qf(c, b_=b_):
                        return qz[:, b_, 0:cw], [r_qz[b_]]

                    def kf(kt, fc=fc):
                        return kbuf[:, fc, kt * 128:(kt + 1) * 128], [r_k[fc][kt * 128 // cw]]

                    def vf(kt, h=h):
                        return vaug[:, kt, h * 65:(h + 1) * 65], [r_v[kt]]
                    units.append(dict(q=qf, k=kf, v=vf, scale=32 ** -0.5))

                def prec(c, h=h, fc=fc):
                    for j in range(2):
                        b_ = (h % 2) * 2 + j
                        rs_ = slice(b_ * 32, (b_ + 1) * 32)
                        CP("pool" if j == 0 else "dve", qz[rs_, b_, 0:cw], qbuf[rs_, fc, c * cw:(c + 1) * cw], [r_q[fc][c]], [r_qz[b_]])

                def ep(c, obs, h=h):
                    nqb = cw // 128
                    n = nqb * 64
                    gb = gate_mm(S, c, 1280 + h * 64, 64)
                    oc1 = o_copy(obs[0], cw)
                    oc2 = o_copy(obs[1], cw)
                    ge = tmpg_ring.next()
                    gc_ = tmpf_ring.next()
                    ACT(tmpg[ge][:, 0:n], ps[gb][:, 0:n], AF.Exp, [r_ps[gb]], [r_tmpg[ge]], scale=-1.0)
                    ACT(tmpf[gc_][:, 0:n], ps[gb][:, 0:n], AF.Copy, [r_ps[gb]], [r_tmpf[gc_]])
                    mb1, ov1 = o_transposed(oc1, cw)
                    mb2, ov2 = o_transposed(oc2, cw)
                    RCP(small[:, 24:24 + nqb], ov1[:, :, 64], [r_ps[mb1]], [r_small])
                    RCP(small[:, 28:28 + nqb], ov2[:, :, 64], [r_ps[mb2]], [r_small])
                    TS("dve", small[:, 28:28 + nqb], small[:, 28:28 + nqb], lamc[:, 3:4], None, ALU.mult, None, [r_small, r_wa], [r_small])
                    a = tmpf_ring.next()
                    b = tmpf_ring.next()
                    av = tmpf[a][:, 0:n].rearrange("p (q d) -> p q d", d=64)
                    bv = tmpf[b][:, 0:n].rearrange("p (q d) -> p q d", d=64)
                    TT("dve", av, ov1[:, :, 0:64], small[:, 24:24 + nqb].unsqueeze(2).broadcast_to([128, nqb, 64]), ALU.mult,
                       [r_ps[mb1], r_small], [r_tmpf[a]])
                    TT("dve", bv, ov2[:, :, 0:64], small[:, 28:28 + nqb].unsqueeze(2).broadcast_to([128, nqb, 64]), ALU.mult,
                       [r_ps[mb2], r_small], [r_tmpf[b]])
                    TT("pool", tmpf[a][:, 0:n], tmpf[a][:, 0:n], tmpf[b][:, 0:n], ALU.add, [r_tmpf[a], r_tmpf[b]], [r_tmpf[a]])
                    TT("pool", tmpf[b][:, 0:n], tmpf[a][:, 0:n], tmpf[a][:, 0:n], ALU.mult, [r_tmpf[a]], [r_tmpf[b]])
                    P.op("dve", lambda e: e.tensor_reduce(out=small[:, 32:32 + nqb], in_=bv, axis=AX.X, op=ALU.add),
                         reads=[r_tmpf[b]], writes=[r_small])
                    rstd_act(small[:, 32:32 + nqb], small[:, 32:32 + nqb], 64.0, [r_small], [r_small])
                    TT("dve", av, av, small[:, 32:32 + nqb].unsqueeze(2).broadcast_to([128, nqb, 64]), ALU.mult,
                       [r_tmpf[a], r_small], [r_tmpf[a]])
                    TT("pool", av, av, subln_bc[:, 0:nqb, :], ALU.mult, [r_tmpf[a], r_wa], [r_tmpf[a]])
                    TS("dve", tmpg[ge][:, 0:n], tmpg[ge][:, 0:n], 1.0, None, ALU.add, None, [r_tmpg[ge]], [r_tmpg[ge]])
                    RCP(tmpg[ge][:, 0:n], tmpg[ge][:, 0:n], [r_tmpg[ge]], [r_tmpg[ge]])
                    TT("dve", tmpg[ge][:, 0:n], tmpf[gc_][:, 0:n], tmpg[ge][:, 0:n], ALU.mult, [r_tmpf[gc_], r_tmpg[ge]], [r_tmpg[ge]])
                    sg = ge
                    oi = og_ring.next()
                    TT("pool", ogs[oi][:, 0:n], tmpf[a][:, 0:n], tmpg[sg][:, 0:n], ALU.mult, [r_tmpf[a], r_tmpg[sg]], [r_og[oi]])
                    store_o(s, c, cw, 256 + h * 64, oi)
                groups.append(dict(units=units, epilogue=ep, prec=prec))
            for b_ in range(4):
                MEMSET("pool", qz[:, b_, :], 0.0, [r_qz[b_]])
            run_attention(S, groups)

        def mixer_A(l, s):
            S = seq_lens[s]
            cw = min(512, S)
            NC = S // cw
            nslots = 2 if 2 * S <= Smax else 1
            P.wait_all("pool", [r_wmix])

            def kr_extra(i, stg, p0, pn):
                if p0 == 256:
                    CP("pool", wmix[:, :, 352:368], stg[:, :, 80:96], [r_wst[i]], [r_wmix])
                    CP("pool", wmix[:, :, 368:384], stg[:, :, 64:80], [r_wst[i]], [r_wmix])
            load_cast(l, 0, 352, 0, extra=kr_extra)
            load_cast(l, OFF["ga"], 256, 384)
            R64 = slice(64, 96)
            qn0 = qbuf[:, 1, :]
            qn1 = kbuf[:, 1, :]

            def qs_(slot, c):
                return qbuf[:, 0, slot * S + c * cw:slot * S + (c + 1) * cw], r_q[0][slot * NC + c]

            def ks_(slot, c):
                return kbuf[:, 0, slot * S + c * cw:slot * S + (c + 1) * cw], r_k[0][slot * NC + c]

            for c in range(NC):
                tl = [r_h[t] for t in range(c * cw // 128, (c + 1) * cw // 128)]
                csl = slice(c * cw, (c + 1) * cw)
                mbq = []
                sqs = []
                for fc, rows in ((0, 128), (1, 64)):
                    mb = proj_fm(S, c, fc * 128, rows=rows)
                    sq = tmpb_ring.next()
                    ACT(tmpb[sq][0:rows, 0:cw], ps[mb][0:rows, 0:cw], AF.Square, [r_ps[mb]], [r_tmpb[sq]])
                    mbq.append(mb)
                    sqs.append(sq)
                sb_ = SB_BANKS[0]
                MM(ps[sb_][:, 0:cw], ones_b[:, :], tmpb[sqs[0]][:, 0:cw], True, False, [r_const, r_tmpb[sqs[0]]], [r_ps[sb_]], sig=False)
                MM(ps[sb_][:, 0:cw], ones_b[0:64, :], tmpb[sqs[1]][0:64, 0:cw], False, True, [r_const, r_tmpb[sqs[1]]], [r_ps[sb_]])
                rs = tmpf_ring.next()
                rstd_act(tmpf[rs][:, 0:cw], ps[sb_][:, 0:cw], 192.0, [r_ps[sb_]], [r_tmpf[rs]])
                TT("dve", qn0[:, csl], ps[mbq[0]][:, 0:cw], tmpf[rs][:, 0:cw], ALU.mult,
                   [r_ps[mbq[0]], r_tmpf[rs]], [r_q[1][c]])
                TT("dve", qn1[0:64, csl], ps[mbq[1]][0:64, 0:cw], tmpf[rs][0:64, 0:cw], ALU.mult,
                   [r_ps[mbq[1]], r_tmpf[rs]], [r_k[1][c]])
                mb = proj_fm(S, c, 192)
                sq = tmpb_ring.next()
                ACT(tmpb[sq][:, 0:cw], ps[mb][:, 0:cw], AF.Square, [r_ps[mb]], [r_tmpb[sq]])
                sb_ = SB_BANKS[1]
                MM(ps[sb_][:, 0:cw], ones_b[:, :], tmpb[sq][:, 0:cw], True, True, [r_const, r_tmpb[sq]], [r_ps[sb_]])
                rs = tmpf_ring.next()
                rstd_act(tmpf[rs][:, 0:cw], ps[sb_][:, 0:cw], 128.0, [r_ps[sb_]], [r_tmpf[rs]])
                TT("dve", lat[:, csl], ps[mb][:, 0:cw], tmpf[rs][:, 0:cw], ALU.mult,
                   [r_ps[mb], r_tmpf[rs]], [r_lat[c]])
                for qb in range(cw // 128):
                    t = c * (cw // 128) + qb
                    vb_ = misc.next()
                    MM(ps[vb_][:, 0:256], lat[:, t * 128:(t + 1) * 128], wukv[:, 256:512], True, True, [r_lat[c], r_wa], [r_ps[vb_]])
                    CP("dve", vaug[:, t, 0:260].rearrange("p (h d) -> p h d", d=65)[:, :, 0:64],
                       ps[vb_][:, 0:256].rearrange("p (h d) -> p h d", d=64), [r_ps[vb_]], [r_v[t]])
                ti = load_tab(tab_mla, c, cw, rows=R64)
                mbk = misc.next()
                mbks = misc.next()
                for k in range(8):
                    MM(ps[mbk][64:96, 0:cw], wmix[:, k, 320:352], h_fm[:, k, c * cw:(c + 1) * cw], k == 0, k == 7,
                       [r_wmix] + tl, [r_ps[mbk]], sig=(k == 7))
                for k in range(8):
                    MM(ps[mbks][64:96, 0:cw], wmix[:, k, 352:384], h_fm[:, k, c * cw:(c + 1) * cw], k == 0, k == 7,
                       [r_wmix] + tl, [r_ps[mbks]], sig=(k == 7))
                k0, rk0 = ks_(0, c)
                rope_evac(mbk, mbks, ti, R64, cw, k0[64:96, :], [rk0])
                if nslots == 2:
                    k1, rk1 = ks_(1, c)
                    CP("pool", k1[64:96, :], k0[64:96, :], [rk0], [rk1])

            def head_proj(h):
                slot = h % nslots
                for c in range(NC):
                    csl = slice(c * cw, (c + 1) * cw)
                    qa_, rq_ = qs_(slot, c)
                    ka_, rk_ = ks_(slot, c)
                    mb = misc.next()
                    MM(ps[mb][0:96, 0:cw], wuq[:, 0, h * 96:(h + 1) * 96], qn0[:, csl], True, False,
                       [r_wa, r_q[1][c]], [r_ps[mb]], sig=False)
                    MM(ps[mb][0:96, 0:cw], wuq[0:64, 1, h * 96:(h + 1) * 96], qn1[0:64, csl], False, True,
                       [r_wa, r_k[1][c]], [r_ps[mb]])
                    mbs = misc.next()
                    MM(ps[mbs][0:96, 0:cw], wuqs[:, 0, h * 96:(h + 1) * 96], qn0[:, csl], True, False,
                       [r_wa, r_q[1][c]], [r_ps[mbs]], sig=False)
                    MM(ps[mbs][0:96, 0:cw], wuqs[0:64, 1, h * 96:(h + 1) * 96], qn1[0:64, csl], False, True,
                       [r_wa, r_k[1][c]], [r_ps[mbs]])
                    CP("dve", qa_[0:64, :], ps[mb][0:64, 0:cw], [r_ps[mb]], [rq_])
                    ti = load_tab(tab_mla, c, cw, rows=R64)
                    rope_evac(mb, mbs, ti, R64, cw, qa_[64:96, :], [rq_])
                    mbk = misc.next()
                    MM(ps[mbk][0:64, 0:cw], wukv[:, h * 64:(h + 1) * 64], lat[:, csl], True, True,
                       [r_wa, r_lat[c]], [r_ps[mbk]])
                    CP("dve", ka_[0:64, :], ps[mbk][0:64, 0:cw], [r_ps[mbk]], [rk_])

            groups = []
            for h in range(4):
                slot = h % nslots

                def qf(c, slot=slot):
                    a_, r_ = qs_(slot, c)
                    return a_[0:96, :], [r_]

                def kf(kt, slot=slot):
                    a_, r_ = ks_(slot, kt * 128 // cw)
                    o_ = (kt * 128) % cw
                    return a_[0:96, o_:o_ + 128], [r_]

                def vf(kt, h=h):
                    return vaug[:, kt, h * 65:(h + 1) * 65], [r_v[kt]]
                u = dict(q=qf, k=kf, v=vf, scale=96 ** -0.5)
                pre = None
                if nslots == 2:
                    if h + 1 < 4:
                        pre = (lambda hh=h + 1: head_proj(hh))
                elif h >= 1:
                    pre = (lambda hh=h: head_proj(hh))
                groups.append(dict(units=[u], epilogue=std_epilogue(s, S, 0, 384, h), pre=pre))
            head_proj(0)
            run_attention(S, groups)

        def mixer_C(l, s):
            S = seq_lens[s]
            cw = min(512, S)
            NC = S // cw
            P.wait_all("pool", [r_wmix])
            load_cast(l, OFF["xc"], 256, 0)
            load_cast(l, OFF["gc"], 256, 256)
            xcv = vaug[:].rearrange("p t d -> p (t d)").bitcast(F32)
            xconv = kbuf[:].rearrange("p a s -> p (a s)").bitcast(F32)
            hsum = qbuf[:].rearrange("p a s -> p (a s)").bitcast(F32)
            xcb = lat[:, :]
            allr = [r for rr in (r_lat, r_v) for r in rr] + [r for sl in (r_q, r_k) for rr in sl for r in rr]
            r_xc, r_xconv, r_hsum, r_xcb = Res("xc"), Res("xconv"), Res("hsum"), Res("xcb")
            for e_ in ("dve", "pool", "act", "pe"):
                P.wait_all(e_, allr)
            for cc in range(2):
                MEMSET("dve", xcv[:, 0:2], 0.0, [r_xc])
                MEMSET("dve", xcv[:, S + 2:S + 4], 0.0, [r_xc])
                for c in range(NC):
                    mb = proj_fm(S, c, cc * 128)
                    if c % 2 == 0:
                        CP("dve", xcv[:, 2 + c * cw:2 + (c + 1) * cw], ps[mb][:, 0:cw], [r_ps[mb]], [r_xc])
                    else:
                        ACT(xcv[:, 2 + c * cw:2 + (c + 1) * cw], ps[mb][:, 0:cw], AF.Copy, [r_ps[mb]], [r_xc])
                cwc = lambda j: pc[:, PC_CW + cc * 4 + j:PC_CW + cc * 4 + j + 1]
                for c in range(NC):
                    sl = slice(c * cw, (c + 1) * cw)
                    TS("dve", xconv[:, sl], xcv[:, c * cw:(c + 1) * cw], cwc(0), pc[:, PC_CB + cc:PC_CB + cc + 1], ALU.mult, ALU.add,
                       [r_xc, r_pc], [r_xconv])
                    for j in range(1, 4):
                        STT(xconv[:, sl], xcv[:, j + c * cw:j + (c + 1) * cw], cwc(j), xconv[:, sl], ALU.mult, ALU.add,
                            [r_xc, r_pc, r_xconv], [r_xconv])
                    ACT(xcb[:, sl], xconv[:, sl], AF.Copy, [r_xconv], [r_xcb])
                for d_ in range(2):
                    order = range(NC) if d_ == 0 else range(NC - 1, -1, -1)
                    prev = None
                    for c in order:
                        sl = slice(c * cw, (c + 1) * cw)
                        mba = misc.next()
                        MM(ps[mba][:, 0:cw], lruw[:, 0, d_, cc, :], xcb[:, sl], True, True, [r_wa, r_xcb], [r_ps[mba]])
                        mbx = misc.next()
                        MM(ps[mbx][:, 0:cw], lruw[:, 1, d_, cc, :], xcb[:, sl], True, True, [r_wa, r_xcb], [r_ps[mbx]])
                        ra, ri = tmpf_ring.next(), tmpf_ring.next()
                        ci = d_ * 2 + cc
                        ACT(tmpf[ra][:, 0:cw], ps[mba][:, 0:cw], AF.Exp, [r_ps[mba], r_wa], [r_tmpf[ra]], scale=-1.0, bias=lsp[:, 4 + ci:5 + ci])
                        ACT(tmpf[ri][:, 0:cw], ps[mbx][:, 0:cw], AF.Exp, [r_ps[mbx], r_wa], [r_tmpf[ri]], scale=-1.0, bias=lsp[:, 8 + ci:9 + ci])
                        ACT(tmpf[ra][:, 0:cw], tmpf[ra][:, 0:cw], AF.Ln, [r_tmpf[ra]], [r_tmpf[ra]], bias=1.0)
                        ACT(tmpf[ri][:, 0:cw], tmpf[ri][:, 0:cw], AF.Ln, [r_tmpf[ri]], [r_tmpf[ri]], bias=1.0)
                        ACT(tmpf[ra][:, 0:cw], tmpf[ra][:, 0:cw], AF.Exp, [r_tmpf[ra]], [r_tmpf[ra]], scale=-1.0)
                        ACT(tmpf[ri][:, 0:cw], tmpf[ri][:, 0:cw], AF.Exp, [r_tmpf[ri]], [r_tmpf[ri]], scale=-1.0)
                        ACT(tmpf[ra][:, 0:cw], tmpf[ra][:, 0:cw], AF.Exp, [r_tmpf[ra], r_wa], [r_tmpf[ra]], scale=lsp[:, ci:ci + 1])
                        rg = tmpf_ring.next()
                        TT("dve", tmpf[rg][:, 0:cw], tmpf[ra][:, 0:cw], tmpf[ra][:, 0:cw], ALU.mult, [r_tmpf[ra]], [r_tmpf[rg]])
                        TS("dve", tmpf[rg][:, 0:cw], tmpf[rg][:, 0:cw], -1.0, 1.0, ALU.mult, ALU.add, [r_tmpf[rg]], [r_tmpf[rg]])
                        ACT(tmpf[rg][:, 0:cw], tmpf[rg][:, 0:cw], AF.Ln, [r_tmpf[rg]], [r_tmpf[rg]])
                        ACT(tmpf[rg][:, 0:cw], tmpf[rg][:, 0:cw], AF.Exp, [r_tmpf[rg]], [r_tmpf[rg]], scale=0.5)
                        TT("pool", tmpf[ri][:, 0:cw], tmpf[ri][:, 0:cw], xconv[:, sl], ALU.mult, [r_tmpf[ri], r_xconv], [r_tmpf[ri]])
                        TT("dve", tmpf[rg][:, 0:cw], tmpf[rg][:, 0:cw], tmpf[ri][:, 0:cw], ALU.mult, [r_tmpf[rg], r_tmpf[ri]], [r_tmpf[rg]])
                        init = 0.0 if prev is None else prev
                        if d_ == 0:
                            P.op("dve", lambda e, ra=ra, rg=rg, init=init, ri=ri: e.tensor_tensor_scan(
                                out=tmpf[ri][:, 0:cw], data0=tmpf[ra][:, 0:cw], data1=tmpf[rg][:, 0:cw], initial=init,
                                op0=ALU.mult, op1=ALU.add), reads=[r_tmpf[ra], r_tmpf[rg], r_small], writes=[r_tmpf[ri]])
                            CP("dve", small[:, 40:41], tmpf[ri][:, cw - 1:cw], [r_tmpf[ri]], [r_small])
                            prev = small[:, 40:41]
                            CP("pool", hsum[:, sl], tmpf[ri][:, 0:cw], [r_tmpf[ri]], [r_hsum])
                        else:
                            P.op("dve", lambda e, ra=ra, rg=rg, init=init, ri=ri: e.tensor_tensor_scan(
                                out=tmpf[ri][:, 0:cw][:, ::-1], data0=tmpf[ra][:, 0:cw][:, ::-1], data1=tmpf[rg][:, 0:cw][:, ::-1],
                                initial=init, op0=ALU.mult, op1=ALU.add), reads=[r_tmpf[ra], r_tmpf[rg], r_small], writes=[r_tmpf[ri]])
                            CP("dve", small[:, 41:42], tmpf[ri][:, 0:1], [r_tmpf[ri]], [r_small])
                            prev = small[:, 41:42]
                            TT("pool", hsum[:, sl], hsum[:, sl], tmpf[ri][:, 0:cw], ALU.add, [r_hsum, r_tmpf[ri]], [r_hsum])
                for c in range(NC):
                    sl = slice(c * cw, (c + 1) * cw)
                    gb = proj_fm(S, c, 256 + cc * 128)
                    a = tmpf_ring.next()
                    ACT(tmpf[a][:, 0:cw], ps[gb][:, 0:cw], AF.Exp, [r_ps[gb]], [r_tmpf[a]], scale=-1.0)
                    ACT(tmpf[a][:, 0:cw], tmpf[a][:, 0:cw], AF.Ln, [r_tmpf[a]], [r_tmpf[a]], bias=1.0)
                    ACT(tmpf[a][:, 0:cw], tmpf[a][:, 0:cw], AF.Exp, [r_tmpf[a]], [r_tmpf[a]], scale=-1.0)
                    TT("dve", tmpf[a][:, 0:cw], ps[gb][:, 0:cw], tmpf[a][:, 0:cw], ALU.mult, [r_ps[gb], r_tmpf[a]], [r_tmpf[a]])
                    b = tmpb_ring.next()
                    TT("dve", tmpb[b][:, 0:cw], tmpf[a][:, 0:cw], hsum[:, sl], ALU.mult, [r_tmpf[a], r_hsum], [r_tmpb[b]])
                    P.dma("pool", f"oc{b}", ocscr[s][cc * 128:(cc + 1) * 128, sl], tmpb[b][:, 0:cw], reads=[r_tmpb[b]], writes=[r_ocscr[s]])
            MEMSET("pool", vaug[:], 1.0, [r_xcb, r_xc, r_xconv, r_hsum])
            for e_ in ("dve", "pool", "act", "pe"):
                P.wait_all(e_, [r_xcb, r_xc, r_xconv, r_hsum])

        def phase3(l, s, last):
            S = seq_lens[s]
            P.wait_all("pool", [r_wmix])
            for k in range(8):
                i = wst_ring.next()
                LOAD(f"wst{i}", wst[i][:, 0:1024], w_out[l][k * 128:(k + 1) * 128, :], [], [r_wst[i]])
                TT("dve", wmix[:, k, 0:1024], wst[i][:, 0:1024], gate_bc[:, :], ALU.mult, [r_wst[i], r_gate], [r_wmix])
            src = x_in[s] if l == 0 else xres[s]
            misc.items = [6, 7, 0, 1, 2, 3]
            if last:
                LOAD("c3", fn_bc, fnorm.partition_broadcast(128), [], [r_mod])
            for t in range(S // 128):
                jb = t % 2
                ot = tmpf[jb][:].bitcast(BF16)[:, 0:768]
                ofm = tmpf[2 + jb][:].bitcast(BF16).rearrange("p (k n) -> p k n", n=128)
                r_ot, r_ofm = r_tmpf[jb], r_tmpf[2 + jb]
                P.dma("sp", f"otl{jb}", ot, oscr[s][t * 128:(t + 1) * 128, :], reads=[r_oscr[s][t]], writes=[r_ot])
                P.dma("sp", f"ofl{jb}", ofm[:, 4:6, :], ocscr[s][:, t * 128:(t + 1) * 128].rearrange("(c p) n -> p c n", p=128),
                      reads=[r_ocscr[s]], writes=[r_ofm])
                mb = misc.next()
                pT = ps[mb][:].bitcast(BF16)
                for j in range(6):
                    TR(pT[:, j * 128:(j + 1) * 128], ot[:, j * 128:(j + 1) * 128], ident_b[:], [r_ot, r_const], [r_ps[mb]], sig=(j == 5))
                CP("dve", ofm[:, 0:4, :], pT[:, 0:512].rearrange("p (j n) -> p j n", n=128), [r_ps[mb]], [r_ofm])
                CP("dve", ofm[:, 6:8, :], pT[:, 512:768].rearrange("p (j n) -> p j n", n=128), [r_ps[mb]], [r_ofm])
                xi = xt_ring.next()
                rd = [] if l == 0 else [r_xres[s][t]]
                LOAD(f"xt{xi}", xt[xi][:], src[t * 128:(t + 1) * 128, :], rd, [r_xt[xi]])
                for n in range(2):
                    yb = misc.next()
                    for k in range(8):
                        MM(ps[yb][:, :], ofm[:, k, :], wmix[:, k, n * 512:(n + 1) * 512], k == 0, k == 7, [r_ofm, r_wmix], [r_ps[yb]], sig=(k == 7))
                    TT("dve", xt[xi][:, n * 512:(n + 1) * 512], xt[xi][:, n * 512:(n + 1) * 512], ps[yb][:, :], ALU.add,
                       [r_ps[yb], r_xt[xi]], [r_xt[xi]])
                if not last:
                    P.dma("pool", f"xst{xi}", xres[s][t * 128:(t + 1) * 128, :], xt[xi][:], reads=[r_xt[xi]], writes=[r_xres[s][t]])
                else:
                    ACT(xn[:], xt[xi][:], AF.Square, [r_xt[xi]], [r_pt[0], r_pt[1], r_small], accum=small[:, 17:18])
                    rstd_act(small[:, 17:18], small[:, 17:18], float(D), [r_small], [r_small])
                    STT(xt[xi][:], xt[xi][:], small[:, 17:18], fn_bc, ALU.mult, ALU.mult, [r_xt[xi], r_small, r_mod], [r_xt[xi]])
                    P.dma("pool", f"xst{xi}", y_out[s][t * 128:(t + 1) * 128, :], xt[xi][:], reads=[r_xt[xi]], writes=[r_y])
            misc.items = [6, 7]

        try:
          for l in range(depth):
            layer_prep(l)
            for s in range(nseq):
                phase0(l, s)
                phase1(l, s)
                if _os.environ.get("KSTOP") == "p1":
                    raise StopIteration
                if "A" in mixers:
                    mixer_A(l, s)
                if "B" in mixers:
                    mixer_B(l, s)
                if "C" in mixers:
                    mixer_C(l, s)
                if "D" in mixers:
                    mixer_D(l, s)
                phase3(l, s, l == depth - 1)
        except StopIteration:
            pass
        allres = [r_y] + [r for rr in r_xres for r in rr] + [r for rr in r_oscr for r in rr] + r_ocscr
        P.wait_all("sp", allres)
        P.emit()
    return nc, P


def _rope_tables(Smax):
    pos = np.arange(Smax, dtype=np.float32)

    def cs(p, theta, half):
        inv = np.power(np.float32(theta), -np.arange(half, dtype=np.float32) / np.float32(half)).astype(np.float32)
        ang = (p[:, None] * inv[None, :]).astype(np.float32)
        return np.cos(ang).astype(np.float32).T, np.sin(ang).astype(np.float32).T

    c, s_ = cs(pos, 10000.0, 16)
    tab_mla = np.zeros((2, 32, Smax), np.float32)
    tab_mla[0, 0:16] = c
    tab_mla[0, 16:32] = c
    tab_mla[1, 0:16] = -s_
    tab_mla[1, 16:32] = s_
    c, s_ = cs(pos, 500000.0, 4)
    blk_c = np.ones((32, Smax), np.float32)
    blk_s = np.zeros((32, Smax), np.float32)
    blk_c[0:4] = c
    blk_c[4:8] = c
    blk_s[0:4] = -s_
    blk_s[4:8] = s_
    tab_diff = np.stack([np.tile(blk_c, (4, 1)), np.tile(blk_s, (4, 1))])
    row = np.floor(pos / 64.0).astype(np.float32)
    col = (pos - row * 64.0).astype(np.float32)
    cr, sr = cs(row, 10000.0, 16)
    cc, sc_ = cs(col, 10000.0, 16)
    blk_c = np.concatenate([cr, cr, cc, cc], 0)
    blk_s = np.concatenate([-sr, sr, -sc_, sc_], 0)
    tab_gqa = np.stack([np.tile(blk_c, (2, 1)), np.tile(blk_s, (2, 1))])
    return tab_mla, np.ascontiguousarray(tab_diff), np.ascontiguousarray(tab_gqa)


def _host_layout(inp, depth):
    f = np.float32
    pcol = np.zeros((depth, 128, NPC), f)
    prow = np.zeros((depth, NPR), f)
    p = np.arange(128)
    partner = np.where((p % 32) < 16, p + 16, p - 16) % 64
    for l in range(depth):
        pcol[l, :, PC_NG:PC_NG + 8] = inp["norm_g"][l].reshape(8, 128).T
        pcol[l, :, PC_ADAB:PC_ADAB + 24] = inp["ada_b"][l].reshape(24, 128).T
        pcol[l, :, PC_QN] = inp["mla_q_norm"][l][0:128]
        pcol[l, 0:64, PC_QN + 1] = inp["mla_q_norm"][l][128:192]
        pcol[l, :, PC_KVN] = inp["mla_kv_norm"][l]
        pcol[l, :, PC_GQ] = inp["gqa_q_norm"][l][p % 64]
        pcol[l, :, PC_GK] = inp["gqa_k_norm"][l][p % 64]
        pcol[l, :, PC_GQS] = inp["gqa_q_norm"][l][partner]
        pcol[l, :, PC_GKS] = inp["gqa_k_norm"][l][partner]
        pcol[l, :, PC_CB:PC_CB + 2] = inp["lru_conv_b"][l].reshape(2, 128).T
        pcol[l, :, PC_CW:PC_CW + 8] = inp["lru_conv_w"][l].reshape(4, 2, 128).transpose(2, 1, 0).reshape(128, 8)
        pcol[l, :, PC_BA:PC_BA + 4] = inp["lru_ba"][l].reshape(2, 2, 128).transpose(2, 0, 1).reshape(128, 4)
        pcol[l, :, PC_BX:PC_BX + 4] = inp["lru_bx"][l].reshape(2, 2, 128).transpose(2, 0, 1).reshape(128, 4)
        pcol[l, :, PC_LAM:PC_LAM + 4] = inp["lru_lambda"][l].reshape(2, 2, 128).transpose(2, 0, 1).reshape(128, 4)
        prow[l, PR_LAM:PR_LAM + 128] = inp["diff_lambda"][l].reshape(-1)
        prow[l, PR_SUB:PR_SUB + 64] = inp["diff_subln"][l]
    lru_w = np.ascontiguousarray(np.stack([inp["lru_wa"], inp["lru_wx"]], axis=1)).astype(f)
    return pcol, prow, lru_w


_CACHE = {}


def run(inputs, seqs_per_core, n_cores, mixers="ABCD"):
    depth = inputs["w_in"].shape[0]
    f = np.float32
    xs = {"p": np.asarray(inputs["x_prompt"], f), "s": np.asarray(inputs["x_sample"], f)}
    cs_ = {"p": np.asarray(inputs["c_prompt"], f), "s": np.asarray(inputs["c_sample"], f)}
    seq_lens = [xs[w].shape[1] for (w, _) in seqs_per_core[0]]
    Smax = max(seq_lens)
    key = (tuple(seq_lens), depth, mixers)
    if key not in _CACHE:
        _CACHE[key] = build(seq_lens, depth, mixers)
    nc, _ = _CACHE[key]
    pcol, prow, lru_w = _host_layout({k: np.asarray(v, f) for k, v in inputs.items()}, depth)
    tab_mla, tab_diff, tab_gqa = _rope_tables(Smax)
    shared = {
        "ada_w": np.asarray(inputs["ada_w"], f), "ada_b": np.asarray(inputs["ada_b"], f),
        "w_in": np.asarray(inputs["w_in"], f), "w_out": np.asarray(inputs["w_out"], f),
        "mla_w_uq": np.asarray(inputs["mla_w_uq"], f), "mla_w_ukv": np.asarray(inputs["mla_w_ukv"], f),
        "lru_w": lru_w, "pcol": pcol, "prow": prow, "final_norm": np.asarray(inputs["final_norm"], f).reshape(1, -1),
        "ident_b": np.eye(128, dtype=f).astype(ml_dtypes.bfloat16), "ident_f": np.eye(128, dtype=f),
        "tab_mla": tab_mla, "tab_diff": tab_diff, "tab_gqa": tab_gqa,
    }
    in_maps = []
    for core in range(n_cores):
        m = dict(shared)
        cf = np.zeros((len(seq_lens), 128, 8), f)
        for i, (w, idx) in enumerate(seqs_per_core[core]):
            m[f"x{i}"] = np.ascontiguousarray(xs[w][idx])
            cf[i] = cs_[w][idx].reshape(8, 128).T
        m["cfm"] = cf
        in_maps.append(m)
    res = run_bass_kernel_spmd(nc, in_maps, core_ids=list(range(n_cores)))
    yp = np.zeros_like(xs["p"])
    ys = np.zeros_like(xs["s"])
    out = {"p": yp, "s": ys}
    for core in range(n_cores):
        for i, (w, idx) in enumerate(seqs_per_core[core]):
            out[w][idx] = res.results[core][f"y{i}"]
    return yp, ys


def kernel(x_prompt, x_sample, c_prompt, c_sample, **weights):
    inputs = dict(x_prompt=x_prompt, x_sample=x_sample, c_prompt=c_prompt, c_sample=c_sample, **weights)
    n_cores = 8
    bp = x_prompt.shape[0] // n_cores
    bs = x_sample.shape[0] // n_cores
    spc = [[("p", c * bp + j) for j in range(bp)] + [("s", c * bs + j) for j in range(bs)] for c in range(n_cores)]
    return run(inputs, spc, n_cores)
```

```python
import contextlib
import math
import numpy as np
import ml_dtypes
import concourse.bass as bass
import concourse.mybir as mybir
from concourse.bass_utils import run_bass_kernel_spmd

F32 = mybir.dt.float32
BF16 = mybir.dt.bfloat16
AF = mybir.ActivationFunctionType
ALU = mybir.AluOpType
AX = mybir.AxisListType

D = 1024
NIN = 2912
EPS = 1e-6
OFF = dict(qa=0, kva=192, kra=320, ga=352, qb=608, kb=864, vb=1120, gb=1376,
           xc=1632, gc=1888, qd=2144, kd=2400, vd=2528, gd=2656)
NPC = 64
PC_NG = 0
PC_ADAB = 8
PC_QN = 32
PC_KVN = 34
PC_GQ = 35
PC_GK = 36
PC_GQS = 37
PC_GKS = 38
PC_CB = 39
PC_CW = 41
PC_BA = 49
PC_BX = 53
PC_LAM = 57
PR_LAM = 0
PR_SUB = 128
NPR = 192


class Res:
    __slots__ = ("name", "w", "r", "excl")

    def __init__(self, name, excl=False):
        self.name = name
        self.w = {}
        self.r = {}
        self.excl = excl


class Prog:
    ENGS = ("pe", "act", "dve", "pool", "sp")

    def __init__(self, nc):
        self.nc = nc
        self.count = {e: 0 for e in self.ENGS}
        self.waited = {e: {} for e in self.ENGS}
        self.stream = {e: [] for e in self.ENGS}
        self.dma_count = {}
        self.n_ins = 0

    def _deps(self, eng, reads, writes):
        need = {}

        def add(s, v, same_ok):
            if s == eng and eng in ("pe", "sp"):
                return
            if s in self.dma_count:
                v = self.dma_count[s]
            if need.get(s, 0) < v:
                need[s] = v
        for r in reads:
            for s, v in r.w.items():
                add(s, v, False)
            if r.excl:
                for s, v in r.r.items():
                    if s != eng:
                        add(s, v, False)
        for w in writes:
            for s, v in w.w.items():
                add(s, v, True)
            for s, v in w.r.items():
                add(s, v, True)
        out = []
        for s, v in need.items():
            if self.waited[eng].get(s, 0) >= v:
                continue
            self.waited[eng][s] = v
            out.append((s, v))
        return out

    def _commit(self, tok, reads, writes):
        s, v = tok
        for r in reads:
            if r.r.get(s, 0) < v:
                r.r[s] = v
        for w in writes:
            w.w[s] = v
            w.r = {}

    def op(self, eng, fn, reads=(), writes=(), signal=True):
        waits = self._deps(eng, reads, writes)
        if signal:
            self.count[eng] += 1
            tok = (eng, self.count[eng])
        else:
            tok = (eng, self.count[eng] + 1)
        self._commit(tok, reads, writes)
        self.stream[eng].append((waits, fn, eng if signal else None, 1))
        self.n_ins += 1

    def dma(self, queue, key, out, in_, reads=(), writes=(), **kw):
        waits = self._deps(queue, reads, writes)
        self.dma_count[key] = self.dma_count.get(key, 0) + 16
        tok = (key, self.dma_count[key])
        self._commit(tok, reads, writes)

        def fn(e, out=out, in_=in_, kw=kw):
            return e.dma_start(out=out, in_=in_, **kw)
        self.stream[queue].append((waits, fn, key, 16))
        self.n_ins += 1

    def wait_all(self, eng, res_list):
        waits = self._deps(eng, (), res_list)
        self.stream[eng].append((waits, None, None, 0))

    def emit(self):
        nc = self.nc
        with contextlib.ExitStack() as st:
            sems = {}
            for e in self.ENGS:
                sems[e] = st.enter_context(nc.semaphore("s_" + e))
            for k in self.dma_count:
                sems[k] = st.enter_context(nc.semaphore("d_" + k))
            block = st.enter_context(nc.Block())

            def replay(ename):
                def body(e):
                    for waits, fn, sig, inc in self.stream[ename]:
                        for s, v in waits:
                            e.wait_ge(sems[s], v)
                        if fn is None:
                            continue
                        ins = fn(e)
                        if sig is not None:
                            ins.then_inc(sems[sig], inc)
                return body

            block.tensor(replay("pe"))
            block.scalar(replay("act"))
            block.vector(replay("dve"))
            block.gpsimd(replay("pool"))
            block.sync(replay("sp"))


class Ring:
    def __init__(self, items):
        self.items = items
        self.i = 0

    def next(self):
        it = self.items[self.i % len(self.items)]
        self.i += 1
        return it


def build(seq_lens, depth, mixers="ABCD"):
    nc = bass.Bass("TRN2", target_bir_lowering=False)
    nseq = len(seq_lens)
    Smax = max(seq_lens)
    NTmax = Smax // 128

    def din(name, shape, dt=F32):
        return nc.dram_tensor(name, list(shape), dt, kind="ExternalInput").ap()

    x_in = [din(f"x{i}", [S, D]) for i, S in enumerate(seq_lens)]
    y_out = [nc.dram_tensor(f"y{i}", [S, D], F32, kind="ExternalOutput").ap() for i, S in enumerate(seq_lens)]
    cfm = din("cfm", [nseq, 128, 8])
    ada_w = din("ada_w", [depth, D, 3 * D])
    ada_b = din("ada_b", [depth, 3 * D])
    w_in = din("w_in", [depth, D, NIN])
    w_out = din("w_out", [depth, D, D])
    w_uq = din("mla_w_uq", [depth, 192, 384])
    w_ukv = din("mla_w_ukv", [depth, 128, 512])
    lru_w = din("lru_w", [depth, 2, 2, 4, 64, 64])
    pcol = din("pcol", [depth, 128, NPC])
    prow = din("prow", [depth, NPR])
    fnorm = din("final_norm", [1, D])
    ident_b_d = din("ident_b", [128, 128], BF16)
    ident_f_d = din("ident_f", [128, 128])
    tab_mla = din("tab_mla", [2, 32, Smax])
    tab_diff = din("tab_diff", [2, 128, Smax])
    tab_gqa = din("tab_gqa", [2, 128, Smax])

    xres = [nc.dram_tensor(f"xres{i}", [S, D], F32).ap() for i, S in enumerate(seq_lens)]
    oscr = [nc.dram_tensor(f"oscr{i}", [S, 768], BF16).ap() for i, S in enumerate(seq_lens)]
    ocscr = [nc.dram_tensor(f"ocscr{i}", [256, S], BF16).ap() for i, S in enumerate(seq_lens)]

    P = Prog(nc)
    st = contextlib.ExitStack()
    with st:
        sb_bytes = [0]

        def sb(name, shape, dt=F32):
            n = 1
            for d_ in shape[1:]:
                n *= d_
            sb_bytes[0] += n * (2 if dt == BF16 else 4)
            return st.enter_context(nc.sbuf_tensor(name, list(shape), dt))

        ident_b = sb("ident_b_s", [128, 128], BF16)
        ident_f = sb("ident_f_s", [128, 65])
        ones_b = sb("ones_b", [128, 128], BF16)
        bones_b = sb("bones_b", [128, 128], BF16)
        epsc = sb("epsc", [128, 1])
        h_fm = sb("h_fm", [128, 8, Smax], BF16)
        wmix = sb("wmix", [128, 8, 1536], BF16)
        wst = [sb(f"wst{i}", [128, 1024]) for i in range(2)]
        qbuf = sb("qbuf", [128, 2, Smax], BF16)
        kbuf = sb("kbuf", [128, 2, Smax], BF16)
        lat = sb("lat", [128, Smax], BF16)
        vaug = sb("vaug", [128, NTmax, 260], BF16)
        xt = [sb(f"xt{i}", [128, 1024]) for i in range(2)]
        tabc = [sb(f"tabc{i}", [128, 512]) for i in range(1)]
        tabs = [sb(f"tabs{i}", [128, 512]) for i in range(1)]
        ptbuf = sb("ptbuf", [128, 4, 512], BF16)
        qz = sb("qz", [128, 4, 512], BF16)
        pts = [ptbuf[:, i, :] for i in range(4)]
        xn = ptbuf[:, 0:2, :].rearrange("p a n -> p (a n)")
        tmpf = [sb(f"tmpf{i}", [128, 512]) for i in range(4)]
        tmpb = [sb(f"tmpb{i}", [128, 512], BF16) for i in range(3)]
        tmpo = [sb(f"tmpo{i}", [128, 512]) for i in range(2)]
        tmpg = [sb(f"tmpg{i}", [128, 256]) for i in range(2)]
        ogs = [sb(f"og{i}", [128, 256], BF16) for i in range(2)]
        pc = sb("pc", [128, NPC])
        pr = sb("pr", [128, NPR])
        sc = sb("sc", [128, 8])
        sc_bc = sb("sc_bc", [128, 8, 128])
        fn_bc = sc_bc[:].rearrange("p k n -> p (k n)")
        shiftcol = sb("shiftcol", [128, 8])
        gscol = sb("gscol", [128, 8])
        gate_bc = sb("gate_bc", [128, 1024])
        small = sb("small", [128, 64])
        wuq = sb("wuq", [128, 2, 384], BF16)
        wuqs = sb("wuqs", [128, 2, 384], BF16)
        wukv = sb("wukv", [128, 512], BF16)
        lamc = sb("lamc", [128, 4])
        subln_bc = sb("subln_bc", [128, 4, 64])
        lruw = sb("lruw", [128, 2, 2, 2, 128], BF16)
        lsp = sb("lsp", [128, 12])

        import os as _os0
        if _os0.environ.get("KDEBUG"):
            print("SBUF bytes/partition:", sb_bytes[0])
        ps = [st.enter_context(nc.psum_tensor(f"ps{i}", [128, 512], F32)) for i in range(8)]
        r_ps = [Res(f"ps{i}", excl=True) for i in range(8)]
        SB_BANKS = (0, 1, 2, 3)
        O_BANKS = (4, 5)
        O_ALL = (4, 5, 6, 7)
        misc = Ring([6, 7])

        r_const = Res("const")
        r_h = [Res(f"h{t}") for t in range(NTmax)]
        r_wmix = Res("wmix")
        r_wst = [Res("wst0"), Res("wst1")]
        wst_ring = Ring([0, 1])
        NCmax = Smax // 512 if Smax >= 512 else 1
        r_q = [[Res(f"q{s}_{c}") for c in range(NTmax)] for s in range(2)]
        r_k = [[Res(f"k{s}_{c}") for c in range(NTmax)] for s in range(2)]
        r_lat = [Res(f"lat{c}") for c in range(NTmax)]
        r_v = [Res(f"v{c}") for c in range(NTmax)]
        r_xt = [Res("xt0"), Res("xt1")]
        xt_ring = Ring([0, 1])
        r_xn = Res("xn")
        r_tab = [Res("tab0"), Res("tab1")]
        tab_ring = Ring([0])
        r_pt = [Res(f"pt{i}") for i in range(4)]
        r_qz = [Res(f"qz{i}") for i in range(4)]
        r_tmpf = [Res(f"tmpf{i}") for i in range(4)]
        tmpf_ring = Ring([0, 1, 2, 3])
        r_tmpb = [Res(f"tmpb{i}") for i in range(3)]
        r_tmpo = [Res("tmpo0"), Res("tmpo1")]
        tmpo_ring = Ring([0, 1])
        r_tmpg = [Res("tmpg0"), Res("tmpg1")]
        tmpg_ring = Ring([0, 1])
        tmpb_ring = Ring([0, 1, 2])
        r_og = [Res("og0"), Res("og1")]
        og_ring = Ring([0, 1])
        r_pc = Res("pc")
        r_mod = Res("mod")
        r_gate = Res("gate")
        r_small = Res("small")
        rcol = [small[:, 44:45], small[:, 46:47]]
        r_rcol = [Res("rcol0"), Res("rcol1")]
        r_wa = Res("wa")
        r_ot = Res("ot")
        r_ofm = Res("ofm")
        r_oscr = [[Res(f"oscr{i}_{t}") for t in range(S // 128)] for i, S in enumerate(seq_lens)]
        r_ocscr = [Res(f"ocscr{i}") for i in range(nseq)]
        r_xres = [[Res(f"xres{i}_{t}") for t in range(S // 128)] for i, S in enumerate(seq_lens)]
        r_y = Res("y")

        def MM(out, lhsT, rhs, start, stop, rd, wr, sig=True, **kw):
            P.op("pe", lambda e: e.matmul(out, lhsT=lhsT, rhs=rhs, start=start, stop=stop, **kw),
                 reads=rd, writes=wr, signal=sig)

        def TR(out, in_, ident, rd, wr, sig=True):
            P.op("pe", lambda e: e.transpose(out, in_, ident), reads=rd, writes=wr, signal=sig)

        def ACT(out, in_, func, rd, wr, scale=1.0, bias=None, accum=None):
            def fn(e):
                kw = {}
                if bias is not None:
                    kw["bias"] = bias
                if accum is not None:
                    kw["accum_out"] = accum
                return e.activation(out=out, in_=in_, func=func, scale=scale, **kw)
            P.op("act", fn, reads=rd, writes=wr)

        def eng_of(P_, name):
            return name

        def TT(eng, out, in0, in1, op, rd, wr):
            P.op(eng, lambda e: e.tensor_tensor(out=out, in0=in0, in1=in1, op=op), reads=rd, writes=wr)

        def TS(eng, out, in0, s1, s2, op0, op1, rd, wr):
            if op1 is None:
                P.op(eng, lambda e: e.tensor_scalar(out=out, in0=in0, scalar1=s1, scalar2=None, op0=op0), reads=rd, writes=wr)
            else:
                P.op(eng, lambda e: e.tensor_scalar(out=out, in0=in0, scalar1=s1, scalar2=s2, op0=op0, op1=op1), reads=rd, writes=wr)

        def STT(out, in0, scalar, in1, op0, op1, rd, wr):
            P.op("dve", lambda e: e.scalar_tensor_tensor(out=out, in0=in0, scalar=scalar, in1=in1, op0=op0, op1=op1),
                 reads=rd, writes=wr)

        def CP(eng, out, in_, rd, wr):
            P.op(eng, lambda e: e.tensor_copy(out=out, in_=in_), reads=rd, writes=wr)

        def RCP(out, in_, rd, wr):
            P.op("dve", lambda e: e.reciprocal(out=out, in_=in_), reads=rd, writes=wr)

        def MEMSET(eng, ap, val, wr):
            P.op(eng, lambda e: e.memset(ap, val), writes=wr)

        dma_i = [0]
        import os as _os
        ROPE_ENG = _os.environ.get("ROPE_ENG", "dve")

        def LOAD(key, out, in_, rd, wr, **kw):
            P.dma("sp", key, out, in_, reads=rd, writes=wr, **kw)

        def rstd_act(out, in_, n, rd, wr):
            ACT(out, in_, AF.Ln, list(rd) + [r_const], wr, scale=1.0 / n, bias=epsc[0:out.shape[0], 0:1])
            ACT(out, out, AF.Exp, wr, wr, scale=-0.5)

        LOAD("c0", ident_b[:], ident_b_d, [], [r_const])
        LOAD("c0", ident_f[:], ident_f_d[:, 0:65], [], [r_const])
        MEMSET("pool", ones_b[:], 1.0, [r_const])
        MEMSET("pool", bones_b[:], 0.0, [r_const])
        MEMSET("pool", bones_b[0:64, 0:64], 1.0, [r_const])
        MEMSET("pool", bones_b[64:128, 64:128], 1.0, [r_const])
        MEMSET("pool", epsc[:], EPS, [r_const])
        MEMSET("pool", vaug[:], 1.0, r_v)
        MEMSET("pool", qz[:], 0.0, r_qz)

        if len(mixers) < 4:
            zt = sb("zt", [128, 768], BF16)
            r_zt = Res("zt")
            MEMSET("pool", zt[:], 0.0, [r_zt])
            for i_, S_ in enumerate(seq_lens):
                for t_ in range(S_ // 128):
                    P.dma("sp", "zst", oscr[i_][t_ * 128:(t_ + 1) * 128, :], zt[:], reads=[r_zt], writes=[r_oscr[i_][t_]])
                    for cc_ in range(2):
                        P.dma("sp", "zst", ocscr[i_][cc_ * 128:(cc_ + 1) * 128, t_ * 128:(t_ + 1) * 128], zt[:, 0:128], reads=[r_zt], writes=[r_ocscr[i_]])

        def load_w(src2d, ncols, dst_col, swap=None, rows=D, dst=None, gaincol=None):
            nk = rows // 128
            i = wst_ring.next()
            stg = wst[i][:, 0:nk * ncols].rearrange("p (k n) -> p k n", k=nk)
            LOAD(f"wst{i}", stg, src2d.rearrange("(k p) n -> p k n", p=128), [], [r_wst[i]])
            return i, stg

        cast_i = [0]

        def cast_cols(stg_ap, i, dst_ap, eng=None):
            cast_i[0] += 1
            if cast_i[0] % 2 == 0:
                ACT(dst_ap, stg_ap, AF.Copy, [r_wst[i]], [r_wmix])
            else:
                CP("dve", dst_ap, stg_ap, [r_wst[i]], [r_wmix])

        def load_cast(l, src_c0, n, dst_c0, extra=None):
            for p0 in range(0, n, 128):
                pn = min(128, n - p0)
                i, stg = load_w(w_in[l][:, src_c0 + p0:src_c0 + p0 + pn], pn, 0)
                cast_cols(stg, i, wmix[:, :, dst_c0 + p0:dst_c0 + p0 + pn])
                if extra is not None:
                    extra(i, stg, p0, pn)

        def swapped(ap3, half):
            return ap3.rearrange("p k (m b j) -> p k m b j", b=2, j=half)

        def layer_prep(l):
            LOAD("c1", pc[:], pcol[l], [r_pc, r_mod, r_wa], [r_pc])
            LOAD("c1", pr[:], prow[l:l + 1, :].partition_broadcast(128), [r_pc], [r_pc])
            if "A" in mixers:
                i = wst_ring.next()
                LOAD(f"wst{i}", wst[i][:, 0:384], w_uq[l][0:128, :], [], [r_wst[i]])
                LOAD(f"wst{i}", wst[i][0:64, 384:768], w_uq[l][128:192, :], [], [r_wst[i]])
                for kk, rows in ((0, 128), (1, 64)):
                    TS("pool", wuq[0:rows, kk, :], wst[i][0:rows, kk * 384:(kk + 1) * 384], pc[0:rows, PC_QN + kk:PC_QN + kk + 1], None,
                       ALU.mult, None, [r_wst[i], r_pc], [r_wa])
                    CP("pool", wuqs[0:rows, kk, :], wuq[0:rows, kk, :], [r_wa], [r_wa])
                    v = wuq[0:rows, kk, :].rearrange("p (h c) -> p h c", c=96)
                    vs = wuqs[0:rows, kk, :].rearrange("p (h c) -> p h c", c=96)
                    CP("pool", vs[:, :, 64:80], v[:, :, 80:96], [r_wa], [r_wa])
                    CP("pool", vs[:, :, 80:96], v[:, :, 64:80], [r_wa], [r_wa])
                i = wst_ring.next()
                LOAD(f"wst{i}", wst[i][:, 0:512], w_ukv[l], [], [r_wst[i]])
                sv = wst[i][:, 0:512].rearrange("p (h t c) -> p h t c", h=4, t=2)
                dv_ = wukv[:, :].rearrange("p (t h c) -> p t h c", t=2, h=4)
                for t_ in range(2):
                    TS("pool", dv_[:, t_], sv[:, :, t_, :], pc[:, PC_KVN:PC_KVN + 1], None, ALU.mult, None,
                       [r_wst[i], r_pc], [r_wa])
            if "B" in mixers:
                lam_init = 0.8 - 0.6 * math.exp(-0.3 * l)
                lv = pr[:, PR_LAM:PR_LAM + 128].rearrange("p (a d) -> p a d", a=4)
                TT("dve", small[:, 0:32], lv[:, 0, :], lv[:, 1, :], ALU.mult, [r_pc], [r_small])
                TT("dve", small[:, 32:64], lv[:, 2, :], lv[:, 3, :], ALU.mult, [r_pc], [r_small])
                P.op("dve", lambda e: e.tensor_reduce(out=lamc[:, 0:2], in_=small[:, 0:64].rearrange("p (a d) -> p a d", a=2),
                                                     axis=AX.X, op=ALU.add), reads=[r_small], writes=[r_wa])
                ACT(lamc[:, 0:2], lamc[:, 0:2], AF.Exp, [r_wa], [r_wa])
                TT("dve", lamc[:, 2:3], lamc[:, 0:1], lamc[:, 1:2], ALU.subtract, [r_wa], [r_wa])
                TS("dve", lamc[:, 3:4], lamc[:, 2:3], -1.0, -lam_init, ALU.mult, ALU.add, [r_wa], [r_wa])
                for qb in range(4):
                    TS("dve", subln_bc[:, qb, :], pr[:, PR_SUB:PR_SUB + 64], 1.0 - lam_init, None, ALU.mult, None,
                       [r_pc], [r_wa])
            if "C" in mixers:
                i = wst_ring.next()
                MEMSET("pool", wst[i][:, 0:1024], 0.0, [r_wst[i]])
                for m in range(2):
                    for d_ in range(2):
                        for hh in range(4):
                            cc, hb = hh // 2, (hh % 2) * 64
                            col = ((m * 2 + d_) * 2 + cc) * 128 + hb
                            LOAD(f"wst{i}", wst[i][hb:hb + 64, col:col + 64], lru_w[l, m, d_, hh], [], [r_wst[i]])
                CP("pool", lruw[:].rearrange("p m d c n -> p (m d c n)"), wst[i][:, 0:1024], [r_wst[i]], [r_wa])
                ACT(lsp[:, 0:4], pc[:, PC_LAM:PC_LAM + 4], AF.Exp, [r_pc], [r_wa], scale=-1.0)
                ACT(lsp[:, 0:4], lsp[:, 0:4], AF.Ln, [r_wa], [r_wa], bias=1.0)
                TS("dve", lsp[:, 0:4], lsp[:, 0:4], -8.0, None, ALU.mult, None, [r_wa], [r_wa])
                TS("dve", lsp[:, 4:8], pc[:, PC_BA:PC_BA + 4], -1.0, None, ALU.mult, None, [r_pc], [r_wa])
                TS("dve", lsp[:, 8:12], pc[:, PC_BX:PC_BX + 4], -1.0, None, ALU.mult, None, [r_pc], [r_wa])

        def phase0(l, s):
            LOAD("c2", sc[:], cfm[s], [r_mod], [r_mod])
            ACT(small[:, 8:16], sc[:], AF.Exp, [r_mod], [r_small], scale=-1.0)
            TS("dve", small[:, 8:16], small[:, 8:16], 1.0, None, ALU.add, None, [r_small], [r_small])
            RCP(small[:, 8:16], small[:, 8:16], [r_small], [r_small])
            TT("dve", sc[:], sc[:], small[:, 8:16], ALU.mult, [r_small, r_mod], [r_mod])
            CP("dve", sc_bc[:], sc[:, :].unsqueeze(2).broadcast_to([128, 8, 128]), [r_mod], [r_mod])
            mb = misc.next()
            for j in range(24):
                i, stg = load_w(ada_w[l][:, j * 128:(j + 1) * 128], 128, 0)
                if j < 16:
                    for k in range(8):
                        MM(ps[mb][:, j * 2:j * 2 + 2], stg[:, k, :], sc_bc[:, k, 0:2],
                           k == 0, k == 7, [r_wst[i], r_mod], [r_ps[mb]], sig=(k == 7))
                    if j == 15:
                        pv = ps[mb][:, 0:32].rearrange("p (j t) -> p j t", t=2)
                        TT("dve", shiftcol[:], pv[:, 0:8, 0], pc[:, PC_ADAB:PC_ADAB + 8], ALU.add, [r_ps[mb], r_pc], [r_mod])
                        TT("dve", gscol[:], pv[:, 8:16, 0], pc[:, PC_ADAB + 8:PC_ADAB + 16], ALU.add, [r_ps[mb], r_pc], [r_mod])
                        STT(gscol[:], gscol[:], 1.0, pc[:, PC_NG:PC_NG + 8], ALU.add, ALU.mult, [r_mod, r_pc], [r_mod])
                else:
                    gb = misc.next()
                    c0 = (j - 16) * 128
                    a = tmpf_ring.next()
                    LOAD(f"abg{a}", tmpf[a][:, 0:128], ada_b[l:l + 1, 2 * D + c0:2 * D + c0 + 128].partition_broadcast(128),
                         [], [r_tmpf[a]])
                    for k in range(8):
                        MM(ps[gb][:, 0:128], sc_bc[:, k, :], stg[:, k, :], k == 0, k == 7, [r_wst[i], r_mod], [r_ps[gb]], sig=(k == 7))
                    TT("dve", gate_bc[:, c0:c0 + 128], ps[gb][:, 0:128], tmpf[a][:, 0:128], ALU.add,
                       [r_ps[gb], r_tmpf[a]], [r_gate])

        def phase1(l, s):
            S = seq_lens[s]
            src = x_in[s] if l == 0 else xres[s]
            misc.items = [6, 7, 0, 1, 2, 3]
            for t in range(S // 128):
                xi = xt_ring.next()
                rd = [] if l == 0 else [r_xres[s][t]]
                LOAD(f"xt{xi}", xt[xi][:], src[t * 128:(t + 1) * 128, :], rd, [r_xt[xi]])
                rc, r_rc = rcol[t % 2], r_rcol[t % 2]
                xnb = ptbuf[:, 2 * (t % 2):2 * (t % 2) + 2, :].rearrange("p a n -> p (a n)")
                r_xnb = [r_pt[2 * (t % 2)], r_pt[2 * (t % 2) + 1]]
                ACT(tmpf[0][:].bitcast(BF16), xt[xi][:], AF.Square, [r_xt[xi]], [r_tmpf[0], r_rc], accum=rc)
                rstd_act(rc, rc, float(D), [r_rc], [r_rc])
                ACT(xnb, xt[xi][:], AF.Copy, [r_xt[xi], r_rc], r_xnb, scale=rc)
                mb = misc.next()
                pT = ps[mb][:].bitcast(BF16)
                for k in range(8):
                    TR(pT[:, k * 128:(k + 1) * 128], xnb[:, k * 128:(k + 1) * 128], ident_b[:], r_xnb + [r_const], [r_ps[mb]], sig=(k == 7))
                for k in range(8):
                    TS("dve", h_fm[:, k, t * 128:(t + 1) * 128], pT[:, k * 128:(k + 1) * 128], gscol[:, k:k + 1], shiftcol[:, k:k + 1],
                       ALU.mult, ALU.add, [r_ps[mb], r_mod], [r_h[t]])
            misc.items = [6, 7]

        def proj_fm(S, c, wcol, rows=128, wrows=None):
            cw = min(512, S)
            mb = misc.next()
            tl = [r_h[t] for t in range(c * cw // 128, (c + 1) * cw // 128)]
            for k in range(8):
                MM(ps[mb][0:rows, 0:cw], wmix[:, k, wcol:wcol + rows], h_fm[:, k, c * cw:(c + 1) * cw], k == 0, k == 7,
                   [r_wmix] + tl, [r_ps[mb]], sig=(k == 7))
            return mb

        def load_tab(tab, c, cw, rows=slice(0, 128), trows=None):
            ti = tab_ring.next()
            nrow = rows.stop - rows.start
            LOAD(f"tab{ti}", tabc[ti][rows, 0:cw], tab[0, 0:nrow, c * cw:(c + 1) * cw], [], [r_tab[ti]])
            LOAD(f"tab{ti}", tabs[ti][rows, 0:cw], tab[1, 0:nrow, c * cw:(c + 1) * cw], [], [r_tab[ti]])
            return ti

        def rope_evac(mb, mbs, ti, rows, cw, out_ap, wr, gcol=None, gscol_=None, rstd=None, rstd_rd=()):
            a = tmpf_ring.next()
            b = tmpf_ring.next()
            if gcol is None:
                TT("dve", tmpf[a][rows, 0:cw], ps[mb][rows, 0:cw], tabc[ti][rows, 0:cw], ALU.mult, [r_ps[mb], r_tab[ti]], [r_tmpf[a]])
                TT("dve", tmpf[b][rows, 0:cw], ps[mbs][rows, 0:cw], tabs[ti][rows, 0:cw], ALU.mult, [r_ps[mbs], r_tab[ti]], [r_tmpf[b]])
            else:
                TS("dve", tmpf[a][rows, 0:cw], ps[mb][rows, 0:cw], gcol, None, ALU.mult, None, [r_ps[mb], r_pc], [r_tmpf[a]])
                TT("dve", tmpf[a][rows, 0:cw], tmpf[a][rows, 0:cw], tabc[ti][rows, 0:cw], ALU.mult, [r_tmpf[a], r_tab[ti]], [r_tmpf[a]])
                TS("dve", tmpf[b][rows, 0:cw], ps[mbs][rows, 0:cw], gscol_, None, ALU.mult, None, [r_ps[mbs], r_pc], [r_tmpf[b]])
                TT("dve", tmpf[b][rows, 0:cw], tmpf[b][rows, 0:cw], tabs[ti][rows, 0:cw], ALU.mult, [r_tmpf[b], r_tab[ti]], [r_tmpf[b]])
            if rstd is None:
                TT(ROPE_ENG, out_ap, tmpf[a][rows, 0:cw], tmpf[b][rows, 0:cw], ALU.add, [r_tmpf[a], r_tmpf[b]], wr)
            else:
                TT(ROPE_ENG, tmpf[a][rows, 0:cw], tmpf[a][rows, 0:cw], tmpf[b][rows, 0:cw], ALU.add, [r_tmpf[a], r_tmpf[b]], [r_tmpf[a]])
                TT(ROPE_ENG, out_ap, tmpf[a][rows, 0:cw], rstd, ALU.mult, [r_tmpf[a]] + list(rstd_rd), wr)

        def run_attention(S, groups):
            cw = min(512, S)
            NC = S // cw
            NKT = S // 128
            steps = []
            oset = [0]
            for g in groups:
                nu = len(g["units"])
                for c in range(NC):
                    if nu == 1:
                        bs = (O_BANKS[oset[0] % 2],)
                        oset[0] += 1
                    else:
                        bs = O_BANKS
                    for ui, u in enumerate(g["units"]):
                        for kt in range(0, NKT, 2):
                            kts = [kt] if kt + 1 >= NKT else [kt, kt + 1]
                            steps.append(dict(g=g, c=c, lanes=[(u, k_, bs[ui]) for k_ in kts],
                                              first=(ui == 0 and kt == 0 and c == 0), cfirst=(ui == 0 and kt == 0),
                                              last=(ui == nu - 1 and kts[-1] == NKT - 1), obs=bs))

            def emit_S(i):
                sp_ = steps[i]
                if sp_["first"] and sp_["g"].get("pre") is not None:
                    sp_["g"]["pre"]()
                if sp_["cfirst"] and sp_["g"].get("prec") is not None:
                    sp_["g"]["prec"](sp_["c"])
                for j, (u, kt, ob) in enumerate(sp_["lanes"]):
                    qap, qres = u["q"](sp_["c"])
                    bank = SB_BANKS[(2 * i + j) % 4]
                    kap, kres = u["k"](kt)
                    MM(ps[bank][:, 0:cw], kap, qap, True, True, list(kres) + list(qres), [r_ps[bank]])

            emit_S(0)
            for i, sp_ in enumerate(steps):
                if i + 1 < len(steps):
                    emit_S(i + 1)
                nl = len(sp_["lanes"])
                for j, (u, kt, ob) in enumerate(sp_["lanes"]):
                    bank = SB_BANKS[(2 * i + j) % 4]
                    pi = (2 * i + j) % 4
                    ACT(pts[pi][:, 0:cw], ps[bank][:, 0:cw], AF.Exp, [r_ps[bank]], [r_pt[pi]], scale=u["scale"])
                for j, (u, kt, ob) in enumerate(sp_["lanes"]):
                    pi = (2 * i + j) % 4
                    vap, vres = u["v"](kt)
                    MM(ps[ob][0:65, 0:cw], vap, pts[pi][:, 0:cw], kt == 0, kt == NKT - 1, [r_pt[pi]] + list(vres),
                       [r_ps[ob]], sig=(kt == NKT - 1 or j == nl - 1))
                if sp_["last"]:
                    sp_["g"]["epilogue"](sp_["c"], sp_["obs"])

        def o_copy(ob, cw):
            a = tmpo_ring.next()
            CP("dve", tmpo[a][0:65, 0:cw], ps[ob][0:65, 0:cw], [r_ps[ob]], [r_tmpo[a]])
            return a

        def o_transposed(a, cw):
            mb = misc.next()
            nqb = cw // 128
            for qb in range(nqb):
                TR(ps[mb][:, qb * 65:(qb + 1) * 65], tmpo[a][0:65, qb * 128:(qb + 1) * 128], ident_f[0:65, 0:65],
                   [r_tmpo[a], r_const], [r_ps[mb]], sig=(qb == nqb - 1))
            return mb, ps[mb][:, 0:nqb * 65].rearrange("p (q d) -> p q d", d=65)

        def gate_mm(S, c, gcol, width):
            cw = min(512, S)
            nqb = cw // 128
            gb = misc.next()
            for qb in range(nqb):
                t = c * nqb + qb
                for k in range(8):
                    MM(ps[gb][:, qb * width:(qb + 1) * width], h_fm[:, k, t * 128:(t + 1) * 128], wmix[:, k, gcol:gcol + width],
                       k == 0, k == 7, [r_h[t], r_wmix], [r_ps[gb]], sig=(qb == nqb - 1 and k == 7))
            return gb

        def gate_fin(gb, n):
            a = tmpg_ring.next()
            ACT(tmpg[a][:, 0:n], ps[gb][:, 0:n], AF.Exp, [r_ps[gb]], [r_tmpg[a]], scale=-1.0)
            TS("dve", tmpg[a][:, 0:n], tmpg[a][:, 0:n], 1.0, None, ALU.add, None, [r_tmpg[a]], [r_tmpg[a]])
            RCP(tmpg[a][:, 0:n], tmpg[a][:, 0:n], [r_tmpg[a]], [r_tmpg[a]])
            TT("dve", tmpg[a][:, 0:n], ps[gb][:, 0:n], tmpg[a][:, 0:n], ALU.mult, [r_ps[gb], r_tmpg[a]], [r_tmpg[a]])
            return a

        def store_o(s, c, cw, col, og_i, width=64):
            nqb = cw // 128
            dst = oscr[s][c * cw:(c + 1) * cw, col:col + width].rearrange("(q p) d -> p q d", p=128)
            src = ogs[og_i][:, 0:nqb * width].rearrange("p (q d) -> p q d", d=width)
            tl = [r_oscr[s][c * nqb + q] for q in range(nqb)]
            P.dma("pool", f"ost{og_i}", dst, src, reads=[r_og[og_i]], writes=tl)

        def std_epilogue(s, S, col0, gcol0, h):
            def ep(c, obs):
                cw = min(512, S)
                nqb = cw // 128
                gb = gate_mm(S, c, gcol0 + h * 64, 64)
                oc_ = o_copy(obs[0], cw)
                mb, ov = o_transposed(oc_, cw)
                RCP(small[:, 20:20 + nqb], ov[:, :, 64], [r_ps[mb]], [r_small])
                a = tmpf_ring.next()
                n = nqb * 64
                av = tmpf[a][:, 0:n].rearrange("p (q d) -> p q d", d=64)
                TT("dve", av, ov[:, :, 0:64], small[:, 20:20 + nqb].unsqueeze(2).broadcast_to([128, nqb, 64]), ALU.mult,
                   [r_ps[mb], r_small], [r_tmpf[a]])
                sg = gate_fin(gb, n)
                oi = og_ring.next()
                TT("pool", ogs[oi][:, 0:n], tmpf[a][:, 0:n], tmpg[sg][:, 0:n], ALU.mult, [r_tmpf[a], r_tmpg[sg]], [r_og[oi]])
                store_o(s, c, cw, col0 + h * 64, oi)
            return ep

        def mixer_D(l, s):
            S = seq_lens[s]
            cw = min(512, S)
            NC = S // cw
            P.wait_all("pool", [r_wmix])
            def q_extra(i, stg, p0, pn):
                sv = swapped(stg, 16)
                dvw = swapped(wmix[:, :, 256 + p0:256 + p0 + pn], 16)
                CP("pool", dvw[:, :, :, 0, :], sv[:, :, :, 1, :], [r_wst[i]], [r_wmix])
                CP("pool", dvw[:, :, :, 1, :], sv[:, :, :, 0, :], [r_wst[i]], [r_wmix])
            load_cast(l, OFF["qd"], 256, 0, extra=q_extra)
            i, stg = load_w(w_in[l][:, OFF["kd"]:OFF["kd"] + 128], 128, 0)
            for g in range(2):
                for r in range(2):
                    c0 = 512 + g * 128 + r * 64
                    CP("pool", wmix[:, :, c0:c0 + 64], stg[:, :, g * 64:(g + 1) * 64], [r_wst[i]], [r_wmix])
                    sv = swapped(stg[:, :, g * 64:(g + 1) * 64], 16)
                    dvw = swapped(wmix[:, :, c0 + 256:c0 + 256 + 64], 16)
                    CP("pool", dvw[:, :, :, 0, :], sv[:, :, :, 1, :], [r_wst[i]], [r_wmix])
                    CP("pool", dvw[:, :, :, 1, :], sv[:, :, :, 0, :], [r_wst[i]], [r_wmix])
            load_cast(l, OFF["vd"], 128, 1024)
            load_cast(l, OFF["gd"], 256, 1152)
            for c in range(NC):
                ti = load_tab(tab_gqa, c, cw)
                for fc in range(4):
                    isq = fc < 2
                    wc = fc * 128 if isq else 512 + (fc - 2) * 128
                    mb = proj_fm(S, c, wc)
                    mbs = proj_fm(S, c, wc + 256)
                    sq = tmpb_ring.next()
                    ACT(tmpb[sq][:, 0:cw], ps[mb][:, 0:cw], AF.Square, [r_ps[mb]], [r_tmpb[sq]])
                    sb_ = SB_BANKS[fc % 2]
                    MM(ps[sb_][:, 0:cw], bones_b[:], tmpb[sq][:, 0:cw], True, True, [r_const, r_tmpb[sq]], [r_ps[sb_]])
                    rs = tmpf_ring.next()
                    rstd_act(tmpf[rs][:, 0:cw], ps[sb_][:, 0:cw], 64.0, [r_ps[sb_]], [r_tmpf[rs]])
                    if isq:
                        out_ap, wr = qbuf[:, fc, c * cw:(c + 1) * cw], [r_q[fc][c]]
                        g1, g2 = pc[:, PC_GQ:PC_GQ + 1], pc[:, PC_GQS:PC_GQS + 1]
                    else:
                        out_ap, wr = kbuf[:, fc - 2, c * cw:(c + 1) * cw], [r_k[fc - 2][c]]
                        g1, g2 = pc[:, PC_GK:PC_GK + 1], pc[:, PC_GKS:PC_GKS + 1]
                    rope_evac(mb, mbs, ti, slice(0, 128), cw, out_ap, wr, gcol=g1, gscol_=g2,
                              rstd=tmpf[rs][:, 0:cw], rstd_rd=[r_tmpf[rs]])
                for qb in range(cw // 128):
                    t = c * (cw // 128) + qb
                    mb = misc.next()
                    for k in range(8):
                        MM(ps[mb][:, 0:128], h_fm[:, k, t * 128:(t + 1) * 128], wmix[:, k, 1024:1152], k == 0, k == 7,
                           [r_h[t], r_wmix], [r_ps[mb]], sig=(k == 7))
                    CP("dve", vaug[:, t, 0:130].rearrange("p (h d) -> p h d", d=65)[:, :, 0:64],
                       ps[mb][:, 0:128].rearrange("p (h d) -> p h d", d=64), [r_ps[mb]], [r_v[t]])
            groups = []
            for fc in range(2):
                units = []
                for hh in range(2):
                    def qf(c, hh=hh):
                        return qz[:, hh, 0:cw], [r_qz[hh]]

                    def kf(kt, fc=fc):
                        return kbuf[:, fc, kt * 128:(kt + 1) * 128], [r_k[fc][kt * 128 // cw]]

                    def vf(kt, fc=fc):
                        return vaug[:, kt, fc * 65:(fc + 1) * 65], [r_v[kt]]
                    units.append(dict(q=qf, k=kf, v=vf, scale=64 ** -0.5))

                def prec(c, fc=fc):
                    CP("pool", qz[0:64, 0, 0:cw], qbuf[0:64, fc, c * cw:(c + 1) * cw], [r_q[fc][c]], [r_qz[0]])
                    CP("dve", qz[64:128, 1, 0:cw], qbuf[64:128, fc, c * cw:(c + 1) * cw], [r_q[fc][c]], [r_qz[1]])
                e0 = std_epilogue(s, S, 512, 1152, 2 * fc)
                e1 = std_epilogue(s, S, 512, 1152, 2 * fc + 1)

                def ep2(c, obs, e0=e0, e1=e1):
                    e0(c, (obs[0],))
                    e1(c, (obs[1],))
                groups.append(dict(units=units, epilogue=ep2, prec=prec))
            MEMSET("pool", qz[64:128, 0, :], 0.0, [r_qz[0]])
            MEMSET("pool", qz[0:64, 1, :], 0.0, [r_qz[1]])
            run_attention(S, groups)

        def mixer_B(l, s):
            S = seq_lens[s]
            cw = min(512, S)
            NC = S // cw
            P.wait_all("pool", [r_wmix])
            for nm, c0 in (("qb", 0), ("kb", 512)):
                def sw_extra(i, stg, p0, pn, c0=c0):
                    d0 = c0 + 256 + p0
                    CP("pool", wmix[:, :, d0:d0 + pn], stg, [r_wst[i]], [r_wmix])
                    sv = stg.rearrange("p k (m j) -> p k m j", j=32)
                    dvw = wmix[:, :, d0:d0 + pn].rearrange("p k (m j) -> p k m j", j=32)
                    CP("pool", dvw[:, :, :, 0:4], sv[:, :, :, 4:8], [r_wst[i]], [r_wmix])
                    CP("pool", dvw[:, :, :, 4:8], sv[:, :, :, 0:4], [r_wst[i]], [r_wmix])
                load_cast(l, OFF[nm], 256, c0, extra=sw_extra)
            load_cast(l, OFF["vb"], 256, 1024)
            load_cast(l, OFF["gb"], 256, 1280)
            for c in range(NC):
                ti = load_tab(tab_diff, c, cw)
                for fc in range(4):
                    isq = fc < 2
                    wc = fc * 128 if isq else 512 + (fc - 2) * 128
                    mb = proj_fm(S, c, wc)
                    mbs = proj_fm(S, c, wc + 256)
                    if isq:
                        out_ap, wr = qbuf[:, fc, c * cw:(c + 1) * cw], [r_q[fc][c]]
                    else:
                        out_ap, wr = kbuf[:, fc - 2, c * cw:(c + 1) * cw], [r_k[fc - 2][c]]
                    rope_evac(mb, mbs, ti, slice(0, 128), cw, out_ap, wr)
                for qb in range(cw // 128):
                    t = c * (cw // 128) + qb
                    mb = misc.next()
                    for k in range(8):
                        MM(ps[mb][:, 0:256], h_fm[:, k, t * 128:(t + 1) * 128], wmix[:, k, 1024:1280], k == 0, k == 7,
                           [r_h[t], r_wmix], [r_ps[mb]], sig=(k == 7))
                    CP("dve", vaug[:, t, 0:260].rearrange("p (h d) -> p h d", d=65)[:, :, 0:64],
                       ps[mb][:, 0:256].rearrange("p (h d) -> p h d", d=64), [r_ps[mb]], [r_v[t]])
            groups = []
            for h in range(4):
                fc = h // 2
                units = []
                for j in range(2):
                    b_ = (h % 2) * 2 + j

                    def qf(c, b_=b_):
                        return qz[:, b_, 0:cw], [r_qz[b_]]

                    def kf(kt, fc=fc):
                        return kbuf[:, fc, kt * 128:(kt + 1) * 128], [r_k[fc][kt * 128 // cw]]

                    def vf(kt, h=h):
                        return vaug[:, kt, h * 65:(h + 1) * 65], [r_v[kt]]
                    units.append(dict(q=qf, k=kf, v=vf, scale=32 ** -0.5))

                def prec(c, h=h, fc=fc):
                    for j in range(2):
                        b_ = (h % 2) * 2 + j
                        rs_ = slice(b_ * 32, (b_ + 1) * 32)
                        CP("pool" if j == 0 else "dve", qz[rs_, b_, 0:cw], qbuf[rs_, fc, c * cw:(c + 1) * cw], [r_q[fc][c]], [r_qz[b_]])

                def ep(c, obs, h=h):
                    nqb = cw // 128
                    n = nqb * 64
                    gb = gate_mm(S, c, 1280 + h * 64, 64)
                    oc1 = o_copy(obs[0], cw)
                    oc2 = o_copy(obs[1], cw)
                    ge = tmpg_ring.next()
                    gc_ = tmpf_ring.next()
                    ACT(tmpg[ge][:, 0:n], ps[gb][:, 0:n], AF.Exp, [r_ps[gb]], [r_tmpg[ge]], scale=-1.0)
                    ACT(tmpf[gc_][:, 0:n], ps[gb][:, 0:n], AF.Copy, [r_ps[gb]], [r_tmpf[gc_]])
                    mb1, ov1 = o_transposed(oc1, cw)
                    mb2, ov2 = o_transposed(oc2, cw)
                    RCP(small[:, 24:24 + nqb], ov1[:, :, 64], [r_ps[mb1]], [r_small])
                    RCP(small[:, 28:28 + nqb], ov2[:, :, 64], [r_ps[mb2]], [r_small])
                    TS("dve", small[:, 28:28 + nqb], small[:, 28:28 + nqb], lamc[:, 3:4], None, ALU.mult, None, [r_small, r_wa], [r_small])
                    a = tmpf_ring.next()
                    b = tmpf_ring.next()
                    av = tmpf[a][:, 0:n].rearrange("p (q d) -> p q d", d=64)
                    bv = tmpf[b][:, 0:n].rearrange("p (q d) -> p q d", d=64)
                    TT("dve", av, ov1[:, :, 0:64], small[:, 24:24 + nqb].unsqueeze(2).broadcast_to([128, nqb, 64]), ALU.mult,
                       [r_ps[mb1], r_small], [r_tmpf[a]])
                    TT("dve", bv, ov2[:, :, 0:64], small[:, 28:28 + nqb].unsqueeze(2).broadcast_to([128, nqb, 64]), ALU.mult,
                       [r_ps[mb2], r_small], [r_tmpf[b]])
                    TT("pool", tmpf[a][:, 0:n], tmpf[a][:, 0:n], tmpf[b][:, 0:n], ALU.add, [r_tmpf[a], r_tmpf[b]], [r_tmpf[a]])
                    TT("pool", tmpf[b][:, 0:n], tmpf[a][:, 0:n], tmpf[a][:, 0:n], ALU.mult, [r_tmpf[a]], [r_tmpf[b]])
                    P.op("dve", lambda e: e.tensor_reduce(out=small[:, 32:32 + nqb], in_=bv, axis=AX.X, op=ALU.add),
                         reads=[r_tmpf[b]], writes=[r_small])
                    rstd_act(small[:, 32:32 + nqb], small[:, 32:32 + nqb], 64.0, [r_small], [r_small])
                    TT("dve", av, av, small[:, 32:32 + nqb].unsqueeze(2).broadcast_to([128, nqb, 64]), ALU.mult,
                       [r_tmpf[a], r_small], [r_tmpf[a]])
                    TT("pool", av, av, subln_bc[:, 0:nqb, :], ALU.mult, [r_tmpf[a], r_wa], [r_tmpf[a]])
                    TS("dve", tmpg[ge][:, 0:n], tmpg[ge][:, 0:n], 1.0, None, ALU.add, None, [r_tmpg[ge]], [r_tmpg[ge]])
                    RCP(tmpg[ge][:, 0:n], tmpg[ge][:, 0:n], [r_tmpg[ge]], [r_tmpg[ge]])
                    TT("dve", tmpg[ge][:, 0:n], tmpf[gc_][:, 0:n], tmpg[ge][:, 0:n], ALU.mult, [r_tmpf[gc_], r_tmpg[ge]], [r_tmpg[ge]])
                    sg = ge
                    oi = og_ring.next()
                    TT("pool", ogs[oi][:, 0:n], tmpf[a][:, 0:n], tmpg[sg][:, 0:n], ALU.mult, [r_tmpf[a], r_tmpg[sg]], [r_og[oi]])
                    store_o(s, c, cw, 256 + h * 64, oi)
                groups.append(dict(units=units, epilogue=ep, prec=prec))
            for b_ in range(4):
                MEMSET("pool", qz[:, b_, :], 0.0, [r_qz[b_]])
            run_attention(S, groups)

        def mixer_A(l, s):
            S = seq_lens[s]
            cw = min(512, S)
            NC = S // cw
            nslots = 2 if 2 * S <= Smax else 1
            P.wait_all("pool", [r_wmix])

            def kr_extra(i, stg, p0, pn):
                if p0 == 256:
                    CP("pool", wmix[:, :, 352:368], stg[:, :, 80:96], [r_wst[i]], [r_wmix])
                    CP("pool", wmix[:, :, 368:384], stg[:, :, 64:80], [r_wst[i]], [r_wmix])
            load_cast(l, 0, 352, 0, extra=kr_extra)
            load_cast(l, OFF["ga"], 256, 384)
            R64 = slice(64, 96)
            qn0 = qbuf[:, 1, :]
            qn1 = kbuf[:, 1, :]

            def qs_(slot, c):
                return qbuf[:, 0, slot * S + c * cw:slot * S + (c + 1) * cw], r_q[0][slot * NC + c]

            def ks_(slot, c):
                return kbuf[:, 0, slot * S + c * cw:slot * S + (c + 1) * cw], r_k[0][slot * NC + c]

            for c in range(NC):
                tl = [r_h[t] for t in range(c * cw // 128, (c + 1) * cw // 128)]
                csl = slice(c * cw, (c + 1) * cw)
                mbq = []
                sqs = []
                for fc, rows in ((0, 128), (1, 64)):
                    mb = proj_fm(S, c, fc * 128, rows=rows)
                    sq = tmpb_ring.next()
                    ACT(tmpb[sq][0:rows, 0:cw], ps[mb][0:rows, 0:cw], AF.Square, [r_ps[mb]], [r_tmpb[sq]])
                    mbq.append(mb)
                    sqs.append(sq)
                sb_ = SB_BANKS[0]
                MM(ps[sb_][:, 0:cw], ones_b[:, :], tmpb[sqs[0]][:, 0:cw], True, False, [r_const, r_tmpb[sqs[0]]], [r_ps[sb_]], sig=False)
                MM(ps[sb_][:, 0:cw], ones_b[0:64, :], tmpb[sqs[1]][0:64, 0:cw], False, True, [r_const, r_tmpb[sqs[1]]], [r_ps[sb_]])
                rs = tmpf_ring.next()
                rstd_act(tmpf[rs][:, 0:cw], ps[sb_][:, 0:cw], 192.0, [r_ps[sb_]], [r_tmpf[rs]])
                TT("dve", qn0[:, csl], ps[mbq[0]][:, 0:cw], tmpf[rs][:, 0:cw], ALU.mult,
                   [r_ps[mbq[0]], r_tmpf[rs]], [r_q[1][c]])
                TT("dve", qn1[0:64, csl], ps[mbq[1]][0:64, 0:cw], tmpf[rs][0:64, 0:cw], ALU.mult,
                   [r_ps[mbq[1]], r_tmpf[rs]], [r_k[1][c]])
                mb = proj_fm(S, c, 192)
                sq = tmpb_ring.next()
                ACT(tmpb[sq][:, 0:cw], ps[mb][:, 0:cw], AF.Square, [r_ps[mb]], [r_tmpb[sq]])
                sb_ = SB_BANKS[1]
                MM(ps[sb_][:, 0:cw], ones_b[:, :], tmpb[sq][:, 0:cw], True, True, [r_const, r_tmpb[sq]], [r_ps[sb_]])
                rs = tmpf_ring.next()
                rstd_act(tmpf[rs][:, 0:cw], ps[sb_][:, 0:cw], 128.0, [r_ps[sb_]], [r_tmpf[rs]])
                TT("dve", lat[:, csl], ps[mb][:, 0:cw], tmpf[rs][:, 0:cw], ALU.mult,
                   [r_ps[mb], r_tmpf[rs]], [r_lat[c]])
                for qb in range(cw // 128):
                    t = c * (cw // 128) + qb
                    vb_ = misc.next()
                    MM(ps[vb_][:, 0:256], lat[:, t * 128:(t + 1) * 128], wukv[:, 256:512], True, True, [r_lat[c], r_wa], [r_ps[vb_]])
                    CP("dve", vaug[:, t, 0:260].rearrange("p (h d) -> p h d", d=65)[:, :, 0:64],
                       ps[vb_][:, 0:256].rearrange("p (h d) -> p h d", d=64), [r_ps[vb_]], [r_v[t]])
                ti = load_tab(tab_mla, c, cw, rows=R64)
                mbk = misc.next()
                mbks = misc.next()
                for k in range(8):
                    MM(ps[mbk][64:96, 0:cw], wmix[:, k, 320:352], h_fm[:, k, c * cw:(c + 1) * cw], k == 0, k == 7,
                       [r_wmix] + tl, [r_ps[mbk]], sig=(k == 7))
                for k in range(8):
                    MM(ps[mbks][64:96, 0:cw], wmix[:, k, 352:384], h_fm[:, k, c * cw:(c + 1) * cw], k == 0, k == 7,
                       [r_wmix] + tl, [r_ps[mbks]], sig=(k == 7))
                k0, rk0 = ks_(0, c)
                rope_evac(mbk, mbks, ti, R64, cw, k0[64:96, :], [rk0])
                if nslots == 2:
                    k1, rk1 = ks_(1, c)
                    CP("pool", k1[64:96, :], k0[64:96, :], [rk0], [rk1])

            def head_proj(h):
                slot = h % nslots
                for c in range(NC):
                    csl = slice(c * cw, (c + 1) * cw)
                    qa_, rq_ = qs_(slot, c)
                    ka_, rk_ = ks_(slot, c)
                    mb = misc.next()
                    MM(ps[mb][0:96, 0:cw], wuq[:, 0, h * 96:(h + 1) * 96], qn0[:, csl], True, False,
                       [r_wa, r_q[1][c]], [r_ps[mb]], sig=False)
                    MM(ps[mb][0:96, 0:cw], wuq[0:64, 1, h * 96:(h + 1) * 96], qn1[0:64, csl], False, True,
                       [r_wa, r_k[1][c]], [r_ps[mb]])
                    mbs = misc.next()
                    MM(ps[mbs][0:96, 0:cw], wuqs[:, 0, h * 96:(h + 1) * 96], qn0[:, csl], True, False,
                       [r_wa, r_q[1][c]], [r_ps[mbs]], sig=False)
                    MM(ps[mbs][0:96, 0:cw], wuqs[0:64, 1, h * 96:(h + 1) * 96], qn1[0:64, csl], False, True,
                       [r_wa, r_k[1][c]], [r_ps[mbs]])
                    CP("dve", qa_[0:64, :], ps[mb][0:64, 0:cw], [r_ps[mb]], [rq_])
                    ti = load_tab(tab_mla, c, cw, rows=R64)
                    rope_evac(mb, mbs, ti, R64, cw, qa_[64:96, :], [rq_])
                    mbk = misc.next()
                    MM(ps[mbk][0:64, 0:cw], wukv[:, h * 64:(h + 1) * 64], lat[:, csl], True, True,
                       [r_wa, r_lat[c]], [r_ps[mbk]])
                    CP("dve", ka_[0:64, :], ps[mbk][0:64, 0:cw], [r_ps[mbk]], [rk_])

            groups = []
            for h in range(4):
                slot = h % nslots

                def qf(c, slot=slot):
                    a_, r_ = qs_(slot, c)
                    return a_[0:96, :], [r_]

                def kf(kt, slot=slot):
                    a_, r_ = ks_(slot, kt * 128 // cw)
                    o_ = (kt * 128) % cw
                    return a_[0:96, o_:o_ + 128], [r_]

                def vf(kt, h=h):
                    return vaug[:, kt, h * 65:(h + 1) * 65], [r_v[kt]]
                u = dict(q=qf, k=kf, v=vf, scale=96 ** -0.5)
                pre = None
                if nslots == 2:
                    if h + 1 < 4:
                        pre = (lambda hh=h + 1: head_proj(hh))
                elif h >= 1:
                    pre = (lambda hh=h: head_proj(hh))
                groups.append(dict(units=[u], epilogue=std_epilogue(s, S, 0, 384, h), pre=pre))
            head_proj(0)
            run_attention(S, groups)

        def mixer_C(l, s):
            S = seq_lens[s]
            cw = min(512, S)
            NC = S // cw
            P.wait_all("pool", [r_wmix])
            load_cast(l, OFF["xc"], 256, 0)
            load_cast(l, OFF["gc"], 256, 256)
            xcv = vaug[:].rearrange("p t d -> p (t d)").bitcast(F32)
            xconv = kbuf[:].rearrange("p a s -> p (a s)").bitcast(F32)
            hsum = qbuf[:].rearrange("p a s -> p (a s)").bitcast(F32)
            xcb = lat[:, :]
            allr = [r for rr in (r_lat, r_v) for r in rr] + [r for sl in (r_q, r_k) for rr in sl for r in rr] + r_qz + r_pt
            r_xc, r_xconv, r_hsum, r_xcb = Res("xc"), Res("xconv"), Res("hsum"), Res("xcb")
            qzf = qz[:].rearrange("p a n -> p (a n)").bitcast(F32)
            ptf = ptbuf[:].rearrange("p a n -> p (a n)").bitcast(F32)
            tmpfL = list(tmpf) + [qzf[:, 0:512], qzf[:, 512:1024], ptf[:, 0:512], ptf[:, 512:1024]]
            r_extra = [Res(f"lrux{i_}") for i_ in range(4)]
            r_tmpfL = list(r_tmpf) + r_extra
            tmpfL_ring = Ring(list(range(8)))
            for e_ in ("dve", "pool", "act", "pe"):
                P.wait_all(e_, allr)
            for cc in range(2):
                MEMSET("dve", xcv[:, 0:2], 0.0, [r_xc])
                MEMSET("dve", xcv[:, S + 2:S + 4], 0.0, [r_xc])
                for c in range(NC):
                    mb = proj_fm(S, c, cc * 128)
                    if c % 2 == 0:
                        CP("dve", xcv[:, 2 + c * cw:2 + (c + 1) * cw], ps[mb][:, 0:cw], [r_ps[mb]], [r_xc])
                    else:
                        ACT(xcv[:, 2 + c * cw:2 + (c + 1) * cw], ps[mb][:, 0:cw], AF.Copy, [r_ps[mb]], [r_xc])
                cwc = lambda j: pc[:, PC_CW + cc * 4 + j:PC_CW + cc * 4 + j + 1]
                for c in range(NC):
                    sl = slice(c * cw, (c + 1) * cw)
                    TS("dve", xconv[:, sl], xcv[:, c * cw:(c + 1) * cw], cwc(0), pc[:, PC_CB + cc:PC_CB + cc + 1], ALU.mult, ALU.add,
                       [r_xc, r_pc], [r_xconv])
                    for j in range(1, 4):
                        STT(xconv[:, sl], xcv[:, j + c * cw:j + (c + 1) * cw], cwc(j), xconv[:, sl], ALU.mult, ALU.add,
                            [r_xc, r_pc, r_xconv], [r_xconv])
                    ACT(xcb[:, sl], xconv[:, sl], AF.Copy, [r_xconv], [r_xcb])
                for d_ in range(2):
                    order = range(NC) if d_ == 0 else range(NC - 1, -1, -1)
                    prev = None
                    for c in order:
                        sl = slice(c * cw, (c + 1) * cw)
                        mba = misc.next()
                        MM(ps[mba][:, 0:cw], lruw[:, 0, d_, cc, :], xcb[:, sl], True, True, [r_wa, r_xcb], [r_ps[mba]])
                        mbx = misc.next()
                        MM(ps[mbx][:, 0:cw], lruw[:, 1, d_, cc, :], xcb[:, sl], True, True, [r_wa, r_xcb], [r_ps[mbx]])
                        ra, ri = tmpfL_ring.next(), tmpfL_ring.next()
                        ci = d_ * 2 + cc
                        ACT(tmpfL[ra][:, 0:cw], ps[mba][:, 0:cw], AF.Exp, [r_ps[mba], r_wa], [r_tmpfL[ra]], scale=-1.0, bias=lsp[:, 4 + ci:5 + ci])
                        ACT(tmpfL[ri][:, 0:cw], ps[mbx][:, 0:cw], AF.Exp, [r_ps[mbx], r_wa], [r_tmpfL[ri]], scale=-1.0, bias=lsp[:, 8 + ci:9 + ci])
                        ACT(tmpfL[ra][:, 0:cw], tmpfL[ra][:, 0:cw], AF.Ln, [r_tmpfL[ra]], [r_tmpfL[ra]], bias=1.0)
                        ACT(tmpfL[ri][:, 0:cw], tmpfL[ri][:, 0:cw], AF.Ln, [r_tmpfL[ri]], [r_tmpfL[ri]], bias=1.0)
                        ACT(tmpfL[ra][:, 0:cw], tmpfL[ra][:, 0:cw], AF.Exp, [r_tmpfL[ra]], [r_tmpfL[ra]], scale=-1.0)
                        ACT(tmpfL[ri][:, 0:cw], tmpfL[ri][:, 0:cw], AF.Exp, [r_tmpfL[ri]], [r_tmpfL[ri]], scale=-1.0)
                        ACT(tmpfL[ra][:, 0:cw], tmpfL[ra][:, 0:cw], AF.Exp, [r_tmpfL[ra], r_wa], [r_tmpfL[ra]], scale=lsp[:, ci:ci + 1])
                        rg = tmpfL_ring.next()
                        TT("dve", tmpfL[rg][:, 0:cw], tmpfL[ra][:, 0:cw], tmpfL[ra][:, 0:cw], ALU.mult, [r_tmpfL[ra]], [r_tmpfL[rg]])
                        TS("dve", tmpfL[rg][:, 0:cw], tmpfL[rg][:, 0:cw], -1.0, 1.0, ALU.mult, ALU.add, [r_tmpfL[rg]], [r_tmpfL[rg]])
                        ACT(tmpfL[rg][:, 0:cw], tmpfL[rg][:, 0:cw], AF.Ln, [r_tmpfL[rg]], [r_tmpfL[rg]])
                        ACT(tmpfL[rg][:, 0:cw], tmpfL[rg][:, 0:cw], AF.Exp, [r_tmpfL[rg]], [r_tmpfL[rg]], scale=0.5)
                        TT("pool", tmpfL[ri][:, 0:cw], tmpfL[ri][:, 0:cw], xconv[:, sl], ALU.mult, [r_tmpfL[ri], r_xconv], [r_tmpfL[ri]])
                        TT("dve", tmpfL[rg][:, 0:cw], tmpfL[rg][:, 0:cw], tmpfL[ri][:, 0:cw], ALU.mult, [r_tmpfL[rg], r_tmpfL[ri]], [r_tmpfL[rg]])
                        init = 0.0 if prev is None else prev
                        if d_ == 0:
                            P.op("dve", lambda e, ra=ra, rg=rg, init=init, ri=ri: e.tensor_tensor_scan(
                                out=tmpfL[ri][:, 0:cw], data0=tmpfL[ra][:, 0:cw], data1=tmpfL[rg][:, 0:cw], initial=init,
                                op0=ALU.mult, op1=ALU.add), reads=[r_tmpfL[ra], r_tmpfL[rg], r_small], writes=[r_tmpfL[ri]])
                            CP("dve", small[:, 40:41], tmpfL[ri][:, cw - 1:cw], [r_tmpfL[ri]], [r_small])
                            prev = small[:, 40:41]
                            CP("pool", hsum[:, sl], tmpfL[ri][:, 0:cw], [r_tmpfL[ri]], [r_hsum])
                        else:
                            P.op("dve", lambda e, ra=ra, rg=rg, init=init, ri=ri: e.tensor_tensor_scan(
                                out=tmpfL[ri][:, 0:cw][:, ::-1], data0=tmpfL[ra][:, 0:cw][:, ::-1], data1=tmpfL[rg][:, 0:cw][:, ::-1],
                                initial=init, op0=ALU.mult, op1=ALU.add), reads=[r_tmpfL[ra], r_tmpfL[rg], r_small], writes=[r_tmpfL[ri]])
                            CP("dve", small[:, 41:42], tmpfL[ri][:, 0:1], [r_tmpfL[ri]], [r_small])
                            prev = small[:, 41:42]
                            TT("pool", hsum[:, sl], hsum[:, sl], tmpfL[ri][:, 0:cw], ALU.add, [r_hsum, r_tmpfL[ri]], [r_hsum])
                for c in range(NC):
                    sl = slice(c * cw, (c + 1) * cw)
                    gb = proj_fm(S, c, 256 + cc * 128)
                    a = tmpfL_ring.next()
                    ACT(tmpfL[a][:, 0:cw], ps[gb][:, 0:cw], AF.Exp, [r_ps[gb]], [r_tmpfL[a]], scale=-1.0)
                    ACT(tmpfL[a][:, 0:cw], tmpfL[a][:, 0:cw], AF.Ln, [r_tmpfL[a]], [r_tmpfL[a]], bias=1.0)
                    ACT(tmpfL[a][:, 0:cw], tmpfL[a][:, 0:cw], AF.Exp, [r_tmpfL[a]], [r_tmpfL[a]], scale=-1.0)
                    TT("dve", tmpfL[a][:, 0:cw], ps[gb][:, 0:cw], tmpfL[a][:, 0:cw], ALU.mult, [r_ps[gb], r_tmpfL[a]], [r_tmpfL[a]])
                    b = tmpb_ring.next()
                    TT("dve", tmpb[b][:, 0:cw], tmpfL[a][:, 0:cw], hsum[:, sl], ALU.mult, [r_tmpfL[a], r_hsum], [r_tmpb[b]])
                    P.dma("pool", f"oc{b}", ocscr[s][cc * 128:(cc + 1) * 128, sl], tmpb[b][:, 0:cw], reads=[r_tmpb[b]], writes=[r_ocscr[s]])
            MEMSET("pool", vaug[:], 1.0, [r_xcb, r_xc, r_xconv, r_hsum])
            for e_ in ("dve", "pool", "act", "pe"):
                P.wait_all(e_, [r_xcb, r_xc, r_xconv, r_hsum] + r_extra)

        def phase3(l, s, last):
            S = seq_lens[s]
            P.wait_all("pool", [r_wmix])
            for k in range(8):
                i = wst_ring.next()
                LOAD(f"wst{i}", wst[i][:, 0:1024], w_out[l][k * 128:(k + 1) * 128, :], [], [r_wst[i]])
                TT("dve", wmix[:, k, 0:1024], wst[i][:, 0:1024], gate_bc[:, :], ALU.mult, [r_wst[i], r_gate], [r_wmix])
            src = x_in[s] if l == 0 else xres[s]
            misc.items = [6, 7, 0, 1, 2, 3]
            if last:
                LOAD("c3", fn_bc, fnorm.partition_broadcast(128), [], [r_mod])
            for t in range(S // 128):
                jb = t % 2
                ot = tmpf[jb][:].bitcast(BF16)[:, 0:768]
                ofm = tmpf[2 + jb][:].bitcast(BF16).rearrange("p (k n) -> p k n", n=128)
                r_ot, r_ofm = r_tmpf[jb], r_tmpf[2 + jb]
                P.dma("sp", f"otl{jb}", ot, oscr[s][t * 128:(t + 1) * 128, :], reads=[r_oscr[s][t]], writes=[r_ot])
                P.dma("sp", f"ofl{jb}", ofm[:, 4:6, :], ocscr[s][:, t * 128:(t + 1) * 128].rearrange("(c p) n -> p c n", p=128),
                      reads=[r_ocscr[s]], writes=[r_ofm])
                mb = misc.next()
                pT = ps[mb][:].bitcast(BF16)
                for j in range(6):
                    TR(pT[:, j * 128:(j + 1) * 128], ot[:, j * 128:(j + 1) * 128], ident_b[:], [r_ot, r_const], [r_ps[mb]], sig=(j == 5))
                CP("dve", ofm[:, 0:4, :], pT[:, 0:512].rearrange("p (j n) -> p j n", n=128), [r_ps[mb]], [r_ofm])
                CP("dve", ofm[:, 6:8, :], pT[:, 512:768].rearrange("p (j n) -> p j n", n=128), [r_ps[mb]], [r_ofm])
                xi = xt_ring.next()
                rd = [] if l == 0 else [r_xres[s][t]]
                LOAD(f"xt{xi}", xt[xi][:], src[t * 128:(t + 1) * 128, :], rd, [r_xt[xi]])
                for n in range(2):
                    yb = misc.next()
                    for k in range(8):
                        MM(ps[yb][:, :], ofm[:, k, :], wmix[:, k, n * 512:(n + 1) * 512], k == 0, k == 7, [r_ofm, r_wmix], [r_ps[yb]], sig=(k == 7))
                    TT("dve", xt[xi][:, n * 512:(n + 1) * 512], xt[xi][:, n * 512:(n + 1) * 512], ps[yb][:, :], ALU.add,
                       [r_ps[yb], r_xt[xi]], [r_xt[xi]])
                if not last:
                    P.dma("pool", f"xst{xi}", xres[s][t * 128:(t + 1) * 128, :], xt[xi][:], reads=[r_xt[xi]], writes=[r_xres[s][t]])
                else:
                    ACT(xn[:], xt[xi][:], AF.Square, [r_xt[xi]], [r_pt[0], r_pt[1], r_small], accum=small[:, 17:18])
                    rstd_act(small[:, 17:18], small[:, 17:18], float(D), [r_small], [r_small])
                    STT(xt[xi][:], xt[xi][:], small[:, 17:18], fn_bc, ALU.mult, ALU.mult, [r_xt[xi], r_small, r_mod], [r_xt[xi]])
                    P.dma("pool", f"xst{xi}", y_out[s][t * 128:(t + 1) * 128, :], xt[xi][:], reads=[r_xt[xi]], writes=[r_y])
            misc.items = [6, 7]

        try:
          for l in range(depth):
            layer_prep(l)
            for s in range(nseq):
                phase0(l, s)
                phase1(l, s)
                if _os.environ.get("KSTOP") == "p1":
                    raise StopIteration
                if "A" in mixers:
                    mixer_A(l, s)
                if "B" in mixers:
                    mixer_B(l, s)
                if "C" in mixers:
                    mixer_C(l, s)
                if "D" in mixers:
                    mixer_D(l, s)
                phase3(l, s, l == depth - 1)
        except StopIteration:
            pass
        allres = [r_y] + [r for rr in r_xres for r in rr] + [r for rr in r_oscr for r in rr] + r_ocscr
        P.wait_all("sp", allres)
        P.emit()
    return nc, P


def _rope_tables(Smax):
    pos = np.arange(Smax, dtype=np.float32)

    def cs(p, theta, half):
        inv = np.power(np.float32(theta), -np.arange(half, dtype=np.float32) / np.float32(half)).astype(np.float32)
        ang = (p[:, None] * inv[None, :]).astype(np.float32)
        return np.cos(ang).astype(np.float32).T, np.sin(ang).astype(np.float32).T

    c, s_ = cs(pos, 10000.0, 16)
    tab_mla = np.zeros((2, 32, Smax), np.float32)
    tab_mla[0, 0:16] = c
    tab_mla[0, 16:32] = c
    tab_mla[1, 0:16] = -s_
    tab_mla[1, 16:32] = s_
    c, s_ = cs(pos, 500000.0, 4)
    blk_c = np.ones((32, Smax), np.float32)
    blk_s = np.zeros((32, Smax), np.float32)
    blk_c[0:4] = c
    blk_c[4:8] = c
    blk_s[0:4] = -s_
    blk_s[4:8] = s_
    tab_diff = np.stack([np.tile(blk_c, (4, 1)), np.tile(blk_s, (4, 1))])
    row = np.floor(pos / 64.0).astype(np.float32)
    col = (pos - row * 64.0).astype(np.float32)
    cr, sr = cs(row, 10000.0, 16)
    cc, sc_ = cs(col, 10000.0, 16)
    blk_c = np.concatenate([cr, cr, cc, cc], 0)
    blk_s = np.concatenate([-sr, sr, -sc_, sc_], 0)
    tab_gqa = np.stack([np.tile(blk_c, (2, 1)), np.tile(blk_s, (2, 1))])
    return tab_mla, np.ascontiguousarray(tab_diff), np.ascontiguousarray(tab_gqa)


def _host_layout(inp, depth):
    f = np.float32
    pcol = np.zeros((depth, 128, NPC), f)
    prow = np.zeros((depth, NPR), f)
    p = np.arange(128)
    partner = np.where((p % 32) < 16, p + 16, p - 16) % 64
    for l in range(depth):
        pcol[l, :, PC_NG:PC_NG + 8] = inp["norm_g"][l].reshape(8, 128).T
        pcol[l, :, PC_ADAB:PC_ADAB + 24] = inp["ada_b"][l].reshape(24, 128).T
        pcol[l, :, PC_QN] = inp["mla_q_norm"][l][0:128]
        pcol[l, 0:64, PC_QN + 1] = inp["mla_q_norm"][l][128:192]
        pcol[l, :, PC_KVN] = inp["mla_kv_norm"][l]
        pcol[l, :, PC_GQ] = inp["gqa_q_norm"][l][p % 64]
        pcol[l, :, PC_GK] = inp["gqa_k_norm"][l][p % 64]
        pcol[l, :, PC_GQS] = inp["gqa_q_norm"][l][partner]
        pcol[l, :, PC_GKS] = inp["gqa_k_norm"][l][partner]
        pcol[l, :, PC_CB:PC_CB + 2] = inp["lru_conv_b"][l].reshape(2, 128).T
        pcol[l, :, PC_CW:PC_CW + 8] = inp["lru_conv_w"][l].reshape(4, 2, 128).transpose(2, 1, 0).reshape(128, 8)
        pcol[l, :, PC_BA:PC_BA + 4] = inp["lru_ba"][l].reshape(2, 2, 128).transpose(2, 0, 1).reshape(128, 4)
        pcol[l, :, PC_BX:PC_BX + 4] = inp["lru_bx"][l].reshape(2, 2, 128).transpose(2, 0, 1).reshape(128, 4)
        pcol[l, :, PC_LAM:PC_LAM + 4] = inp["lru_lambda"][l].reshape(2, 2, 128).transpose(2, 0, 1).reshape(128, 4)
        prow[l, PR_LAM:PR_LAM + 128] = inp["diff_lambda"][l].reshape(-1)
        prow[l, PR_SUB:PR_SUB + 64] = inp["diff_subln"][l]
    lru_w = np.ascontiguousarray(np.stack([inp["lru_wa"], inp["lru_wx"]], axis=1)).astype(f)
    return pcol, prow, lru_w


_CACHE = {}


def run(inputs, seqs_per_core, n_cores, mixers="ABCD"):
    depth = inputs["w_in"].shape[0]
    f = np.float32
    xs = {"p": np.asarray(inputs["x_prompt"], f), "s": np.asarray(inputs["x_sample"], f)}
    cs_ = {"p": np.asarray(inputs["c_prompt"], f), "s": np.asarray(inputs["c_sample"], f)}
    seq_lens = [xs[w].shape[1] for (w, _) in seqs_per_core[0]]
    Smax = max(seq_lens)
    key = (tuple(seq_lens), depth, mixers)
    if key not in _CACHE:
        _CACHE[key] = build(seq_lens, depth, mixers)
    nc, _ = _CACHE[key]
    pcol, prow, lru_w = _host_layout({k: np.asarray(v, f) for k, v in inputs.items()}, depth)
    tab_mla, tab_diff, tab_gqa = _rope_tables(Smax)
    shared = {
        "ada_w": np.asarray(inputs["ada_w"], f), "ada_b": np.asarray(inputs["ada_b"], f),
        "w_in": np.asarray(inputs["w_in"], f), "w_out": np.asarray(inputs["w_out"], f),
        "mla_w_uq": np.asarray(inputs["mla_w_uq"], f), "mla_w_ukv": np.asarray(inputs["mla_w_ukv"], f),
        "lru_w": lru_w, "pcol": pcol, "prow": prow, "final_norm": np.asarray(inputs["final_norm"], f).reshape(1, -1),
        "ident_b": np.eye(128, dtype=f).astype(ml_dtypes.bfloat16), "ident_f": np.eye(128, dtype=f),
        "tab_mla": tab_mla, "tab_diff": tab_diff, "tab_gqa": tab_gqa,
    }
    in_maps = []
    for core in range(n_cores):
        m = dict(shared)
        cf = np.zeros((len(seq_lens), 128, 8), f)
        for i, (w, idx) in enumerate(seqs_per_core[core]):
            m[f"x{i}"] = np.ascontiguousarray(xs[w][idx])
            cf[i] = cs_[w][idx].reshape(8, 128).T
        m["cfm"] = cf
        in_maps.append(m)
    res = run_bass_kernel_spmd(nc, in_maps, core_ids=list(range(n_cores)))
    yp = np.zeros_like(xs["p"])
    ys = np.zeros_like(xs["s"])
    out = {"p": yp, "s": ys}
    for core in range(n_cores):
        for i, (w, idx) in enumerate(seqs_per_core[core]):
            out[w][idx] = res.results[core][f"y{i}"]
    return yp, ys


def kernel(x_prompt, x_sample, c_prompt, c_sample, **weights):
    inputs = dict(x_prompt=x_prompt, x_sample=x_sample, c_prompt=c_prompt, c_sample=c_sample, **weights)
    n_cores = 8
    bp = x_prompt.shape[0] // n_cores
    bs = x_sample.shape[0] // n_cores
    spc = [[("p", c * bp + j) for j in range(bp)] + [("s", c * bs + j) for j in range(bs)] for c in range(n_cores)]
    return run(inputs, spc, n_cores)
```

```python
import contextlib
import math
import numpy as np
import ml_dtypes
import concourse.bass as bass
import concourse.mybir as mybir
from concourse.bass_utils import run_bass_kernel_spmd

F32 = mybir.dt.float32
BF16 = mybir.dt.bfloat16
AF = mybir.ActivationFunctionType
ALU = mybir.AluOpType
AX = mybir.AxisListType

D = 1024
NIN = 2912
EPS = 1e-6
OFF = dict(qa=0, kva=192, kra=320, ga=352, qb=608, kb=864, vb=1120, gb=1376,
           xc=1632, gc=1888, qd=2144, kd=2400, vd=2528, gd=2656)
NPC = 64
PC_NG = 0
PC_ADAB = 8
PC_QN = 32
PC_KVN = 34
PC_GQ = 35
PC_GK = 36
PC_GQS = 37
PC_GKS = 38
PC_CB = 39
PC_CW = 41
PC_BA = 49
PC_BX = 53
PC_LAM = 57
PR_LAM = 0
PR_SUB = 128
NPR = 192


class Res:
    __slots__ = ("name", "w", "r", "excl")

    def __init__(self, name, excl=False):
        self.name = name
        self.w = {}
        self.r = {}
        self.excl = excl


class Prog:
    ENGS = ("pe", "act", "dve", "pool", "sp")

    def __init__(self, nc):
        self.nc = nc
        self.count = {e: 0 for e in self.ENGS}
        self.waited = {e: {} for e in self.ENGS}
        self.stream = {e: [] for e in self.ENGS}
        self.dma_count = {}
        self.n_ins = 0

    def _deps(self, eng, reads, writes):
        need = {}

        def add(s, v, same_ok):
            if s == eng and eng in ("pe", "sp"):
                return
            if s in self.dma_count:
                v = self.dma_count[s]
            if need.get(s, 0) < v:
                need[s] = v
        for r in reads:
            for s, v in r.w.items():
                add(s, v, False)
            if r.excl:
                for s, v in r.r.items():
                    if s != eng:
                        add(s, v, False)
        for w in writes:
            for s, v in w.w.items():
                add(s, v, True)
            for s, v in w.r.items():
                add(s, v, True)
        out = []
        for s, v in need.items():
            if self.waited[eng].get(s, 0) >= v:
                continue
            self.waited[eng][s] = v
            out.append((s, v))
        return out

    def _commit(self, tok, reads, writes):
        s, v = tok
        for r in reads:
            if r.r.get(s, 0) < v:
                r.r[s] = v
        for w in writes:
            w.w[s] = v
            w.r = {}

    def op(self, eng, fn, reads=(), writes=(), signal=True):
        waits = self._deps(eng, reads, writes)
        if signal:
            self.count[eng] += 1
            tok = (eng, self.count[eng])
        else:
            tok = (eng, self.count[eng] + 1)
        self._commit(tok, reads, writes)
        self.stream[eng].append((waits, fn, eng if signal else None, 1))
        self.n_ins += 1

    def dma(self, queue, key, out, in_, reads=(), writes=(), **kw):
        waits = self._deps(queue, reads, writes)
        self.dma_count[key] = self.dma_count.get(key, 0) + 16
        tok = (key, self.dma_count[key])
        self._commit(tok, reads, writes)

        def fn(e, out=out, in_=in_, kw=kw):
            return e.dma_start(out=out, in_=in_, **kw)
        self.stream[queue].append((waits, fn, key, 16))
        self.n_ins += 1

    def wait_all(self, eng, res_list):
        waits = self._deps(eng, (), res_list)
        self.stream[eng].append((waits, None, None, 0))

    def emit(self):
        nc = self.nc
        with contextlib.ExitStack() as st:
            sems = {}
            for e in self.ENGS:
                sems[e] = st.enter_context(nc.semaphore("s_" + e))
            for k in self.dma_count:
                sems[k] = st.enter_context(nc.semaphore("d_" + k))
            block = st.enter_context(nc.Block())

            def replay(ename):
                def body(e):
                    for waits, fn, sig, inc in self.stream[ename]:
                        for s, v in waits:
                            e.wait_ge(sems[s], v)
                        if fn is None:
                            continue
                        ins = fn(e)
                        if sig is not None:
                            ins.then_inc(sems[sig], inc)
                return body

            block.tensor(replay("pe"))
            block.scalar(replay("act"))
            block.vector(replay("dve"))
            block.gpsimd(replay("pool"))
            block.sync(replay("sp"))


class Ring:
    def __init__(self, items):
        self.items = items
        self.i = 0

    def next(self):
        it = self.items[self.i % len(self.items)]
        self.i += 1
        return it


def build(seq_lens, depth, mixers="ABCD"):
    nc = bass.Bass("TRN2", target_bir_lowering=False)
    nseq = len(seq_lens)
    Smax = max(seq_lens)
    NTmax = Smax // 128

    def din(name, shape, dt=F32):
        return nc.dram_tensor(name, list(shape), dt, kind="ExternalInput").ap()

    x_in = [din(f"x{i}", [S, D]) for i, S in enumerate(seq_lens)]
    y_out = [nc.dram_tensor(f"y{i}", [S, D], F32, kind="ExternalOutput").ap() for i, S in enumerate(seq_lens)]
    cfm = din("cfm", [nseq, 128, 8])
    ada_w = din("ada_w", [depth, D, 3 * D])
    ada_b = din("ada_b", [depth, 3 * D])
    w_in = din("w_in", [depth, D, NIN])
    w_out = din("w_out", [depth, D, D])
    w_uq = din("mla_w_uq", [depth, 192, 384])
    w_ukv = din("mla_w_ukv", [depth, 128, 512])
    lru_w = din("lru_w", [depth, 2, 2, 4, 64, 64])
    pcol = din("pcol", [depth, 128, NPC])
    prow = din("prow", [depth, NPR])
    fnorm = din("final_norm", [1, D])
    ident_b_d = din("ident_b", [128, 128], BF16)
    ident_f_d = din("ident_f", [128, 128])
    tab_mla = din("tab_mla", [2, 32, Smax])
    tab_diff = din("tab_diff", [2, 128, Smax])
    tab_gqa = din("tab_gqa", [2, 128, Smax])

    xres = [nc.dram_tensor(f"xres{i}", [S, D], F32).ap() for i, S in enumerate(seq_lens)]
    oscr = [nc.dram_tensor(f"oscr{i}", [S, 768], BF16).ap() for i, S in enumerate(seq_lens)]
    ocscr = [nc.dram_tensor(f"ocscr{i}", [256, S], BF16).ap() for i, S in enumerate(seq_lens)]

    P = Prog(nc)
    st = contextlib.ExitStack()
    with st:
        sb_bytes = [0]

        def sb(name, shape, dt=F32):
            n = 1
            for d_ in shape[1:]:
                n *= d_
            sb_bytes[0] += n * (2 if dt == BF16 else 4)
            return st.enter_context(nc.sbuf_tensor(name, list(shape), dt))

        ident_b = sb("ident_b_s", [128, 128], BF16)
        ident_f = sb("ident_f_s", [128, 65])
        ones_b = sb("ones_b", [128, 128], BF16)
        bones_b = sb("bones_b", [128, 128], BF16)
        epsc = sb("epsc", [128, 1])
        h_fm = sb("h_fm", [128, 8, Smax], BF16)
        wmix = sb("wmix", [128, 8, 1536], BF16)
        wst = [sb(f"wst{i}", [128, 1024]) for i in range(2)]
        qbuf = sb("qbuf", [128, 2, Smax], BF16)
        kbuf = sb("kbuf", [128, 2, Smax], BF16)
        lat = sb("lat", [128, Smax], BF16)
        vaug = sb("vaug", [128, NTmax, 260], BF16)
        xt = [sb(f"xt{i}", [128, 1024]) for i in range(2)]
        tabc = [sb(f"tabc{i}", [128, 512]) for i in range(1)]
        tabs = [sb(f"tabs{i}", [128, 512]) for i in range(1)]
        ptbuf = sb("ptbuf", [128, 4, 512], BF16)
        qz = sb("qz", [128, 4, 512], BF16)
        pts = [ptbuf[:, i, :] for i in range(4)]
        xn = ptbuf[:, 0:2, :].rearrange("p a n -> p (a n)")
        tmpf = [sb(f"tmpf{i}", [128, 512]) for i in range(4)]
        tmpb = [sb(f"tmpb{i}", [128, 512], BF16) for i in range(3)]
        tmpo = [sb(f"tmpo{i}", [128, 512]) for i in range(2)]
        tmpg = [sb(f"tmpg{i}", [128, 256]) for i in range(2)]
        ogs = [sb(f"og{i}", [128, 256], BF16) for i in range(2)]
        pc = sb("pc", [128, NPC])
        pr = sb("pr", [128, NPR])
        sc = sb("sc", [128, 8])
        sc_bc = sb("sc_bc", [128, 8, 128])
        fn_bc = sc_bc[:].rearrange("p k n -> p (k n)")
        shiftcol = sb("shiftcol", [128, 8])
        gscol = sb("gscol", [128, 8])
        gate_bc = sb("gate_bc", [128, 1024])
        small = sb("small", [128, 64])
        wuq = sb("wuq", [128, 2, 384], BF16)
        wuqs = sb("wuqs", [128, 2, 384], BF16)
        wukv = sb("wukv", [128, 512], BF16)
        lamc = sb("lamc", [128, 4])
        subln_bc = sb("subln_bc", [128, 4, 64])
        lruw = sb("lruw", [128, 2, 2, 2, 128], BF16)
        lsp = sb("lsp", [128, 12])

        import os as _os0
        if _os0.environ.get("KDEBUG"):
            print("SBUF bytes/partition:", sb_bytes[0])
        ps = [st.enter_context(nc.psum_tensor(f"ps{i}", [128, 512], F32)) for i in range(8)]
        r_ps = [Res(f"ps{i}", excl=True) for i in range(8)]
        SB_BANKS = (0, 1, 2, 3)
        O_BANKS = (4, 5)
        O_ALL = (4, 5, 6, 7)
        misc = Ring([6, 7])

        r_const = Res("const")
        r_h = [Res(f"h{t}") for t in range(NTmax)]
        r_wmix = Res("wmix")
        r_wst = [Res("wst0"), Res("wst1")]
        wst_ring = Ring([0, 1])
        NCmax = Smax // 512 if Smax >= 512 else 1
        r_q = [[Res(f"q{s}_{c}") for c in range(NTmax)] for s in range(2)]
        r_k = [[Res(f"k{s}_{c}") for c in range(NTmax)] for s in range(2)]
        r_lat = [Res(f"lat{c}") for c in range(NTmax)]
        r_v = [Res(f"v{c}") for c in range(NTmax)]
        r_xt = [Res("xt0"), Res("xt1")]
        xt_ring = Ring([0, 1])
        r_xn = Res("xn")
        r_tab = [Res("tab0"), Res("tab1")]
        tab_ring = Ring([0])
        r_pt = [Res(f"pt{i}") for i in range(4)]
        r_qz = [Res(f"qz{i}") for i in range(4)]
        r_tmpf = [Res(f"tmpf{i}") for i in range(4)]
        tmpf_ring = Ring([0, 1, 2, 3])
        r_tmpb = [Res(f"tmpb{i}") for i in range(3)]
        r_tmpo = [Res("tmpo0"), Res("tmpo1")]
        tmpo_ring = Ring([0, 1])
        r_tmpg = [Res("tmpg0"), Res("tmpg1")]
        tmpg_ring = Ring([0, 1])
        tmpb_ring = Ring([0, 1, 2])
        r_og = [Res("og0"), Res("og1")]
        og_ring = Ring([0, 1])
        r_pc = Res("pc")
        r_mod = Res("mod")
        r_gate = Res("gate")
        r_small = Res("small")
        rcol = [small[:, 44:45], small[:, 46:47]]
        r_rcol = [Res("rcol0"), Res("rcol1")]
        r_wa = Res("wa")
        r_ot = Res("ot")
        r_ofm = Res("ofm")
        r_oscr = [[Res(f"oscr{i}_{t}") for t in range(S // 128)] for i, S in enumerate(seq_lens)]
        r_ocscr = [Res(f"ocscr{i}") for i in range(nseq)]
        r_xres = [[Res(f"xres{i}_{t}") for t in range(S // 128)] for i, S in enumerate(seq_lens)]
        r_y = Res("y")

        def MM(out, lhsT, rhs, start, stop, rd, wr, sig=True, **kw):
            P.op("pe", lambda e: e.matmul(out, lhsT=lhsT, rhs=rhs, start=start, stop=stop, **kw),
                 reads=rd, writes=wr, signal=sig)

        def TR(out, in_, ident, rd, wr, sig=True):
            P.op("pe", lambda e: e.transpose(out, in_, ident), reads=rd, writes=wr, signal=sig)

        def ACT(out, in_, func, rd, wr, scale=1.0, bias=None, accum=None):
            def fn(e):
                kw = {}
                if bias is not None:
                    kw["bias"] = bias
                if accum is not None:
                    kw["accum_out"] = accum
                return e.activation(out=out, in_=in_, func=func, scale=scale, **kw)
            P.op("act", fn, reads=rd, writes=wr)

        def eng_of(P_, name):
            return name

        def TT(eng, out, in0, in1, op, rd, wr):
            P.op(eng, lambda e: e.tensor_tensor(out=out, in0=in0, in1=in1, op=op), reads=rd, writes=wr)

        def TS(eng, out, in0, s1, s2, op0, op1, rd, wr):
            if op1 is None:
                P.op(eng, lambda e: e.tensor_scalar(out=out, in0=in0, scalar1=s1, scalar2=None, op0=op0), reads=rd, writes=wr)
            else:
                P.op(eng, lambda e: e.tensor_scalar(out=out, in0=in0, scalar1=s1, scalar2=s2, op0=op0, op1=op1), reads=rd, writes=wr)

        def STT(out, in0, scalar, in1, op0, op1, rd, wr):
            P.op("dve", lambda e: e.scalar_tensor_tensor(out=out, in0=in0, scalar=scalar, in1=in1, op0=op0, op1=op1),
                 reads=rd, writes=wr)

        def CP(eng, out, in_, rd, wr):
            P.op(eng, lambda e: e.tensor_copy(out=out, in_=in_), reads=rd, writes=wr)

        def RCP(out, in_, rd, wr):
            P.op("dve", lambda e: e.reciprocal(out=out, in_=in_), reads=rd, writes=wr)

        def MEMSET(eng, ap, val, wr):
            P.op(eng, lambda e: e.memset(ap, val), writes=wr)

        dma_i = [0]
        import os as _os
        ROPE_ENG = _os.environ.get("ROPE_ENG", "dve")

        def LOAD(key, out, in_, rd, wr, **kw):
            P.dma("sp", key, out, in_, reads=rd, writes=wr, **kw)

        def rstd_act(out, in_, n, rd, wr):
            ACT(out, in_, AF.Ln, list(rd) + [r_const], wr, scale=1.0 / n, bias=epsc[0:out.shape[0], 0:1])
            ACT(out, out, AF.Exp, wr, wr, scale=-0.5)

        LOAD("c0", ident_b[:], ident_b_d, [], [r_const])
        LOAD("c0", ident_f[:], ident_f_d[:, 0:65], [], [r_const])
        MEMSET("pool", ones_b[:], 1.0, [r_const])
        MEMSET("pool", bones_b[:], 0.0, [r_const])
        MEMSET("pool", bones_b[0:64, 0:64], 1.0, [r_const])
        MEMSET("pool", bones_b[64:128, 64:128], 1.0, [r_const])
        MEMSET("pool", epsc[:], EPS, [r_const])
        MEMSET("pool", vaug[:], 1.0, r_v)
        MEMSET("pool", qz[:], 0.0, r_qz)

        if len(mixers) < 4:
            zt = sb("zt", [128, 768], BF16)
            r_zt = Res("zt")
            MEMSET("pool", zt[:], 0.0, [r_zt])
            for i_, S_ in enumerate(seq_lens):
                for t_ in range(S_ // 128):
                    P.dma("sp", "zst", oscr[i_][t_ * 128:(t_ + 1) * 128, :], zt[:], reads=[r_zt], writes=[r_oscr[i_][t_]])
                    for cc_ in range(2):
                        P.dma("sp", "zst", ocscr[i_][cc_ * 128:(cc_ + 1) * 128, t_ * 128:(t_ + 1) * 128], zt[:, 0:128], reads=[r_zt], writes=[r_ocscr[i_]])

        def load_w(src2d, ncols, dst_col, swap=None, rows=D, dst=None, gaincol=None):
            nk = rows // 128
            i = wst_ring.next()
            stg = wst[i][:, 0:nk * ncols].rearrange("p (k n) -> p k n", k=nk)
            LOAD(f"wst{i}", stg, src2d.rearrange("(k p) n -> p k n", p=128), [], [r_wst[i]])
            return i, stg

        cast_i = [0]

        def cast_cols(stg_ap, i, dst_ap, eng=None):
            cast_i[0] += 1
            if cast_i[0] % 2 == 0:
                ACT(dst_ap, stg_ap, AF.Copy, [r_wst[i]], [r_wmix])
            else:
                CP("dve", dst_ap, stg_ap, [r_wst[i]], [r_wmix])

        def load_cast(l, src_c0, n, dst_c0, extra=None):
            for p0 in range(0, n, 128):
                pn = min(128, n - p0)
                i, stg = load_w(w_in[l][:, src_c0 + p0:src_c0 + p0 + pn], pn, 0)
                cast_cols(stg, i, wmix[:, :, dst_c0 + p0:dst_c0 + p0 + pn])
                if extra is not None:
                    extra(i, stg, p0, pn)

        def swapped(ap3, half):
            return ap3.rearrange("p k (m b j) -> p k m b j", b=2, j=half)

        def layer_prep(l):
            LOAD("c1", pc[:], pcol[l], [r_pc, r_mod, r_wa], [r_pc])
            LOAD("c1", pr[:], prow[l:l + 1, :].partition_broadcast(128), [r_pc], [r_pc])
            if "A" in mixers:
                i = wst_ring.next()
                LOAD(f"wst{i}", wst[i][:, 0:384], w_uq[l][0:128, :], [], [r_wst[i]])
                LOAD(f"wst{i}", wst[i][0:64, 384:768], w_uq[l][128:192, :], [], [r_wst[i]])
                for kk, rows in ((0, 128), (1, 64)):
                    TS("pool", wuq[0:rows, kk, :], wst[i][0:rows, kk * 384:(kk + 1) * 384], pc[0:rows, PC_QN + kk:PC_QN + kk + 1], None,
                       ALU.mult, None, [r_wst[i], r_pc], [r_wa])
                    CP("pool", wuqs[0:rows, kk, :], wuq[0:rows, kk, :], [r_wa], [r_wa])
                    v = wuq[0:rows, kk, :].rearrange("p (h c) -> p h c", c=96)
                    vs = wuqs[0:rows, kk, :].rearrange("p (h c) -> p h c", c=96)
                    CP("pool", vs[:, :, 64:80], v[:, :, 80:96], [r_wa], [r_wa])
                    CP("pool", vs[:, :, 80:96], v[:, :, 64:80], [r_wa], [r_wa])
                i = wst_ring.next()
                LOAD(f"wst{i}", wst[i][:, 0:512], w_ukv[l], [], [r_wst[i]])
                sv = wst[i][:, 0:512].rearrange("p (h t c) -> p h t c", h=4, t=2)
                dv_ = wukv[:, :].rearrange("p (t h c) -> p t h c", t=2, h=4)
                for t_ in range(2):
                    TS("pool", dv_[:, t_], sv[:, :, t_, :], pc[:, PC_KVN:PC_KVN + 1], None, ALU.mult, None,
                       [r_wst[i], r_pc], [r_wa])
            if "B" in mixers:
                lam_init = 0.8 - 0.6 * math.exp(-0.3 * l)
                lv = pr[:, PR_LAM:PR_LAM + 128].rearrange("p (a d) -> p a d", a=4)
                TT("dve", small[:, 0:32], lv[:, 0, :], lv[:, 1, :], ALU.mult, [r_pc], [r_small])
                TT("dve", small[:, 32:64], lv[:, 2, :], lv[:, 3, :], ALU.mult, [r_pc], [r_small])
                P.op("dve", lambda e: e.tensor_reduce(out=lamc[:, 0:2], in_=small[:, 0:64].rearrange("p (a d) -> p a d", a=2),
                                                     axis=AX.X, op=ALU.add), reads=[r_small], writes=[r_wa])
                ACT(lamc[:, 0:2], lamc[:, 0:2], AF.Exp, [r_wa], [r_wa])
                TT("dve", lamc[:, 2:3], lamc[:, 0:1], lamc[:, 1:2], ALU.subtract, [r_wa], [r_wa])
                TS("dve", lamc[:, 3:4], lamc[:, 2:3], -1.0, -lam_init, ALU.mult, ALU.add, [r_wa], [r_wa])
                for qb in range(4):
                    TS("dve", subln_bc[:, qb, :], pr[:, PR_SUB:PR_SUB + 64], 1.0 - lam_init, None, ALU.mult, None,
                       [r_pc], [r_wa])
            if "C" in mixers:
                i = wst_ring.next()
                MEMSET("pool", wst[i][:, 0:1024], 0.0, [r_wst[i]])
                for m in range(2):
                    for d_ in range(2):
                        for hh in range(4):
                            cc, hb = hh // 2, (hh % 2) * 64
                            col = ((m * 2 + d_) * 2 + cc) * 128 + hb
                            LOAD(f"wst{i}", wst[i][hb:hb + 64, col:col + 64], lru_w[l, m, d_, hh], [], [r_wst[i]])
                CP("pool", lruw[:].rearrange("p m d c n -> p (m d c n)"), wst[i][:, 0:1024], [r_wst[i]], [r_wa])
                ACT(lsp[:, 0:4], pc[:, PC_LAM:PC_LAM + 4], AF.Exp, [r_pc], [r_wa], scale=-1.0)
                ACT(lsp[:, 0:4], lsp[:, 0:4], AF.Ln, [r_wa], [r_wa], bias=1.0)
                TS("dve", lsp[:, 0:4], lsp[:, 0:4], -8.0, None, ALU.mult, None, [r_wa], [r_wa])
                TS("dve", lsp[:, 4:8], pc[:, PC_BA:PC_BA + 4], -1.0, None, ALU.mult, None, [r_pc], [r_wa])
                TS("dve", lsp[:, 8:12], pc[:, PC_BX:PC_BX + 4], -1.0, None, ALU.mult, None, [r_pc], [r_wa])

        def phase0(l, s):
            LOAD("c2", sc[:], cfm[s], [r_mod], [r_mod])
            ACT(small[:, 8:16], sc[:], AF.Exp, [r_mod], [r_small], scale=-1.0)
            TS("dve", small[:, 8:16], small[:, 8:16], 1.0, None, ALU.add, None, [r_small], [r_small])
            RCP(small[:, 8:16], small[:, 8:16], [r_small], [r_small])
            TT("dve", sc[:], sc[:], small[:, 8:16], ALU.mult, [r_small, r_mod], [r_mod])
            CP("dve", sc_bc[:], sc[:, :].unsqueeze(2).broadcast_to([128, 8, 128]), [r_mod], [r_mod])
            mb = misc.next()
            for j in range(24):
                i, stg = load_w(ada_w[l][:, j * 128:(j + 1) * 128], 128, 0)
                if j < 16:
                    for k in range(8):
                        MM(ps[mb][:, j * 2:j * 2 + 2], stg[:, k, :], sc_bc[:, k, 0:2],
                           k == 0, k == 7, [r_wst[i], r_mod], [r_ps[mb]], sig=(k == 7))
                    if j == 15:
                        pv = ps[mb][:, 0:32].rearrange("p (j t) -> p j t", t=2)
                        TT("dve", shiftcol[:], pv[:, 0:8, 0], pc[:, PC_ADAB:PC_ADAB + 8], ALU.add, [r_ps[mb], r_pc], [r_mod])
                        TT("dve", gscol[:], pv[:, 8:16, 0], pc[:, PC_ADAB + 8:PC_ADAB + 16], ALU.add, [r_ps[mb], r_pc], [r_mod])
                        STT(gscol[:], gscol[:], 1.0, pc[:, PC_NG:PC_NG + 8], ALU.add, ALU.mult, [r_mod, r_pc], [r_mod])
                else:
                    gb = misc.next()
                    c0 = (j - 16) * 128
                    a = tmpf_ring.next()
                    LOAD(f"abg{a}", tmpf[a][:, 0:128], ada_b[l:l + 1, 2 * D + c0:2 * D + c0 + 128].partition_broadcast(128),
                         [], [r_tmpf[a]])
                    for k in range(8):
                        MM(ps[gb][:, 0:128], sc_bc[:, k, :], stg[:, k, :], k == 0, k == 7, [r_wst[i], r_mod], [r_ps[gb]], sig=(k == 7))
                    TT("dve", gate_bc[:, c0:c0 + 128], ps[gb][:, 0:128], tmpf[a][:, 0:128], ALU.add,
                       [r_ps[gb], r_tmpf[a]], [r_gate])

        def phase1(l, s):
            S = seq_lens[s]
            src = x_in[s] if l == 0 else xres[s]
            misc.items = [6, 7, 0, 1, 2, 3]
            for t in range(S // 128):
                xi = xt_ring.next()
                rd = [] if l == 0 else [r_xres[s][t]]
                LOAD(f"xt{xi}", xt[xi][:], src[t * 128:(t + 1) * 128, :], rd, [r_xt[xi]])
                rc, r_rc = rcol[t % 2], r_rcol[t % 2]
                xnb = ptbuf[:, 2 * (t % 2):2 * (t % 2) + 2, :].rearrange("p a n -> p (a n)")
                r_xnb = [r_pt[2 * (t % 2)], r_pt[2 * (t % 2) + 1]]
                ACT(tmpf[0][:].bitcast(BF16), xt[xi][:], AF.Square, [r_xt[xi]], [r_tmpf[0], r_rc], accum=rc)
                rstd_act(rc, rc, float(D), [r_rc], [r_rc])
                ACT(xnb, xt[xi][:], AF.Copy, [r_xt[xi], r_rc], r_xnb, scale=rc)
                mb = misc.next()
                pT = ps[mb][:].bitcast(BF16)
                for k in range(8):
                    TR(pT[:, k * 128:(k + 1) * 128], xnb[:, k * 128:(k + 1) * 128], ident_b[:], r_xnb + [r_const], [r_ps[mb]], sig=(k == 7))
                for k in range(8):
                    TS("dve", h_fm[:, k, t * 128:(t + 1) * 128], pT[:, k * 128:(k + 1) * 128], gscol[:, k:k + 1], shiftcol[:, k:k + 1],
                       ALU.mult, ALU.add, [r_ps[mb], r_mod], [r_h[t]])
            misc.items = [6, 7]

        def proj_fm(S, c, wcol, rows=128, wrows=None):
            cw = min(512, S)
            mb = misc.next()
            tl = [r_h[t] for t in range(c * cw // 128, (c + 1) * cw // 128)]
            for k in range(8):
                MM(ps[mb][0:rows, 0:cw], wmix[:, k, wcol:wcol + rows], h_fm[:, k, c * cw:(c + 1) * cw], k == 0, k == 7,
                   [r_wmix] + tl, [r_ps[mb]], sig=(k == 7))
            return mb

        def load_tab(tab, c, cw, rows=slice(0, 128), trows=None):
            ti = tab_ring.next()
            nrow = rows.stop - rows.start
            LOAD(f"tab{ti}", tabc[ti][rows, 0:cw], tab[0, 0:nrow, c * cw:(c + 1) * cw], [], [r_tab[ti]])
            LOAD(f"tab{ti}", tabs[ti][rows, 0:cw], tab[1, 0:nrow, c * cw:(c + 1) * cw], [], [r_tab[ti]])
            return ti

        def rope_evac(mb, mbs, ti, rows, cw, out_ap, wr, gcol=None, gscol_=None, rstd=None, rstd_rd=()):
            a = tmpf_ring.next()
            b = tmpf_ring.next()
            if gcol is None:
                TT("dve", tmpf[a][rows, 0:cw], ps[mb][rows, 0:cw], tabc[ti][rows, 0:cw], ALU.mult, [r_ps[mb], r_tab[ti]], [r_tmpf[a]])
                TT("dve", tmpf[b][rows, 0:cw], ps[mbs][rows, 0:cw], tabs[ti][rows, 0:cw], ALU.mult, [r_ps[mbs], r_tab[ti]], [r_tmpf[b]])
            else:
                TS("dve", tmpf[a][rows, 0:cw], ps[mb][rows, 0:cw], gcol, None, ALU.mult, None, [r_ps[mb], r_pc], [r_tmpf[a]])
                TT("dve", tmpf[a][rows, 0:cw], tmpf[a][rows, 0:cw], tabc[ti][rows, 0:cw], ALU.mult, [r_tmpf[a], r_tab[ti]], [r_tmpf[a]])
                TS("dve", tmpf[b][rows, 0:cw], ps[mbs][rows, 0:cw], gscol_, None, ALU.mult, None, [r_ps[mbs], r_pc], [r_tmpf[b]])
                TT("dve", tmpf[b][rows, 0:cw], tmpf[b][rows, 0:cw], tabs[ti][rows, 0:cw], ALU.mult, [r_tmpf[b], r_tab[ti]], [r_tmpf[b]])
            if rstd is None:
                TT(ROPE_ENG, out_ap, tmpf[a][rows, 0:cw], tmpf[b][rows, 0:cw], ALU.add, [r_tmpf[a], r_tmpf[b]], wr)
            else:
                TT(ROPE_ENG, tmpf[a][rows, 0:cw], tmpf[a][rows, 0:cw], tmpf[b][rows, 0:cw], ALU.add, [r_tmpf[a], r_tmpf[b]], [r_tmpf[a]])
                TT(ROPE_ENG, out_ap, tmpf[a][rows, 0:cw], rstd, ALU.mult, [r_tmpf[a]] + list(rstd_rd), wr)

        def run_attention(S, groups):
            cw = min(512, S)
            NC = S // cw
            NKT = S // 128
            steps = []
            oset = [0]
            for g in groups:
                nu = len(g["units"])
                for c in range(NC):
                    if nu == 1:
                        bs = (O_BANKS[oset[0] % 2],)
                        oset[0] += 1
                    else:
                        bs = O_BANKS
                    for ui, u in enumerate(g["units"]):
                        for kt in range(0, NKT, 2):
                            kts = [kt] if kt + 1 >= NKT else [kt, kt + 1]
                            steps.append(dict(g=g, c=c, lanes=[(u, k_, bs[ui]) for k_ in kts],
                                              first=(ui == 0 and kt == 0 and c == 0), cfirst=(ui == 0 and kt == 0),
                                              last=(ui == nu - 1 and kts[-1] == NKT - 1), obs=bs))

            def emit_S(i):
                sp_ = steps[i]
                if sp_["first"] and sp_["g"].get("pre") is not None:
                    sp_["g"]["pre"]()
                if sp_["cfirst"] and sp_["g"].get("prec") is not None:
                    sp_["g"]["prec"](sp_["c"])
                for j, (u, kt, ob) in enumerate(sp_["lanes"]):
                    qap, qres = u["q"](sp_["c"])
                    bank = SB_BANKS[(2 * i + j) % 4]
                    kap, kres = u["k"](kt)
                    MM(ps[bank][:, 0:cw], kap, qap, True, True, list(kres) + list(qres), [r_ps[bank]])

            pending = []
            seq_ = [0]
            cur = [0]

            def defer(fn, delay):
                seq_[0] += 1
                pending.append((cur[0] + delay, seq_[0], fn))

            def run_due(i, flush=False):
                pending.sort(key=lambda x: (x[0], x[1]))
                while pending and (flush or pending[0][0] <= i):
                    pending.pop(0)[2]()
            DEFER[0] = defer

            emit_S(0)
            for i, sp_ in enumerate(steps):
                cur[0] = i
                run_due(i)
                if i + 1 < len(steps):
                    emit_S(i + 1)
                nl = len(sp_["lanes"])
                for j, (u, kt, ob) in enumerate(sp_["lanes"]):
                    bank = SB_BANKS[(2 * i + j) % 4]
                    pi = (2 * i + j) % 4
                    ACT(pts[pi][:, 0:cw], ps[bank][:, 0:cw], AF.Exp, [r_ps[bank]], [r_pt[pi]], scale=u["scale"])
                for j, (u, kt, ob) in enumerate(sp_["lanes"]):
                    pi = (2 * i + j) % 4
                    vap, vres = u["v"](kt)
                    MM(ps[ob][0:65, 0:cw], vap, pts[pi][:, 0:cw], kt == 0, kt == NKT - 1, [r_pt[pi]] + list(vres),
                       [r_ps[ob]], sig=(kt == NKT - 1 or j == nl - 1))
                if sp_["cfirst"] and sp_["g"].get("cstart") is not None:
                    sp_["g"]["cstart"](sp_["c"])
                if sp_["last"]:
                    run_due(i, flush=True)
                    sp_["g"]["epilogue"](sp_["c"], sp_["obs"])
            run_due(0, flush=True)

        DEFER = [None]

        def gate_early(S, gcol, width, state):
            def cstart(c):
                cw = min(512, S)
                n = (cw // 128) * width
                gb = gate_mm(S, c, gcol, width)

                def fin():
                    a = tmpg_ring.next()
                    ACT(tmpg[a][:, 0:n], ps[gb][:, 0:n], AF.Exp, [r_ps[gb]], [r_tmpg[a]], scale=-1.0)
                    TS("dve", tmpg[a][:, 0:n], tmpg[a][:, 0:n], 1.0, None, ALU.add, None, [r_tmpg[a]], [r_tmpg[a]])
                    RCP(tmpg[a][:, 0:n], tmpg[a][:, 0:n], [r_tmpg[a]], [r_tmpg[a]])
                    TT("dve", tmpg[a][:, 0:n], ps[gb][:, 0:n], tmpg[a][:, 0:n], ALU.mult, [r_ps[gb], r_tmpg[a]], [r_tmpg[a]])
                    state[c] = a
                DEFER[0](fin, 2 if (S // 128) >= 8 else 1)
            return cstart

        def o_copy(ob, cw):
            a = tmpo_ring.next()
            CP("dve", tmpo[a][0:65, 0:cw], ps[ob][0:65, 0:cw], [r_ps[ob]], [r_tmpo[a]])
            return a

        def o_transposed(a, cw):
            mb = misc.next()
            nqb = cw // 128
            for qb in range(nqb):
                TR(ps[mb][:, qb * 65:(qb + 1) * 65], tmpo[a][0:65, qb * 128:(qb + 1) * 128], ident_f[0:65, 0:65],
                   [r_tmpo[a], r_const], [r_ps[mb]], sig=(qb == nqb - 1))
            return mb, ps[mb][:, 0:nqb * 65].rearrange("p (q d) -> p q d", d=65)

        def gate_mm(S, c, gcol, width):
            cw = min(512, S)
            nqb = cw // 128
            gb = misc.next()
            for qb in range(nqb):
                t = c * nqb + qb
                for k in range(8):
                    MM(ps[gb][:, qb * width:(qb + 1) * width], h_fm[:, k, t * 128:(t + 1) * 128], wmix[:, k, gcol:gcol + width],
                       k == 0, k == 7, [r_h[t], r_wmix], [r_ps[gb]], sig=(qb == nqb - 1 and k == 7))
            return gb

        def gate_fin(gb, n):
            a = tmpg_ring.next()
            ACT(tmpg[a][:, 0:n], ps[gb][:, 0:n], AF.Exp, [r_ps[gb]], [r_tmpg[a]], scale=-1.0)
            TS("dve", tmpg[a][:, 0:n], tmpg[a][:, 0:n], 1.0, None, ALU.add, None, [r_tmpg[a]], [r_tmpg[a]])
            RCP(tmpg[a][:, 0:n], tmpg[a][:, 0:n], [r_tmpg[a]], [r_tmpg[a]])
            TT("dve", tmpg[a][:, 0:n], ps[gb][:, 0:n], tmpg[a][:, 0:n], ALU.mult, [r_ps[gb], r_tmpg[a]], [r_tmpg[a]])
            return a

        def store_o(s, c, cw, col, og_i, width=64):
            nqb = cw // 128
            dst = oscr[s][c * cw:(c + 1) * cw, col:col + width].rearrange("(q p) d -> p q d", p=128)
            src = ogs[og_i][:, 0:nqb * width].rearrange("p (q d) -> p q d", d=width)
            tl = [r_oscr[s][c * nqb + q] for q in range(nqb)]
            P.dma("pool", f"ost{og_i}", dst, src, reads=[r_og[og_i]], writes=tl)

        def std_epilogue(s, S, col0, h, state, gw=64, goff=0):
            def ep(c, obs):
                cw = min(512, S)
                nqb = cw // 128
                n = nqb * 64
                oc_ = o_copy(obs[0], cw)
                mb, ov = o_transposed(oc_, cw)

                def partB():
                    sc_ = 20 + 4 * (h % 2)
                    RCP(small[:, sc_:sc_ + nqb], ov[:, :, 64], [r_ps[mb]], [r_small])
                    a = tmpf_ring.next()
                    av = tmpf[a][:, 0:n].rearrange("p (q d) -> p q d", d=64)
                    TT("dve", av, ov[:, :, 0:64], small[:, sc_:sc_ + nqb].unsqueeze(2).broadcast_to([128, nqb, 64]), ALU.mult,
                       [r_ps[mb], r_small], [r_tmpf[a]])
                    sg = state[c]
                    sgv = tmpg[sg][:, 0:nqb * gw].rearrange("p (q d) -> p q d", d=gw)[:, :, goff:goff + 64]
                    oi = og_ring.next()
                    TT("pool", ogs[oi][:, 0:n].rearrange("p (q d) -> p q d", d=64), av, sgv, ALU.mult,
                       [r_tmpf[a], r_tmpg[sg]], [r_og[oi]])
                    store_o(s, c, cw, col0 + h * 64, oi)
                DEFER[0](partB, 1)
            return ep

        def mixer_D(l, s):
            S = seq_lens[s]
            cw = min(512, S)
            NC = S // cw
            P.wait_all("pool", [r_wmix])
            def q_extra(i, stg, p0, pn):
                sv = swapped(stg, 16)
                dvw = swapped(wmix[:, :, 256 + p0:256 + p0 + pn], 16)
                CP("pool", dvw[:, :, :, 0, :], sv[:, :, :, 1, :], [r_wst[i]], [r_wmix])
                CP("pool", dvw[:, :, :, 1, :], sv[:, :, :, 0, :], [r_wst[i]], [r_wmix])
            load_cast(l, OFF["qd"], 256, 0, extra=q_extra)
            i, stg = load_w(w_in[l][:, OFF["kd"]:OFF["kd"] + 128], 128, 0)
            for g in range(2):
                for r in range(2):
                    c0 = 512 + g * 128 + r * 64
                    CP("pool", wmix[:, :, c0:c0 + 64], stg[:, :, g * 64:(g + 1) * 64], [r_wst[i]], [r_wmix])
                    sv = swapped(stg[:, :, g * 64:(g + 1) * 64], 16)
                    dvw = swapped(wmix[:, :, c0 + 256:c0 + 256 + 64], 16)
                    CP("pool", dvw[:, :, :, 0, :], sv[:, :, :, 1, :], [r_wst[i]], [r_wmix])
                    CP("pool", dvw[:, :, :, 1, :], sv[:, :, :, 0, :], [r_wst[i]], [r_wmix])
            load_cast(l, OFF["vd"], 128, 1024)
            load_cast(l, OFF["gd"], 256, 1152)
            for c in range(NC):
                ti = load_tab(tab_gqa, c, cw)
                for fc in range(4):
                    isq = fc < 2
                    wc = fc * 128 if isq else 512 + (fc - 2) * 128
                    mb = proj_fm(S, c, wc)
                    mbs = proj_fm(S, c, wc + 256)
                    sq = tmpb_ring.next()
                    ACT(tmpb[sq][:, 0:cw], ps[mb][:, 0:cw], AF.Square, [r_ps[mb]], [r_tmpb[sq]])
                    sb_ = SB_BANKS[fc % 2]
                    MM(ps[sb_][:, 0:cw], bones_b[:], tmpb[sq][:, 0:cw], True, True, [r_const, r_tmpb[sq]], [r_ps[sb_]])
                    rs = tmpf_ring.next()
                    rstd_act(tmpf[rs][:, 0:cw], ps[sb_][:, 0:cw], 64.0, [r_ps[sb_]], [r_tmpf[rs]])
                    if isq:
                        out_ap, wr = qbuf[:, fc, c * cw:(c + 1) * cw], [r_q[fc][c]]
                        g1, g2 = pc[:, PC_GQ:PC_GQ + 1], pc[:, PC_GQS:PC_GQS + 1]
                    else:
                        out_ap, wr = kbuf[:, fc - 2, c * cw:(c + 1) * cw], [r_k[fc - 2][c]]
                        g1, g2 = pc[:, PC_GK:PC_GK + 1], pc[:, PC_GKS:PC_GKS + 1]
                    rope_evac(mb, mbs, ti, slice(0, 128), cw, out_ap, wr, gcol=g1, gscol_=g2,
                              rstd=tmpf[rs][:, 0:cw], rstd_rd=[r_tmpf[rs]])
                for qb in range(cw // 128):
                    t = c * (cw // 128) + qb
                    mb = misc.next()
                    for k in range(8):
                        MM(ps[mb][:, 0:128], h_fm[:, k, t * 128:(t + 1) * 128], wmix[:, k, 1024:1152], k == 0, k == 7,
                           [r_h[t], r_wmix], [r_ps[mb]], sig=(k == 7))
                    CP("dve", vaug[:, t, 0:130].rearrange("p (h d) -> p h d", d=65)[:, :, 0:64],
                       ps[mb][:, 0:128].rearrange("p (h d) -> p h d", d=64), [r_ps[mb]], [r_v[t]])
            groups = []
            for fc in range(2):
                units = []
                for hh in range(2):
                    def qf(c, hh=hh):
                        return qz[:, hh, 0:cw], [r_qz[hh]]

                    def kf(kt, fc=fc):
                        return kbuf[:, fc, kt * 128:(kt + 1) * 128], [r_k[fc][kt * 128 // cw]]

                    def vf(kt, fc=fc):
                        return vaug[:, kt, fc * 65:(fc + 1) * 65], [r_v[kt]]
                    units.append(dict(q=qf, k=kf, v=vf, scale=64 ** -0.5))

                def prec(c, fc=fc):
                    CP("pool", qz[0:64, 0, 0:cw], qbuf[0:64, fc, c * cw:(c + 1) * cw], [r_q[fc][c]], [r_qz[0]])
                    CP("dve", qz[64:128, 1, 0:cw], qbuf[64:128, fc, c * cw:(c + 1) * cw], [r_q[fc][c]], [r_qz[1]])
                gst0, gst1 = {}, {}
                e0 = std_epilogue(s, S, 512, 2 * fc, gst0)
                e1 = std_epilogue(s, S, 512, 2 * fc + 1, gst1)
                gh0 = gate_early(S, 1152 + 2 * fc * 64, 64, gst0)
                gh1 = gate_early(S, 1152 + (2 * fc + 1) * 64, 64, gst1)

                def cst2(c, gh0=gh0, gh1=gh1):
                    gh0(c)
                    gh1(c)

                def ep2(c, obs, e0=e0, e1=e1):
                    e0(c, (obs[0],))
                    e1(c, (obs[1],))
                groups.append(dict(units=units, epilogue=ep2, prec=prec, cstart=cst2))
            MEMSET("pool", qz[64:128, 0, :], 0.0, [r_qz[0]])
            MEMSET("pool", qz[0:64, 1, :], 0.0, [r_qz[1]])
            run_attention(S, groups)

        def mixer_B(l, s):
            S = seq_lens[s]
            cw = min(512, S)
            NC = S // cw
            P.wait_all("pool", [r_wmix])
            for nm, c0 in (("qb", 0), ("kb", 512)):
                def sw_extra(i, stg, p0, pn, c0=c0):
                    d0 = c0 + 256 + p0
                    CP("pool", wmix[:, :, d0:d0 + pn], stg, [r_wst[i]], [r_wmix])
                    sv = stg.rearrange("p k (m j) -> p k m j", j=32)
                    dvw = wmix[:, :, d0:d0 + pn].rearrange("p k (m j) -> p k m j", j=32)
                    CP("pool", dvw[:, :, :, 0:4], sv[:, :, :, 4:8], [r_wst[i]], [r_wmix])
                    CP("pool", dvw[:, :, :, 4:8], sv[:, :, :, 0:4], [r_wst[i]], [r_wmix])
                load_cast(l, OFF[nm], 256, c0, extra=sw_extra)
            load_cast(l, OFF["vb"], 256, 1024)
            load_cast(l, OFF["gb"], 256, 1280)
            for c in range(NC):
                ti = load_tab(tab_diff, c, cw)
                for fc in range(4):
                    isq = fc < 2
                    wc = fc * 128 if isq else 512 + (fc - 2) * 128
                    mb = proj_fm(S, c, wc)
                    mbs = proj_fm(S, c, wc + 256)
                    if isq:
                        out_ap, wr = qbuf[:, fc, c * cw:(c + 1) * cw], [r_q[fc][c]]
                    else:
                        out_ap, wr = kbuf[:, fc - 2, c * cw:(c + 1) * cw], [r_k[fc - 2][c]]
                    rope_evac(mb, mbs, ti, slice(0, 128), cw, out_ap, wr)
                for qb in range(cw // 128):
                    t = c * (cw // 128) + qb
                    mb = misc.next()
                    for k in range(8):
                        MM(ps[mb][:, 0:256], h_fm[:, k, t * 128:(t + 1) * 128], wmix[:, k, 1024:1280], k == 0, k == 7,
                           [r_h[t], r_wmix], [r_ps[mb]], sig=(k == 7))
                    CP("dve", vaug[:, t, 0:260].rearrange("p (h d) -> p h d", d=65)[:, :, 0:64],
                       ps[mb][:, 0:256].rearrange("p (h d) -> p h d", d=64), [r_ps[mb]], [r_v[t]])
            groups = []
            for h in range(4):
                fc = h // 2
                units = []
                for j in range(2):
                    b_ = (h % 2) * 2 + j

                    def qf(c, b_=b_):
                        return qz[:, b_, 0:cw], [r_qz[b_]]

                    def kf(kt, fc=fc):
                        return kbuf[:, fc, kt * 128:(kt + 1) * 128], [r_k[fc][kt * 128 // cw]]

                    def vf(kt, h=h):
                        return vaug[:, kt, h * 65:(h + 1) * 65], [r_v[kt]]
                    units.append(dict(q=qf, k=kf, v=vf, scale=32 ** -0.5))

                def prec(c, h=h, fc=fc):
                    for j in range(2):
                        b_ = (h % 2) * 2 + j
                        rs_ = slice(b_ * 32, (b_ + 1) * 32)
                        CP("pool" if j == 0 else "dve", qz[rs_, b_, 0:cw], qbuf[rs_, fc, c * cw:(c + 1) * cw], [r_q[fc][c]], [r_qz[b_]])

                gst = {}

                def ep(c, obs, h=h, gst=gst):
                    nqb = cw // 128
                    n = nqb * 64
                    oc1 = o_copy(obs[0], cw)
                    oc2 = o_copy(obs[1], cw)
                    mb1, ov1 = o_transposed(oc1, cw)
                    mb2, ov2 = o_transposed(oc2, cw)
                    st_ = {}

                    def partB1():
                        RCP(small[:, 24:24 + nqb], ov1[:, :, 64], [r_ps[mb1]], [r_small])
                        RCP(small[:, 28:28 + nqb], ov2[:, :, 64], [r_ps[mb2]], [r_small])
                        TS("dve", small[:, 28:28 + nqb], small[:, 28:28 + nqb], lamc[:, 3:4], None, ALU.mult, None, [r_small, r_wa], [r_small])
                        a = tmpf_ring.next()
                        b = tmpf_ring.next()
                        st_["a"], st_["b"] = a, b
                        av = tmpf[a][:, 0:n].rearrange("p (q d) -> p q d", d=64)
                        bv = tmpf[b][:, 0:n].rearrange("p (q d) -> p q d", d=64)
                        TT("dve", av, ov1[:, :, 0:64], small[:, 24:24 + nqb].unsqueeze(2).broadcast_to([128, nqb, 64]), ALU.mult,
                           [r_ps[mb1], r_small], [r_tmpf[a]])
                        TT("dve", bv, ov2[:, :, 0:64], small[:, 28:28 + nqb].unsqueeze(2).broadcast_to([128, nqb, 64]), ALU.mult,
                           [r_ps[mb2], r_small], [r_tmpf[b]])
                        TT("dve", tmpf[a][:, 0:n], tmpf[a][:, 0:n], tmpf[b][:, 0:n], ALU.add, [r_tmpf[a], r_tmpf[b]], [r_tmpf[a]])
                        TT("dve", tmpf[b][:, 0:n], tmpf[a][:, 0:n], tmpf[a][:, 0:n], ALU.mult, [r_tmpf[a]], [r_tmpf[b]])
                        P.op("dve", lambda e: e.tensor_reduce(out=small[:, 32:32 + nqb], in_=bv, axis=AX.X, op=ALU.add),
                             reads=[r_tmpf[b]], writes=[r_small])

                    def partB2():
                        a = st_["a"]
                        av = tmpf[a][:, 0:n].rearrange("p (q d) -> p q d", d=64)
                        rstd_act(small[:, 36:36 + nqb], small[:, 32:32 + nqb], 64.0, [r_small], [r_small])
                        TT("dve", av, av, small[:, 36:36 + nqb].unsqueeze(2).broadcast_to([128, nqb, 64]), ALU.mult,
                           [r_tmpf[a], r_small], [r_tmpf[a]])
                        TT("pool", av, av, subln_bc[:, 0:nqb, :], ALU.mult, [r_tmpf[a], r_wa], [r_tmpf[a]])
                        sg = gst[c]
                        oi = og_ring.next()
                        TT("pool", ogs[oi][:, 0:n], tmpf[a][:, 0:n], tmpg[sg][:, 0:n], ALU.mult, [r_tmpf[a], r_tmpg[sg]], [r_og[oi]])
                        store_o(s, c, cw, 256 + h * 64, oi)
                    DEFER[0](partB1, 1)
                    DEFER[0](partB2, 5)
                groups.append(dict(units=units, epilogue=ep, prec=prec, cstart=gate_early(S, 1280 + h * 64, 64, gst)))
            for b_ in range(4):
                MEMSET("pool", qz[:, b_, :], 0.0, [r_qz[b_]])
            run_attention(S, groups)

        def mixer_A(l, s):
            S = seq_lens[s]
            cw = min(512, S)
            NC = S // cw
            nslots = 2 if 2 * S <= Smax else 1
            P.wait_all("pool", [r_wmix])

            def kr_extra(i, stg, p0, pn):
                if p0 == 256:
                    CP("pool", wmix[:, :, 352:368], stg[:, :, 80:96], [r_wst[i]], [r_wmix])
                    CP("pool", wmix[:, :, 368:384], stg[:, :, 64:80], [r_wst[i]], [r_wmix])
            load_cast(l, 0, 352, 0, extra=kr_extra)
            load_cast(l, OFF["ga"], 256, 384)
            R64 = slice(64, 96)
            qn0 = qbuf[:, 1, :]
            qn1 = kbuf[:, 1, :]

            def qs_(slot, c):
                return qbuf[:, 0, slot * S + c * cw:slot * S + (c + 1) * cw], r_q[0][slot * NC + c]

            def ks_(slot, c):
                return kbuf[:, 0, slot * S + c * cw:slot * S + (c + 1) * cw], r_k[0][slot * NC + c]

            for c in range(NC):
                tl = [r_h[t] for t in range(c * cw // 128, (c + 1) * cw // 128)]
                csl = slice(c * cw, (c + 1) * cw)
                mbq = []
                sqs = []
                for fc, rows in ((0, 128), (1, 64)):
                    mb = proj_fm(S, c, fc * 128, rows=rows)
                    sq = tmpb_ring.next()
                    ACT(tmpb[sq][0:rows, 0:cw], ps[mb][0:rows, 0:cw], AF.Square, [r_ps[mb]], [r_tmpb[sq]])
                    mbq.append(mb)
                    sqs.append(sq)
                sb_ = SB_BANKS[0]
                MM(ps[sb_][:, 0:cw], ones_b[:, :], tmpb[sqs[0]][:, 0:cw], True, False, [r_const, r_tmpb[sqs[0]]], [r_ps[sb_]], sig=False)
                MM(ps[sb_][:, 0:cw], ones_b[0:64, :], tmpb[sqs[1]][0:64, 0:cw], False, True, [r_const, r_tmpb[sqs[1]]], [r_ps[sb_]])
                rs = tmpf_ring.next()
                rstd_act(tmpf[rs][:, 0:cw], ps[sb_][:, 0:cw], 192.0, [r_ps[sb_]], [r_tmpf[rs]])
                TT("dve", qn0[:, csl], ps[mbq[0]][:, 0:cw], tmpf[rs][:, 0:cw], ALU.mult,
                   [r_ps[mbq[0]], r_tmpf[rs]], [r_q[1][c]])
                TT("dve", qn1[0:64, csl], ps[mbq[1]][0:64, 0:cw], tmpf[rs][0:64, 0:cw], ALU.mult,
                   [r_ps[mbq[1]], r_tmpf[rs]], [r_k[1][c]])
                mb = proj_fm(S, c, 192)
                sq = tmpb_ring.next()
                ACT(tmpb[sq][:, 0:cw], ps[mb][:, 0:cw], AF.Square, [r_ps[mb]], [r_tmpb[sq]])
                sb_ = SB_BANKS[1]
                MM(ps[sb_][:, 0:cw], ones_b[:, :], tmpb[sq][:, 0:cw], True, True, [r_const, r_tmpb[sq]], [r_ps[sb_]])
                rs = tmpf_ring.next()
                rstd_act(tmpf[rs][:, 0:cw], ps[sb_][:, 0:cw], 128.0, [r_ps[sb_]], [r_tmpf[rs]])
                TT("dve", lat[:, csl], ps[mb][:, 0:cw], tmpf[rs][:, 0:cw], ALU.mult,
                   [r_ps[mb], r_tmpf[rs]], [r_lat[c]])
                for qb in range(cw // 128):
                    t = c * (cw // 128) + qb
                    vb_ = misc.next()
                    MM(ps[vb_][:, 0:256], lat[:, t * 128:(t + 1) * 128], wukv[:, 256:512], True, True, [r_lat[c], r_wa], [r_ps[vb_]])
                    CP("dve", vaug[:, t, 0:260].rearrange("p (h d) -> p h d", d=65)[:, :, 0:64],
                       ps[vb_][:, 0:256].rearrange("p (h d) -> p h d", d=64), [r_ps[vb_]], [r_v[t]])
                ti = load_tab(tab_mla, c, cw, rows=R64)
                mbk = misc.next()
                mbks = misc.next()
                for k in range(8):
                    MM(ps[mbk][64:96, 0:cw], wmix[:, k, 320:352], h_fm[:, k, c * cw:(c + 1) * cw], k == 0, k == 7,
                       [r_wmix] + tl, [r_ps[mbk]], sig=(k == 7))
                for k in range(8):
                    MM(ps[mbks][64:96, 0:cw], wmix[:, k, 352:384], h_fm[:, k, c * cw:(c + 1) * cw], k == 0, k == 7,
                       [r_wmix] + tl, [r_ps[mbks]], sig=(k == 7))
                k0, rk0 = ks_(0, c)
                rope_evac(mbk, mbks, ti, R64, cw, k0[64:96, :], [rk0])
                if nslots == 2:
                    k1, rk1 = ks_(1, c)
                    CP("pool", k1[64:96, :], k0[64:96, :], [rk0], [rk1])

            def head_proj(h):
                slot = h % nslots
                for c in range(NC):
                    csl = slice(c * cw, (c + 1) * cw)
                    qa_, rq_ = qs_(slot, c)
                    ka_, rk_ = ks_(slot, c)
                    mb = misc.next()
                    MM(ps[mb][0:96, 0:cw], wuq[:, 0, h * 96:(h + 1) * 96], qn0[:, csl], True, False,
                       [r_wa, r_q[1][c]], [r_ps[mb]], sig=False)
                    MM(ps[mb][0:96, 0:cw], wuq[0:64, 1, h * 96:(h + 1) * 96], qn1[0:64, csl], False, True,
                       [r_wa, r_k[1][c]], [r_ps[mb]])
                    mbs = misc.next()
                    MM(ps[mbs][0:96, 0:cw], wuqs[:, 0, h * 96:(h + 1) * 96], qn0[:, csl], True, False,
                       [r_wa, r_q[1][c]], [r_ps[mbs]], sig=False)
                    MM(ps[mbs][0:96, 0:cw], wuqs[0:64, 1, h * 96:(h + 1) * 96], qn1[0:64, csl], False, True,
                       [r_wa, r_k[1][c]], [r_ps[mbs]])
                    CP("dve", qa_[0:64, :], ps[mb][0:64, 0:cw], [r_ps[mb]], [rq_])
                    ti = load_tab(tab_mla, c, cw, rows=R64)
                    rope_evac(mb, mbs, ti, R64, cw, qa_[64:96, :], [rq_])
                    mbk = misc.next()
                    MM(ps[mbk][0:64, 0:cw], wukv[:, h * 64:(h + 1) * 64], lat[:, csl], True, True,
                       [r_wa, r_lat[c]], [r_ps[mbk]])
                    CP("dve", ka_[0:64, :], ps[mbk][0:64, 0:cw], [r_ps[mbk]], [rk_])

            groups = []
            for h in range(4):
                slot = h % nslots

                def qf(c, slot=slot):
                    a_, r_ = qs_(slot, c)
                    return a_[0:96, :], [r_]

                def kf(kt, slot=slot):
                    a_, r_ = ks_(slot, kt * 128 // cw)
                    o_ = (kt * 128) % cw
                    return a_[0:96, o_:o_ + 128], [r_]

                def vf(kt, h=h):
                    return vaug[:, kt, h * 65:(h + 1) * 65], [r_v[kt]]
                u = dict(q=qf, k=kf, v=vf, scale=96 ** -0.5)
                pre = None
                if nslots == 2:
                    if h + 1 < 4:
                        pre = (lambda hh=h + 1: head_proj(hh))
                elif h >= 1:
                    pre = (lambda hh=h: head_proj(hh))
                gst = {}
                groups.append(dict(units=[u], epilogue=std_epilogue(s, S, 0, h, gst), pre=pre,
                                   cstart=gate_early(S, 384 + h * 64, 64, gst)))
            head_proj(0)
            run_attention(S, groups)

        def mixer_C(l, s):
            S = seq_lens[s]
            cw = min(512, S)
            NC = S // cw
            P.wait_all("pool", [r_wmix])
            load_cast(l, OFF["xc"], 256, 0)
            load_cast(l, OFF["gc"], 256, 256)
            xcv = vaug[:].rearrange("p t d -> p (t d)").bitcast(F32)
            xconv = kbuf[:].rearrange("p a s -> p (a s)").bitcast(F32)
            hsum = qbuf[:].rearrange("p a s -> p (a s)").bitcast(F32)
            xcb = lat[:, :]
            allr = [r for rr in (r_lat, r_v) for r in rr] + [r for sl in (r_q, r_k) for rr in sl for r in rr] + r_qz + r_pt
            r_xc, r_xconv, r_hsum, r_xcb = Res("xc"), Res("xconv"), Res("hsum"), Res("xcb")
            qzf = qz[:].rearrange("p a n -> p (a n)").bitcast(F32)
            ptf = ptbuf[:].rearrange("p a n -> p (a n)").bitcast(F32)
            tmpfL = list(tmpf) + [qzf[:, 0:512], qzf[:, 512:1024], ptf[:, 0:512], ptf[:, 512:1024]]
            r_extra = [Res(f"lrux{i_}") for i_ in range(4)]
            r_tmpfL = list(r_tmpf) + r_extra
            tmpfL_ring = Ring(list(range(8)))
            for e_ in ("dve", "pool", "act", "pe"):
                P.wait_all(e_, allr)
            for cc in range(2):
                MEMSET("dve", xcv[:, 0:2], 0.0, [r_xc])
                MEMSET("dve", xcv[:, S + 2:S + 4], 0.0, [r_xc])
                for c in range(NC):
                    mb = proj_fm(S, c, cc * 128)
                    if c % 2 == 0:
                        CP("dve", xcv[:, 2 + c * cw:2 + (c + 1) * cw], ps[mb][:, 0:cw], [r_ps[mb]], [r_xc])
                    else:
                        ACT(xcv[:, 2 + c * cw:2 + (c + 1) * cw], ps[mb][:, 0:cw], AF.Copy, [r_ps[mb]], [r_xc])
                cwc = lambda j: pc[:, PC_CW + cc * 4 + j:PC_CW + cc * 4 + j + 1]
                for c in range(NC):
                    sl = slice(c * cw, (c + 1) * cw)
                    TS("dve", xconv[:, sl], xcv[:, c * cw:(c + 1) * cw], cwc(0), pc[:, PC_CB + cc:PC_CB + cc + 1], ALU.mult, ALU.add,
                       [r_xc, r_pc], [r_xconv])
                    for j in range(1, 4):
                        STT(xconv[:, sl], xcv[:, j + c * cw:j + (c + 1) * cw], cwc(j), xconv[:, sl], ALU.mult, ALU.add,
                            [r_xc, r_pc, r_xconv], [r_xconv])
                    ACT(xcb[:, sl], xconv[:, sl], AF.Copy, [r_xconv], [r_xcb])
                for d_ in range(2):
                    order = range(NC) if d_ == 0 else range(NC - 1, -1, -1)
                    prev = None
                    for c in order:
                        sl = slice(c * cw, (c + 1) * cw)
                        mba = misc.next()
                        MM(ps[mba][:, 0:cw], lruw[:, 0, d_, cc, :], xcb[:, sl], True, True, [r_wa, r_xcb], [r_ps[mba]])
                        mbx = misc.next()
                        MM(ps[mbx][:, 0:cw], lruw[:, 1, d_, cc, :], xcb[:, sl], True, True, [r_wa, r_xcb], [r_ps[mbx]])
                        ra, ri = tmpfL_ring.next(), tmpfL_ring.next()
                        ci = d_ * 2 + cc
                        ACT(tmpfL[ra][:, 0:cw], ps[mba][:, 0:cw], AF.Exp, [r_ps[mba], r_wa], [r_tmpfL[ra]], scale=-1.0, bias=lsp[:, 4 + ci:5 + ci])
                        ACT(tmpfL[ri][:, 0:cw], ps[mbx][:, 0:cw], AF.Exp, [r_ps[mbx], r_wa], [r_tmpfL[ri]], scale=-1.0, bias=lsp[:, 8 + ci:9 + ci])
                        ACT(tmpfL[ra][:, 0:cw], tmpfL[ra][:, 0:cw], AF.Ln, [r_tmpfL[ra]], [r_tmpfL[ra]], bias=1.0)
                        ACT(tmpfL[ri][:, 0:cw], tmpfL[ri][:, 0:cw], AF.Ln, [r_tmpfL[ri]], [r_tmpfL[ri]], bias=1.0)
                        ACT(tmpfL[ra][:, 0:cw], tmpfL[ra][:, 0:cw], AF.Exp, [r_tmpfL[ra]], [r_tmpfL[ra]], scale=-1.0)
                        ACT(tmpfL[ri][:, 0:cw], tmpfL[ri][:, 0:cw], AF.Exp, [r_tmpfL[ri]], [r_tmpfL[ri]], scale=-1.0)
                        ACT(tmpfL[ra][:, 0:cw], tmpfL[ra][:, 0:cw], AF.Exp, [r_tmpfL[ra], r_wa], [r_tmpfL[ra]], scale=lsp[:, ci:ci + 1])
                        rg = tmpfL_ring.next()
                        TT("dve", tmpfL[rg][:, 0:cw], tmpfL[ra][:, 0:cw], tmpfL[ra][:, 0:cw], ALU.mult, [r_tmpfL[ra]], [r_tmpfL[rg]])
                        TS("dve", tmpfL[rg][:, 0:cw], tmpfL[rg][:, 0:cw], -1.0, 1.0, ALU.mult, ALU.add, [r_tmpfL[rg]], [r_tmpfL[rg]])
                        ACT(tmpfL[rg][:, 0:cw], tmpfL[rg][:, 0:cw], AF.Ln, [r_tmpfL[rg]], [r_tmpfL[rg]])
                        ACT(tmpfL[rg][:, 0:cw], tmpfL[rg][:, 0:cw], AF.Exp, [r_tmpfL[rg]], [r_tmpfL[rg]], scale=0.5)
                        TT("pool", tmpfL[ri][:, 0:cw], tmpfL[ri][:, 0:cw], xconv[:, sl], ALU.mult, [r_tmpfL[ri], r_xconv], [r_tmpfL[ri]])
                        TT("dve", tmpfL[rg][:, 0:cw], tmpfL[rg][:, 0:cw], tmpfL[ri][:, 0:cw], ALU.mult, [r_tmpfL[rg], r_tmpfL[ri]], [r_tmpfL[rg]])
                        init = 0.0 if prev is None else prev
                        if d_ == 0:
                            P.op("dve", lambda e, ra=ra, rg=rg, init=init, ri=ri: e.tensor_tensor_scan(
                                out=tmpfL[ri][:, 0:cw], data0=tmpfL[ra][:, 0:cw], data1=tmpfL[rg][:, 0:cw], initial=init,
                                op0=ALU.mult, op1=ALU.add), reads=[r_tmpfL[ra], r_tmpfL[rg], r_small], writes=[r_tmpfL[ri]])
                            CP("dve", small[:, 40:41], tmpfL[ri][:, cw - 1:cw], [r_tmpfL[ri]], [r_small])
                            prev = small[:, 40:41]
                            CP("pool", hsum[:, sl], tmpfL[ri][:, 0:cw], [r_tmpfL[ri]], [r_hsum])
                        else:
                            P.op("dve", lambda e, ra=ra, rg=rg, init=init, ri=ri: e.tensor_tensor_scan(
                                out=tmpfL[ri][:, 0:cw][:, ::-1], data0=tmpfL[ra][:, 0:cw][:, ::-1], data1=tmpfL[rg][:, 0:cw][:, ::-1],
                                initial=init, op0=ALU.mult, op1=ALU.add), reads=[r_tmpfL[ra], r_tmpfL[rg], r_small], writes=[r_tmpfL[ri]])
                            CP("dve", small[:, 41:42], tmpfL[ri][:, 0:1], [r_tmpfL[ri]], [r_small])
                            prev = small[:, 41:42]
                            TT("pool", hsum[:, sl], hsum[:, sl], tmpfL[ri][:, 0:cw], ALU.add, [r_hsum, r_tmpfL[ri]], [r_hsum])
                for c in range(NC):
                    sl = slice(c * cw, (c + 1) * cw)
                    gb = proj_fm(S, c, 256 + cc * 128)
                    a = tmpfL_ring.next()
                    ACT(tmpfL[a][:, 0:cw], ps[gb][:, 0:cw], AF.Exp, [r_ps[gb]], [r_tmpfL[a]], scale=-1.0)
                    ACT(tmpfL[a][:, 0:cw], tmpfL[a][:, 0:cw], AF.Ln, [r_tmpfL[a]], [r_tmpfL[a]], bias=1.0)
                    ACT(tmpfL[a][:, 0:cw], tmpfL[a][:, 0:cw], AF.Exp, [r_tmpfL[a]], [r_tmpfL[a]], scale=-1.0)
                    TT("dve", tmpfL[a][:, 0:cw], ps[gb][:, 0:cw], tmpfL[a][:, 0:cw], ALU.mult, [r_ps[gb], r_tmpfL[a]], [r_tmpfL[a]])
                    b = tmpb_ring.next()
                    TT("dve", tmpb[b][:, 0:cw], tmpfL[a][:, 0:cw], hsum[:, sl], ALU.mult, [r_tmpfL[a], r_hsum], [r_tmpb[b]])
                    P.dma("pool", f"oc{b}", ocscr[s][cc * 128:(cc + 1) * 128, sl], tmpb[b][:, 0:cw], reads=[r_tmpb[b]], writes=[r_ocscr[s]])
            MEMSET("pool", vaug[:], 1.0, [r_xcb, r_xc, r_xconv, r_hsum])
            for e_ in ("dve", "pool", "act", "pe"):
                P.wait_all(e_, [r_xcb, r_xc, r_xconv, r_hsum] + r_extra)

        def phase3(l, s, last):
            S = seq_lens[s]
            P.wait_all("pool", [r_wmix])
            for k in range(8):
                i = wst_ring.next()
                LOAD(f"wst{i}", wst[i][:, 0:1024], w_out[l][k * 128:(k + 1) * 128, :], [], [r_wst[i]])
                TT("dve", wmix[:, k, 0:1024], wst[i][:, 0:1024], gate_bc[:, :], ALU.mult, [r_wst[i], r_gate], [r_wmix])
            src = x_in[s] if l == 0 else xres[s]
            misc.items = [6, 7, 0, 1, 2, 3]
            if last:
                LOAD("c3", fn_bc, fnorm.partition_broadcast(128), [], [r_mod])
            for t in range(S // 128):
                jb = t % 2
                ot = tmpf[jb][:].bitcast(BF16)[:, 0:768]
                ofm = tmpf[2 + jb][:].bitcast(BF16).rearrange("p (k n) -> p k n", n=128)
                r_ot, r_ofm = r_tmpf[jb], r_tmpf[2 + jb]
                P.dma("sp", f"otl{jb}", ot, oscr[s][t * 128:(t + 1) * 128, :], reads=[r_oscr[s][t]], writes=[r_ot])
                P.dma("sp", f"ofl{jb}", ofm[:, 4:6, :], ocscr[s][:, t * 128:(t + 1) * 128].rearrange("(c p) n -> p c n", p=128),
                      reads=[r_ocscr[s]], writes=[r_ofm])
                mb = misc.next()
                pT = ps[mb][:].bitcast(BF16)
                for j in range(6):
                    TR(pT[:, j * 128:(j + 1) * 128], ot[:, j * 128:(j + 1) * 128], ident_b[:], [r_ot, r_const], [r_ps[mb]], sig=(j == 5))
                CP("dve", ofm[:, 0:4, :], pT[:, 0:512].rearrange("p (j n) -> p j n", n=128), [r_ps[mb]], [r_ofm])
                CP("dve", ofm[:, 6:8, :], pT[:, 512:768].rearrange("p (j n) -> p j n", n=128), [r_ps[mb]], [r_ofm])
                xi = xt_ring.next()
                rd = [] if l == 0 else [r_xres[s][t]]
                LOAD(f"xt{xi}", xt[xi][:], src[t * 128:(t + 1) * 128, :], rd, [r_xt[xi]])
                for n in range(2):
                    yb = misc.next()
                    for k in range(8):
                        MM(ps[yb][:, :], ofm[:, k, :], wmix[:, k, n * 512:(n + 1) * 512], k == 0, k == 7, [r_ofm, r_wmix], [r_ps[yb]], sig=(k == 7))
                    TT("dve", xt[xi][:, n * 512:(n + 1) * 512], xt[xi][:, n * 512:(n + 1) * 512], ps[yb][:, :], ALU.add,
                       [r_ps[yb], r_xt[xi]], [r_xt[xi]])
                if not last:
                    P.dma("pool", f"xst{xi}", xres[s][t * 128:(t + 1) * 128, :], xt[xi][:], reads=[r_xt[xi]], writes=[r_xres[s][t]])
                else:
                    ACT(xn[:], xt[xi][:], AF.Square, [r_xt[xi]], [r_pt[0], r_pt[1], r_small], accum=small[:, 17:18])
                    rstd_act(small[:, 17:18], small[:, 17:18], float(D), [r_small], [r_small])
                    STT(xt[xi][:], xt[xi][:], small[:, 17:18], fn_bc, ALU.mult, ALU.mult, [r_xt[xi], r_small, r_mod], [r_xt[xi]])
                    P.dma("pool", f"xst{xi}", y_out[s][t * 128:(t + 1) * 128, :], xt[xi][:], reads=[r_xt[xi]], writes=[r_y])
            misc.items = [6, 7]

        try:
          for l in range(depth):
            layer_prep(l)
            for s in range(nseq):
                phase0(l, s)
                phase1(l, s)
                if _os.environ.get("KSTOP") == "p1":
                    raise StopIteration
                if "A" in mixers:
                    mixer_A(l, s)
                if "B" in mixers:
                    mixer_B(l, s)
                if "C" in mixers:
                    mixer_C(l, s)
                if "D" in mixers:
                    mixer_D(l, s)
                phase3(l, s, l == depth - 1)
        except StopIteration:
            pass
        allres = [r_y] + [r for rr in r_xres for r in rr] + [r for rr in r_oscr for r in rr] + r_ocscr
        P.wait_all("sp", allres)
        P.emit()
    return nc, P


def _rope_tables(Smax):
    pos = np.arange(Smax, dtype=np.float32)

    def cs(p, theta, half):
        inv = np.power(np.float32(theta), -np.arange(half, dtype=np.float32) / np.float32(half)).astype(np.float32)
        ang = (p[:, None] * inv[None, :]).astype(np.float32)
        return np.cos(ang).astype(np.float32).T, np.sin(ang).astype(np.float32).T

    c, s_ = cs(pos, 10000.0, 16)
    tab_mla = np.zeros((2, 32, Smax), np.float32)
    tab_mla[0, 0:16] = c
    tab_mla[0, 16:32] = c
    tab_mla[1, 0:16] = -s_
    tab_mla[1, 16:32] = s_
    c, s_ = cs(pos, 500000.0, 4)
    blk_c = np.ones((32, Smax), np.float32)
    blk_s = np.zeros((32, Smax), np.float32)
    blk_c[0:4] = c
    blk_c[4:8] = c
    blk_s[0:4] = -s_
    blk_s[4:8] = s_
    tab_diff = np.stack([np.tile(blk_c, (4, 1)), np.tile(blk_s, (4, 1))])
    row = np.floor(pos / 64.0).astype(np.float32)
    col = (pos - row * 64.0).astype(np.float32)
    cr, sr = cs(row, 10000.0, 16)
    cc, sc_ = cs(col, 10000.0, 16)
    blk_c = np.concatenate([cr, cr, cc, cc], 0)
    blk_s = np.concatenate([-sr, sr, -sc_, sc_], 0)
    tab_gqa = np.stack([np.tile(blk_c, (2, 1)), np.tile(blk_s, (2, 1))])
    return tab_mla, np.ascontiguousarray(tab_diff), np.ascontiguousarray(tab_gqa)


def _host_layout(inp, depth):
    f = np.float32
    pcol = np.zeros((depth, 128, NPC), f)
    prow = np.zeros((depth, NPR), f)
    p = np.arange(128)
    partner = np.where((p % 32) < 16, p + 16, p - 16) % 64
    for l in range(depth):
        pcol[l, :, PC_NG:PC_NG + 8] = inp["norm_g"][l].reshape(8, 128).T
        pcol[l, :, PC_ADAB:PC_ADAB + 24] = inp["ada_b"][l].reshape(24, 128).T
        pcol[l, :, PC_QN] = inp["mla_q_norm"][l][0:128]
        pcol[l, 0:64, PC_QN + 1] = inp["mla_q_norm"][l][128:192]
        pcol[l, :, PC_KVN] = inp["mla_kv_norm"][l]
        pcol[l, :, PC_GQ] = inp["gqa_q_norm"][l][p % 64]
        pcol[l, :, PC_GK] = inp["gqa_k_norm"][l][p % 64]
        pcol[l, :, PC_GQS] = inp["gqa_q_norm"][l][partner]
        pcol[l, :, PC_GKS] = inp["gqa_k_norm"][l][partner]
        pcol[l, :, PC_CB:PC_CB + 2] = inp["lru_conv_b"][l].reshape(2, 128).T
        pcol[l, :, PC_CW:PC_CW + 8] = inp["lru_conv_w"][l].reshape(4, 2, 128).transpose(2, 1, 0).reshape(128, 8)
        pcol[l, :, PC_BA:PC_BA + 4] = inp["lru_ba"][l].reshape(2, 2, 128).transpose(2, 0, 1).reshape(128, 4)
        pcol[l, :, PC_BX:PC_BX + 4] = inp["lru_bx"][l].reshape(2, 2, 128).transpose(2, 0, 1).reshape(128, 4)
        pcol[l, :, PC_LAM:PC_LAM + 4] = inp["lru_lambda"][l].reshape(2, 2, 128).transpose(2, 0, 1).reshape(128, 4)
        prow[l, PR_LAM:PR_LAM + 128] = inp["diff_lambda"][l].reshape(-1)
        prow[l, PR_SUB:PR_SUB + 64] = inp["diff_subln"][l]
    lru_w = np.ascontiguousarray(np.stack([inp["lru_wa"], inp["lru_wx"]], axis=1)).astype(f)
    return pcol, prow, lru_w


_CACHE = {}


def run(inputs, seqs_per_core, n_cores, mixers="ABCD"):
    depth = inputs["w_in"].shape[0]
    f = np.float32
    xs = {"p": np.asarray(inputs["x_prompt"], f), "s": np.asarray(inputs["x_sample"], f)}
    cs_ = {"p": np.asarray(inputs["c_prompt"], f), "s": np.asarray(inputs["c_sample"], f)}
    seq_lens = [xs[w].shape[1] for (w, _) in seqs_per_core[0]]
    Smax = max(seq_lens)
    key = (tuple(seq_lens), depth, mixers)
    if key not in _CACHE:
        _CACHE[key] = build(seq_lens, depth, mixers)
    nc, _ = _CACHE[key]
    pcol, prow, lru_w = _host_layout({k: np.asarray(v, f) for k, v in inputs.items()}, depth)
    tab_mla, tab_diff, tab_gqa = _rope_tables(Smax)
    shared = {
        "ada_w": np.asarray(inputs["ada_w"], f), "ada_b": np.asarray(inputs["ada_b"], f),
        "w_in": np.asarray(inputs["w_in"], f), "w_out": np.asarray(inputs["w_out"], f),
        "mla_w_uq": np.asarray(inputs["mla_w_uq"], f), "mla_w_ukv": np.asarray(inputs["mla_w_ukv"], f),
        "lru_w": lru_w, "pcol": pcol, "prow": prow, "final_norm": np.asarray(inputs["final_norm"], f).reshape(1, -1),
        "ident_b": np.eye(128, dtype=f).astype(ml_dtypes.bfloat16), "ident_f": np.eye(128, dtype=f),
        "tab_mla": tab_mla, "tab_diff": tab_diff, "tab_gqa": tab_gqa,
    }
    in_maps = []
    for core in range(n_cores):
        m = dict(shared)
        cf = np.zeros((len(seq_lens), 128, 8), f)
        for i, (w, idx) in enumerate(seqs_per_core[core]):
            m[f"x{i}"] = np.ascontiguousarray(xs[w][idx])
            cf[i] = cs_[w][idx].reshape(8, 128).T
        m["cfm"] = cf
        in_maps.append(m)
    res = run_bass_kernel_spmd(nc, in_maps, core_ids=list(range(n_cores)))
    yp = np.zeros_like(xs["p"])
    ys = np.zeros_like(xs["s"])
    out = {"p": yp, "s": ys}
    for core in range(n_cores):
        for i, (w, idx) in enumerate(seqs_per_core[core]):
            out[w][idx] = res.results[core][f"y{i}"]
    return yp, ys


def kernel(x_prompt, x_sample, c_prompt, c_sample, **weights):
    inputs = dict(x_prompt=x_prompt, x_sample=x_sample, c_prompt=c_prompt, c_sample=c_sample, **weights)
    n_cores = 8
    bp = x_prompt.shape[0] // n_cores
    bs = x_sample.shape[0] // n_cores
    spc = [[("p", c * bp + j) for j in range(bp)] + [("s", c * bs + j) for j in range(bs)] for c in range(n_cores)]
    return run(inputs, spc, n_cores)
```

```python
import contextlib
import math
import numpy as np
import ml_dtypes
import concourse.bass as bass
import concourse.mybir as mybir
from concourse.bass_utils import run_bass_kernel_spmd

F32 = mybir.dt.float32
BF16 = mybir.dt.bfloat16
AF = mybir.ActivationFunctionType
ALU = mybir.AluOpType
AX = mybir.AxisListType

D = 1024
NIN = 2912
EPS = 1e-6
OFF = dict(qa=0, kva=192, kra=320, ga=352, qb=608, kb=864, vb=1120, gb=1376,
           xc=1632, gc=1888, qd=2144, kd=2400, vd=2528, gd=2656)
NPC = 64
PC_NG = 0
PC_ADAB = 8
PC_QN = 32
PC_KVN = 34
PC_GQ = 35
PC_GK = 36
PC_GQS = 37
PC_GKS = 38
PC_CB = 39
PC_CW = 41
PC_BA = 49
PC_BX = 53
PC_LAM = 57
PR_LAM = 0
PR_SUB = 128
NPR = 192


class Res:
    __slots__ = ("name", "w", "r", "excl")

    def __init__(self, name, excl=False):
        self.name = name
        self.w = {}
        self.r = {}
        self.excl = excl


class Prog:
    ENGS = ("pe", "act", "dve", "pool", "sp")

    def __init__(self, nc):
        self.nc = nc
        self.count = {e: 0 for e in self.ENGS}
        self.waited = {e: {} for e in self.ENGS}
        self.stream = {e: [] for e in self.ENGS}
        self.dma_count = {}
        self.n_ins = 0

    def _deps(self, eng, reads, writes):
        need = {}

        def add(s, v, same_ok):
            if s == eng and eng in ("pe", "sp"):
                return
            if s in self.dma_count:
                v = self.dma_count[s]
            if need.get(s, 0) < v:
                need[s] = v
        for r in reads:
            for s, v in r.w.items():
                add(s, v, False)
            if r.excl:
                for s, v in r.r.items():
                    if s != eng:
                        add(s, v, False)
        for w in writes:
            for s, v in w.w.items():
                add(s, v, True)
            for s, v in w.r.items():
                add(s, v, True)
        out = []
        for s, v in need.items():
            if self.waited[eng].get(s, 0) >= v:
                continue
            self.waited[eng][s] = v
            out.append((s, v))
        return out

    def _commit(self, tok, reads, writes):
        s, v = tok
        for r in reads:
            if r.r.get(s, 0) < v:
                r.r[s] = v
        for w in writes:
            w.w[s] = v
            w.r = {}

    def op(self, eng, fn, reads=(), writes=(), signal=True):
        waits = self._deps(eng, reads, writes)
        if signal:
            self.count[eng] += 1
            tok = (eng, self.count[eng])
        else:
            tok = (eng, self.count[eng] + 1)
        self._commit(tok, reads, writes)
        self.stream[eng].append((waits, fn, eng if signal else None, 1))
        self.n_ins += 1

    def dma(self, queue, key, out, in_, reads=(), writes=(), **kw):
        waits = self._deps(queue, reads, writes)
        self.dma_count[key] = self.dma_count.get(key, 0) + 16
        tok = (key, self.dma_count[key])
        self._commit(tok, reads, writes)

        def fn(e, out=out, in_=in_, kw=kw):
            return e.dma_start(out=out, in_=in_, **kw)
        self.stream[queue].append((waits, fn, key, 16))
        self.n_ins += 1

    def wait_all(self, eng, res_list):
        waits = self._deps(eng, (), res_list)
        self.stream[eng].append((waits, None, None, 0))

    def emit(self):
        nc = self.nc
        with contextlib.ExitStack() as st:
            sems = {}
            for e in self.ENGS:
                sems[e] = st.enter_context(nc.semaphore("s_" + e))
            for k in self.dma_count:
                sems[k] = st.enter_context(nc.semaphore("d_" + k))
            block = st.enter_context(nc.Block())

            def replay(ename):
                def body(e):
                    for waits, fn, sig, inc in self.stream[ename]:
                        for s, v in waits:
                            e.wait_ge(sems[s], v)
                        if fn is None:
                            continue
                        ins = fn(e)
                        if sig is not None:
                            ins.then_inc(sems[sig], inc)
                return body

            block.tensor(replay("pe"))
            block.scalar(replay("act"))
            block.vector(replay("dve"))
            block.gpsimd(replay("pool"))
            block.sync(replay("sp"))


class Ring:
    def __init__(self, items):
        self.items = items
        self.i = 0

    def next(self):
        it = self.items[self.i % len(self.items)]
        self.i += 1
        return it


def build(seq_lens, depth, mixers="ABCD"):
    nc = bass.Bass("TRN2", target_bir_lowering=False)
    nseq = len(seq_lens)
    Smax = max(seq_lens)
    NTmax = Smax // 128

    def din(name, shape, dt=F32):
        return nc.dram_tensor(name, list(shape), dt, kind="ExternalInput").ap()

    x_in = [din(f"x{i}", [S, D]) for i, S in enumerate(seq_lens)]
    y_out = [nc.dram_tensor(f"y{i}", [S, D], F32, kind="ExternalOutput").ap() for i, S in enumerate(seq_lens)]
    cfm = din("cfm", [nseq, 128, 8])
    ada_w = din("ada_w", [depth, D, 3 * D])
    ada_b = din("ada_b", [depth, 3 * D])
    w_in = din("w_in", [depth, D, NIN])
    w_out = din("w_out", [depth, D, D])
    w_uq = din("mla_w_uq", [depth, 192, 384])
    w_ukv = din("mla_w_ukv", [depth, 128, 512])
    lru_w = din("lru_w", [depth, 2, 2, 4, 64, 64])
    pcol = din("pcol", [depth, 128, NPC])
    prow = din("prow", [depth, NPR])
    fnorm = din("final_norm", [1, D])
    ident_b_d = din("ident_b", [128, 128], BF16)
    ident_f_d = din("ident_f", [128, 128])
    tab_mla = din("tab_mla", [2, 32, Smax])
    tab_diff = din("tab_diff", [2, 128, Smax])
    tab_gqa = din("tab_gqa", [2, 128, Smax])

    xres = [nc.dram_tensor(f"xres{i}", [S, D], F32).ap() for i, S in enumerate(seq_lens)]
    oscr = [nc.dram_tensor(f"oscr{i}", [S, 768], BF16).ap() for i, S in enumerate(seq_lens)]
    ocscr = [nc.dram_tensor(f"ocscr{i}", [256, S], BF16).ap() for i, S in enumerate(seq_lens)]

    P = Prog(nc)
    st = contextlib.ExitStack()
    with st:
        sb_bytes = [0]

        def sb(name, shape, dt=F32):
            n = 1
            for d_ in shape[1:]:
                n *= d_
            sb_bytes[0] += n * (2 if dt == BF16 else 4)
            return st.enter_context(nc.sbuf_tensor(name, list(shape), dt))

        ident_b = sb("ident_b_s", [128, 128], BF16)
        ident_f = sb("ident_f_s", [128, 65])
        ones_b = sb("ones_b", [128, 128], BF16)
        bones_b = sb("bones_b", [128, 128], BF16)
        epsc = sb("epsc", [128, 1])
        h_fm = sb("h_fm", [128, 8, Smax], BF16)
        wmix = sb("wmix", [128, 8, 1536], BF16)
        wst = [sb(f"wst{i}", [128, 1024]) for i in range(2)]
        qbuf = sb("qbuf", [128, 2, Smax], BF16)
        kbuf = sb("kbuf", [128, 2, Smax], BF16)
        lat = sb("lat", [128, Smax], BF16)
        vaug = sb("vaug", [128, NTmax, 260], BF16)
        xt = [sb(f"xt{i}", [128, 1024]) for i in range(2)]
        tabc = [sb(f"tabc{i}", [128, 512]) for i in range(1)]
        tabs = [sb(f"tabs{i}", [128, 512]) for i in range(1)]
        ptbuf = sb("ptbuf", [128, 4, 512], BF16)
        qz = sb("qz", [128, 4, 512], BF16)
        pts = [ptbuf[:, i, :] for i in range(4)]
        xn = ptbuf[:, 0:2, :].rearrange("p a n -> p (a n)")
        tmpf = [sb(f"tmpf{i}", [128, 512]) for i in range(4)]
        tmpb = [sb(f"tmpb{i}", [128, 512], BF16) for i in range(3)]
        tmpo = [sb(f"tmpo{i}", [128, 512]) for i in range(2)]
        tmpg = [sb(f"tmpg{i}", [128, 256]) for i in range(2)]
        ogs = [sb(f"og{i}", [128, 256], BF16) for i in range(2)]
        pc = sb("pc", [128, NPC])
        pr = sb("pr", [128, NPR])
        sc = sb("sc", [128, 8])
        sc_bc = sb("sc_bc", [128, 8, 128])
        fn_bc = sc_bc[:].rearrange("p k n -> p (k n)")
        shiftcol = sb("shiftcol", [128, 8])
        gscol = sb("gscol", [128, 8])
        gate_bc = sb("gate_bc", [128, 1024])
        small = sb("small", [128, 64])
        wuq = sb("wuq", [128, 2, 384], BF16)
        wuqs = sb("wuqs", [128, 2, 384], BF16)
        wukv = sb("wukv", [128, 512], BF16)
        lamc = sb("lamc", [128, 4])
        subln_bc = sb("subln_bc", [128, 4, 64])
        lruw = sb("lruw", [128, 2, 2, 2, 128], BF16)
        lsp = sb("lsp", [128, 12])

        import os as _os0
        if _os0.environ.get("KDEBUG"):
            print("SBUF bytes/partition:", sb_bytes[0])
        ps = [st.enter_context(nc.psum_tensor(f"ps{i}", [128, 512], F32)) for i in range(8)]
        r_ps = [Res(f"ps{i}", excl=True) for i in range(8)]
        SB_BANKS = (0, 1, 2, 3)
        O_BANKS = (4, 5)
        O_ALL = (4, 5, 6, 7)
        misc = Ring([6, 7])

        r_const = Res("const")
        r_h = [Res(f"h{t}") for t in range(NTmax)]
        r_wmix = Res("wmix")
        r_wst = [Res("wst0"), Res("wst1")]
        wst_ring = Ring([0, 1])
        NCmax = Smax // 512 if Smax >= 512 else 1
        r_q = [[Res(f"q{s}_{c}") for c in range(NTmax)] for s in range(2)]
        r_k = [[Res(f"k{s}_{c}") for c in range(NTmax)] for s in range(2)]
        r_lat = [Res(f"lat{c}") for c in range(NTmax)]
        r_v = [Res(f"v{c}") for c in range(NTmax)]
        r_xt = [Res("xt0"), Res("xt1")]
        xt_ring = Ring([0, 1])
        r_xn = Res("xn")
        r_tab = [Res("tab0"), Res("tab1")]
        tab_ring = Ring([0])
        r_pt = [Res(f"pt{i}") for i in range(4)]
        r_qz = [Res(f"qz{i}") for i in range(4)]
        r_tmpf = [Res(f"tmpf{i}") for i in range(4)]
        tmpf_ring = Ring([0, 1, 2, 3])
        r_tmpb = [Res(f"tmpb{i}") for i in range(3)]
        r_tmpo = [Res("tmpo0"), Res("tmpo1")]
        tmpo_ring = Ring([0, 1])
        r_tmpg = [Res("tmpg0"), Res("tmpg1")]
        tmpg_ring = Ring([0, 1])
        tmpb_ring = Ring([0, 1, 2])
        r_og = [Res("og0"), Res("og1")]
        og_ring = Ring([0, 1])
        r_pc = Res("pc")
        r_mod = Res("mod")
        r_gate = Res("gate")
        r_small = Res("small")
        rcol = [small[:, 44:45], small[:, 46:47]]
        r_rcol = [Res("rcol0"), Res("rcol1")]
        r_wa = Res("wa")
        r_ot = Res("ot")
        r_ofm = Res("ofm")
        r_oscr = [[Res(f"oscr{i}_{t}") for t in range(S // 128)] for i, S in enumerate(seq_lens)]
        r_ocscr = [Res(f"ocscr{i}") for i in range(nseq)]
        r_xres = [[Res(f"xres{i}_{t}") for t in range(S // 128)] for i, S in enumerate(seq_lens)]
        r_y = Res("y")

        def MM(out, lhsT, rhs, start, stop, rd, wr, sig=True, **kw):
            P.op("pe", lambda e: e.matmul(out, lhsT=lhsT, rhs=rhs, start=start, stop=stop, **kw),
                 reads=rd, writes=wr, signal=sig)

        def TR(out, in_, ident, rd, wr, sig=True):
            P.op("pe", lambda e: e.transpose(out, in_, ident), reads=rd, writes=wr, signal=sig)

        def ACT(out, in_, func, rd, wr, scale=1.0, bias=None, accum=None):
            def fn(e):
                kw = {}
                if bias is not None:
                    kw["bias"] = bias
                if accum is not None:
                    kw["accum_out"] = accum
                return e.activation(out=out, in_=in_, func=func, scale=scale, **kw)
            P.op("act", fn, reads=rd, writes=wr)

        def eng_of(P_, name):
            return name

        def TT(eng, out, in0, in1, op, rd, wr):
            P.op(eng, lambda e: e.tensor_tensor(out=out, in0=in0, in1=in1, op=op), reads=rd, writes=wr)

        def TS(eng, out, in0, s1, s2, op0, op1, rd, wr):
            if op1 is None:
                P.op(eng, lambda e: e.tensor_scalar(out=out, in0=in0, scalar1=s1, scalar2=None, op0=op0), reads=rd, writes=wr)
            else:
                P.op(eng, lambda e: e.tensor_scalar(out=out, in0=in0, scalar1=s1, scalar2=s2, op0=op0, op1=op1), reads=rd, writes=wr)

        def STT(out, in0, scalar, in1, op0, op1, rd, wr):
            P.op("dve", lambda e: e.scalar_tensor_tensor(out=out, in0=in0, scalar=scalar, in1=in1, op0=op0, op1=op1),
                 reads=rd, writes=wr)

        def CP(eng, out, in_, rd, wr):
            P.op(eng, lambda e: e.tensor_copy(out=out, in_=in_), reads=rd, writes=wr)

        def RCP(out, in_, rd, wr):
            P.op("dve", lambda e: e.reciprocal(out=out, in_=in_), reads=rd, writes=wr)

        def MEMSET(eng, ap, val, wr):
            P.op(eng, lambda e: e.memset(ap, val), writes=wr)

        dma_i = [0]
        import os as _os
        ROPE_ENG = _os.environ.get("ROPE_ENG", "dve")

        def LOAD(key, out, in_, rd, wr, **kw):
            P.dma("sp", key, out, in_, reads=rd, writes=wr, **kw)

        def rstd_act(out, in_, n, rd, wr):
            ACT(out, in_, AF.Ln, list(rd) + [r_const], wr, scale=1.0 / n, bias=epsc[0:out.shape[0], 0:1])
            ACT(out, out, AF.Exp, wr, wr, scale=-0.5)

        LOAD("c0", ident_b[:], ident_b_d, [], [r_const])
        LOAD("c0", ident_f[:], ident_f_d[:, 0:65], [], [r_const])
        MEMSET("pool", ones_b[:], 1.0, [r_const])
        MEMSET("pool", bones_b[:], 0.0, [r_const])
        MEMSET("pool", bones_b[0:64, 0:64], 1.0, [r_const])
        MEMSET("pool", bones_b[64:128, 64:128], 1.0, [r_const])
        MEMSET("pool", epsc[:], EPS, [r_const])
        MEMSET("pool", vaug[:], 1.0, r_v)
        MEMSET("pool", qz[:], 0.0, r_qz)

        if len(mixers) < 4:
            zt = sb("zt", [128, 768], BF16)
            r_zt = Res("zt")
            MEMSET("pool", zt[:], 0.0, [r_zt])
            for i_, S_ in enumerate(seq_lens):
                for t_ in range(S_ // 128):
                    P.dma("sp", "zst", oscr[i_][t_ * 128:(t_ + 1) * 128, :], zt[:], reads=[r_zt], writes=[r_oscr[i_][t_]])
                    for cc_ in range(2):
                        P.dma("sp", "zst", ocscr[i_][cc_ * 128:(cc_ + 1) * 128, t_ * 128:(t_ + 1) * 128], zt[:, 0:128], reads=[r_zt], writes=[r_ocscr[i_]])

        def load_w(src2d, ncols, dst_col, swap=None, rows=D, dst=None, gaincol=None):
            nk = rows // 128
            i = wst_ring.next()
            stg = wst[i][:, 0:nk * ncols].rearrange("p (k n) -> p k n", k=nk)
            LOAD(f"wst{i}", stg, src2d.rearrange("(k p) n -> p k n", p=128), [], [r_wst[i]])
            return i, stg

        cast_i = [0]

        def cast_cols(stg_ap, i, dst_ap, eng=None):
            cast_i[0] += 1
            if cast_i[0] % 2 == 0:
                ACT(dst_ap, stg_ap, AF.Copy, [r_wst[i]], [r_wmix])
            else:
                CP("dve", dst_ap, stg_ap, [r_wst[i]], [r_wmix])

        def load_cast(l, src_c0, n, dst_c0, extra=None):
            for p0 in range(0, n, 128):
                pn = min(128, n - p0)
                i, stg = load_w(w_in[l][:, src_c0 + p0:src_c0 + p0 + pn], pn, 0)
                cast_cols(stg, i, wmix[:, :, dst_c0 + p0:dst_c0 + p0 + pn])
                if extra is not None:
                    extra(i, stg, p0, pn)

        def swapped(ap3, half):
            return ap3.rearrange("p k (m b j) -> p k m b j", b=2, j=half)

        def layer_prep(l):
            LOAD("c1", pc[:], pcol[l], [r_pc, r_mod, r_wa], [r_pc])
            LOAD("c1", pr[:], prow[l:l + 1, :].partition_broadcast(128), [r_pc], [r_pc])
            if "A" in mixers:
                i = wst_ring.next()
                LOAD(f"wst{i}", wst[i][:, 0:384], w_uq[l][0:128, :], [], [r_wst[i]])
                LOAD(f"wst{i}", wst[i][0:64, 384:768], w_uq[l][128:192, :], [], [r_wst[i]])
                for kk, rows in ((0, 128), (1, 64)):
                    TS("pool", wuq[0:rows, kk, :], wst[i][0:rows, kk * 384:(kk + 1) * 384], pc[0:rows, PC_QN + kk:PC_QN + kk + 1], None,
                       ALU.mult, None, [r_wst[i], r_pc], [r_wa])
                    CP("pool", wuqs[0:rows, kk, :], wuq[0:rows, kk, :], [r_wa], [r_wa])
                    v = wuq[0:rows, kk, :].rearrange("p (h c) -> p h c", c=96)
                    vs = wuqs[0:rows, kk, :].rearrange("p (h c) -> p h c", c=96)
                    CP("pool", vs[:, :, 64:80], v[:, :, 80:96], [r_wa], [r_wa])
                    CP("pool", vs[:, :, 80:96], v[:, :, 64:80], [r_wa], [r_wa])
                i = wst_ring.next()
                LOAD(f"wst{i}", wst[i][:, 0:512], w_ukv[l], [], [r_wst[i]])
                sv = wst[i][:, 0:512].rearrange("p (h t c) -> p h t c", h=4, t=2)
                dv_ = wukv[:, :].rearrange("p (t h c) -> p t h c", t=2, h=4)
                for t_ in range(2):
                    TS("pool", dv_[:, t_], sv[:, :, t_, :], pc[:, PC_KVN:PC_KVN + 1], None, ALU.mult, None,
                       [r_wst[i], r_pc], [r_wa])
            if "B" in mixers:
                lam_init = 0.8 - 0.6 * math.exp(-0.3 * l)
                lv = pr[:, PR_LAM:PR_LAM + 128].rearrange("p (a d) -> p a d", a=4)
                TT("dve", small[:, 0:32], lv[:, 0, :], lv[:, 1, :], ALU.mult, [r_pc], [r_small])
                TT("dve", small[:, 32:64], lv[:, 2, :], lv[:, 3, :], ALU.mult, [r_pc], [r_small])
                P.op("dve", lambda e: e.tensor_reduce(out=lamc[:, 0:2], in_=small[:, 0:64].rearrange("p (a d) -> p a d", a=2),
                                                     axis=AX.X, op=ALU.add), reads=[r_small], writes=[r_wa])
                ACT(lamc[:, 0:2], lamc[:, 0:2], AF.Exp, [r_wa], [r_wa])
                TT("dve", lamc[:, 2:3], lamc[:, 0:1], lamc[:, 1:2], ALU.subtract, [r_wa], [r_wa])
                TS("dve", lamc[:, 3:4], lamc[:, 2:3], -1.0, -lam_init, ALU.mult, ALU.add, [r_wa], [r_wa])
                for qb in range(4):
                    TS("dve", subln_bc[:, qb, :], pr[:, PR_SUB:PR_SUB + 64], 1.0 - lam_init, None, ALU.mult, None,
                       [r_pc], [r_wa])
            if "C" in mixers:
                i = wst_ring.next()
                MEMSET("pool", wst[i][:, 0:1024], 0.0, [r_wst[i]])
                for m in range(2):
                    for d_ in range(2):
                        for hh in range(4):
                            cc, hb = hh // 2, (hh % 2) * 64
                            col = ((m * 2 + d_) * 2 + cc) * 128 + hb
                            LOAD(f"wst{i}", wst[i][hb:hb + 64, col:col + 64], lru_w[l, m, d_, hh], [], [r_wst[i]])
                CP("pool", lruw[:].rearrange("p m d c n -> p (m d c n)"), wst[i][:, 0:1024], [r_wst[i]], [r_wa])
                ACT(lsp[:, 0:4], pc[:, PC_LAM:PC_LAM + 4], AF.Exp, [r_pc], [r_wa], scale=-1.0)
                ACT(lsp[:, 0:4], lsp[:, 0:4], AF.Ln, [r_wa], [r_wa], bias=1.0)
                TS("dve", lsp[:, 0:4], lsp[:, 0:4], -8.0, None, ALU.mult, None, [r_wa], [r_wa])
                TS("dve", lsp[:, 4:8], pc[:, PC_BA:PC_BA + 4], -1.0, None, ALU.mult, None, [r_pc], [r_wa])
                TS("dve", lsp[:, 8:12], pc[:, PC_BX:PC_BX + 4], -1.0, None, ALU.mult, None, [r_pc], [r_wa])

        def phase0(l, s):
            LOAD("c2", sc[:], cfm[s], [r_mod], [r_mod])
            ACT(small[:, 8:16], sc[:], AF.Exp, [r_mod], [r_small], scale=-1.0)
            TS("dve", small[:, 8:16], small[:, 8:16], 1.0, None, ALU.add, None, [r_small], [r_small])
            RCP(small[:, 8:16], small[:, 8:16], [r_small], [r_small])
            TT("dve", sc[:], sc[:], small[:, 8:16], ALU.mult, [r_small, r_mod], [r_mod])
            CP("dve", sc_bc[:], sc[:, :].unsqueeze(2).broadcast_to([128, 8, 128]), [r_mod], [r_mod])
            mb = misc.next()
            for j in range(24):
                i, stg = load_w(ada_w[l][:, j * 128:(j + 1) * 128], 128, 0)
                if j < 16:
                    for k in range(8):
                        MM(ps[mb][:, j * 2:j * 2 + 2], stg[:, k, :], sc_bc[:, k, 0:2],
                           k == 0, k == 7, [r_wst[i], r_mod], [r_ps[mb]], sig=(k == 7))
                    if j == 15:
                        pv = ps[mb][:, 0:32].rearrange("p (j t) -> p j t", t=2)
                        TT("dve", shiftcol[:], pv[:, 0:8, 0], pc[:, PC_ADAB:PC_ADAB + 8], ALU.add, [r_ps[mb], r_pc], [r_mod])
                        TT("dve", gscol[:], pv[:, 8:16, 0], pc[:, PC_ADAB + 8:PC_ADAB + 16], ALU.add, [r_ps[mb], r_pc], [r_mod])
                        STT(gscol[:], gscol[:], 1.0, pc[:, PC_NG:PC_NG + 8], ALU.add, ALU.mult, [r_mod, r_pc], [r_mod])
                else:
                    gb = misc.next()
                    c0 = (j - 16) * 128
                    a = tmpf_ring.next()
                    LOAD(f"abg{a}", tmpf[a][:, 0:128], ada_b[l:l + 1, 2 * D + c0:2 * D + c0 + 128].partition_broadcast(128),
                         [], [r_tmpf[a]])
                    for k in range(8):
                        MM(ps[gb][:, 0:128], sc_bc[:, k, :], stg[:, k, :], k == 0, k == 7, [r_wst[i], r_mod], [r_ps[gb]], sig=(k == 7))
                    TT("dve", gate_bc[:, c0:c0 + 128], ps[gb][:, 0:128], tmpf[a][:, 0:128], ALU.add,
                       [r_ps[gb], r_tmpf[a]], [r_gate])

        def phase1(l, s):
            S = seq_lens[s]
            src = x_in[s] if l == 0 else xres[s]
            misc.items = [6, 7, 0, 1, 2, 3]
            for t in range(S // 128):
                xi = xt_ring.next()
                rd = [] if l == 0 else [r_xres[s][t]]
                LOAD(f"xt{xi}", xt[xi][:], src[t * 128:(t + 1) * 128, :], rd, [r_xt[xi]])
                rc, r_rc = rcol[t % 2], r_rcol[t % 2]
                xnb = ptbuf[:, 2 * (t % 2):2 * (t % 2) + 2, :].rearrange("p a n -> p (a n)")
                r_xnb = [r_pt[2 * (t % 2)], r_pt[2 * (t % 2) + 1]]
                ACT(tmpf[0][:].bitcast(BF16), xt[xi][:], AF.Square, [r_xt[xi]], [r_tmpf[0], r_rc], accum=rc)
                rstd_act(rc, rc, float(D), [r_rc], [r_rc])
                ACT(xnb, xt[xi][:], AF.Copy, [r_xt[xi], r_rc], r_xnb, scale=rc)
                mb = misc.next()
                pT = ps[mb][:].bitcast(BF16)
                for k in range(8):
                    TR(pT[:, k * 128:(k + 1) * 128], xnb[:, k * 128:(k + 1) * 128], ident_b[:], r_xnb + [r_const], [r_ps[mb]], sig=(k == 7))
                for k in range(8):
                    TS("dve", h_fm[:, k, t * 128:(t + 1) * 128], pT[:, k * 128:(k + 1) * 128], gscol[:, k:k + 1], shiftcol[:, k:k + 1],
                       ALU.mult, ALU.add, [r_ps[mb], r_mod], [r_h[t]])
            misc.items = [6, 7]

        def proj_fm(S, c, wcol, rows=128, wrows=None):
            cw = min(512, S)
            mb = misc.next()
            tl = [r_h[t] for t in range(c * cw // 128, (c + 1) * cw // 128)]
            for k in range(8):
                MM(ps[mb][0:rows, 0:cw], wmix[:, k, wcol:wcol + rows], h_fm[:, k, c * cw:(c + 1) * cw], k == 0, k == 7,
                   [r_wmix] + tl, [r_ps[mb]], sig=(k == 7))
            return mb

        def load_tab(tab, c, cw, rows=slice(0, 128), trows=None):
            ti = tab_ring.next()
            nrow = rows.stop - rows.start
            LOAD(f"tab{ti}", tabc[ti][rows, 0:cw], tab[0, 0:nrow, c * cw:(c + 1) * cw], [], [r_tab[ti]])
            LOAD(f"tab{ti}", tabs[ti][rows, 0:cw], tab[1, 0:nrow, c * cw:(c + 1) * cw], [], [r_tab[ti]])
            return ti

        def rope_evac(mb, mbs, ti, rows, cw, out_ap, wr, gcol=None, gscol_=None, rstd=None, rstd_rd=()):
            a = tmpf_ring.next()
            b = tmpf_ring.next()
            if gcol is None:
                TT("dve", tmpf[a][rows, 0:cw], ps[mb][rows, 0:cw], tabc[ti][rows, 0:cw], ALU.mult, [r_ps[mb], r_tab[ti]], [r_tmpf[a]])
                TT("dve", tmpf[b][rows, 0:cw], ps[mbs][rows, 0:cw], tabs[ti][rows, 0:cw], ALU.mult, [r_ps[mbs], r_tab[ti]], [r_tmpf[b]])
            else:
                TS("dve", tmpf[a][rows, 0:cw], ps[mb][rows, 0:cw], gcol, None, ALU.mult, None, [r_ps[mb], r_pc], [r_tmpf[a]])
                TT("dve", tmpf[a][rows, 0:cw], tmpf[a][rows, 0:cw], tabc[ti][rows, 0:cw], ALU.mult, [r_tmpf[a], r_tab[ti]], [r_tmpf[a]])
                TS("dve", tmpf[b][rows, 0:cw], ps[mbs][rows, 0:cw], gscol_, None, ALU.mult, None, [r_ps[mbs], r_pc], [r_tmpf[b]])
                TT("dve", tmpf[b][rows, 0:cw], tmpf[b][rows, 0:cw], tabs[ti][rows, 0:cw], ALU.mult, [r_tmpf[b], r_tab[ti]], [r_tmpf[b]])
            if rstd is None:
                TT(ROPE_ENG, out_ap, tmpf[a][rows, 0:cw], tmpf[b][rows, 0:cw], ALU.add, [r_tmpf[a], r_tmpf[b]], wr)
            else:
                TT(ROPE_ENG, tmpf[a][rows, 0:cw], tmpf[a][rows, 0:cw], tmpf[b][rows, 0:cw], ALU.add, [r_tmpf[a], r_tmpf[b]], [r_tmpf[a]])
                TT(ROPE_ENG, out_ap, tmpf[a][rows, 0:cw], rstd, ALU.mult, [r_tmpf[a]] + list(rstd_rd), wr)

        def run_attention(S, groups):
            cw = min(512, S)
            NC = S // cw
            NKT = S // 128
            steps = []
            oset = [0]
            for g in groups:
                nu = len(g["units"])
                for c in range(NC):
                    if nu == 1:
                        bs = (O_BANKS[oset[0] % 2],)
                        oset[0] += 1
                    else:
                        bs = O_BANKS
                    for ui, u in enumerate(g["units"]):
                        for kt in range(0, NKT, 2):
                            kts = [kt] if kt + 1 >= NKT else [kt, kt + 1]
                            steps.append(dict(g=g, c=c, lanes=[(u, k_, bs[ui]) for k_ in kts],
                                              first=(ui == 0 and kt == 0 and c == 0), cfirst=(ui == 0 and kt == 0),
                                              last=(ui == nu - 1 and kts[-1] == NKT - 1), obs=bs))

            def emit_S(i):
                sp_ = steps[i]
                if sp_["first"] and sp_["g"].get("pre") is not None:
                    sp_["g"]["pre"]()
                if sp_["cfirst"] and sp_["g"].get("prec") is not None:
                    sp_["g"]["prec"](sp_["c"])
                for j, (u, kt, ob) in enumerate(sp_["lanes"]):
                    qap, qres = u["q"](sp_["c"])
                    bank = SB_BANKS[(2 * i + j) % 4]
                    kap, kres = u["k"](kt)
                    MM(ps[bank][:, 0:cw], kap, qap, True, True, list(kres) + list(qres), [r_ps[bank]])

            pending = []
            seq_ = [0]
            cur = [0]

            def defer(fn, delay):
                seq_[0] += 1
                pending.append((cur[0] + delay, seq_[0], fn))

            def run_due(i, flush=False):
                pending.sort(key=lambda x: (x[0], x[1]))
                while pending and (flush or pending[0][0] <= i):
                    pending.pop(0)[2]()
            DEFER[0] = defer

            emit_S(0)
            for i, sp_ in enumerate(steps):
                cur[0] = i
                run_due(i)
                if i + 1 < len(steps):
                    emit_S(i + 1)
                nl = len(sp_["lanes"])
                for j, (u, kt, ob) in enumerate(sp_["lanes"]):
                    bank = SB_BANKS[(2 * i + j) % 4]
                    pi = (2 * i + j) % 4
                    ACT(pts[pi][:, 0:cw], ps[bank][:, 0:cw], AF.Exp, [r_ps[bank]], [r_pt[pi]], scale=u["scale"])
                for j, (u, kt, ob) in enumerate(sp_["lanes"]):
                    pi = (2 * i + j) % 4
                    vap, vres = u["v"](kt)
                    MM(ps[ob][0:65, 0:cw], vap, pts[pi][:, 0:cw], kt == 0, kt == NKT - 1, [r_pt[pi]] + list(vres),
                       [r_ps[ob]], sig=(kt == NKT - 1 or j == nl - 1))
                if sp_["cfirst"] and sp_["g"].get("cstart") is not None:
                    sp_["g"]["cstart"](sp_["c"])
                if sp_["last"]:
                    run_due(i, flush=True)
                    sp_["g"]["epilogue"](sp_["c"], sp_["obs"])
            run_due(0, flush=True)

        DEFER = [None]

        def gate_early(S, gcol, width, state):
            def cstart(c):
                cw = min(512, S)
                n = (cw // 128) * width
                gb = gate_mm(S, c, gcol, width)

                def fin():
                    a = tmpg_ring.next()
                    ACT(tmpg[a][:, 0:n], ps[gb][:, 0:n], AF.Exp, [r_ps[gb]], [r_tmpg[a]], scale=-1.0)
                    TS("dve", tmpg[a][:, 0:n], tmpg[a][:, 0:n], 1.0, None, ALU.add, None, [r_tmpg[a]], [r_tmpg[a]])
                    RCP(tmpg[a][:, 0:n], tmpg[a][:, 0:n], [r_tmpg[a]], [r_tmpg[a]])
                    TT("dve", tmpg[a][:, 0:n], ps[gb][:, 0:n], tmpg[a][:, 0:n], ALU.mult, [r_ps[gb], r_tmpg[a]], [r_tmpg[a]])
                    state[c] = a
                DEFER[0](fin, 2 if (S // 128) >= 8 else 1)
            return cstart

        def o_copy(ob, cw):
            a = tmpo_ring.next()
            CP("dve", tmpo[a][0:65, 0:cw], ps[ob][0:65, 0:cw], [r_ps[ob]], [r_tmpo[a]])
            return a

        def o_transposed(a, cw):
            mb = misc.next()
            nqb = cw // 128
            for qb in range(nqb):
                TR(ps[mb][:, qb * 65:(qb + 1) * 65], tmpo[a][0:65, qb * 128:(qb + 1) * 128], ident_f[0:65, 0:65],
                   [r_tmpo[a], r_const], [r_ps[mb]], sig=(qb == nqb - 1))
            return mb, ps[mb][:, 0:nqb * 65].rearrange("p (q d) -> p q d", d=65)

        def gate_mm(S, c, gcol, width):
            cw = min(512, S)
            nqb = cw // 128
            gb = misc.next()
            for qb in range(nqb):
                t = c * nqb + qb
                for k in range(8):
                    MM(ps[gb][:, qb * width:(qb + 1) * width], h_fm[:, k, t * 128:(t + 1) * 128], wmix[:, k, gcol:gcol + width],
                       k == 0, k == 7, [r_h[t], r_wmix], [r_ps[gb]], sig=(qb == nqb - 1 and k == 7))
            return gb

        def gate_fin(gb, n):
            a = tmpg_ring.next()
            ACT(tmpg[a][:, 0:n], ps[gb][:, 0:n], AF.Exp, [r_ps[gb]], [r_tmpg[a]], scale=-1.0)
            TS("dve", tmpg[a][:, 0:n], tmpg[a][:, 0:n], 1.0, None, ALU.add, None, [r_tmpg[a]], [r_tmpg[a]])
            RCP(tmpg[a][:, 0:n], tmpg[a][:, 0:n], [r_tmpg[a]], [r_tmpg[a]])
            TT("dve", tmpg[a][:, 0:n], ps[gb][:, 0:n], tmpg[a][:, 0:n], ALU.mult, [r_ps[gb], r_tmpg[a]], [r_tmpg[a]])
            return a

        def store_o(s, c, cw, col, og_i, width=64):
            nqb = cw // 128
            dst = oscr[s][c * cw:(c + 1) * cw, col:col + width].rearrange("(q p) d -> p q d", p=128)
            src = ogs[og_i][:, 0:nqb * width].rearrange("p (q d) -> p q d", d=width)
            tl = [r_oscr[s][c * nqb + q] for q in range(nqb)]
            P.dma("pool", f"ost{og_i}", dst, src, reads=[r_og[og_i]], writes=tl)

        def std_epilogue(s, S, col0, h, state, gw=64, goff=0):
            def ep(c, obs):
                cw = min(512, S)
                nqb = cw // 128
                n = nqb * 64
                oc_ = o_copy(obs[0], cw)
                mb, ov = o_transposed(oc_, cw)

                def partB():
                    sc_ = 20 + 4 * (h % 2)
                    RCP(small[:, sc_:sc_ + nqb], ov[:, :, 64], [r_ps[mb]], [r_small])
                    a = tmpf_ring.next()
                    av = tmpf[a][:, 0:n].rearrange("p (q d) -> p q d", d=64)
                    TT("dve", av, ov[:, :, 0:64], small[:, sc_:sc_ + nqb].unsqueeze(2).broadcast_to([128, nqb, 64]), ALU.mult,
                       [r_ps[mb], r_small], [r_tmpf[a]])
                    sg = state[c]
                    sgv = tmpg[sg][:, 0:nqb * gw].rearrange("p (q d) -> p q d", d=gw)[:, :, goff:goff + 64]
                    oi = og_ring.next()
                    TT("pool", ogs[oi][:, 0:n].rearrange("p (q d) -> p q d", d=64), av, sgv, ALU.mult,
                       [r_tmpf[a], r_tmpg[sg]], [r_og[oi]])
                    store_o(s, c, cw, col0 + h * 64, oi)
                DEFER[0](partB, 1)
            return ep

        def mixer_D(l, s):
            S = seq_lens[s]
            cw = min(512, S)
            NC = S // cw
            P.wait_all("pool", [r_wmix])
            def q_extra(i, stg, p0, pn):
                sv = swapped(stg, 16)
                dvw = swapped(wmix[:, :, 256 + p0:256 + p0 + pn], 16)
                CP("pool", dvw[:, :, :, 0, :], sv[:, :, :, 1, :], [r_wst[i]], [r_wmix])
                CP("pool", dvw[:, :, :, 1, :], sv[:, :, :, 0, :], [r_wst[i]], [r_wmix])
            load_cast(l, OFF["qd"], 256, 0, extra=q_extra)
            i, stg = load_w(w_in[l][:, OFF["kd"]:OFF["kd"] + 128], 128, 0)
            for g in range(2):
                for r in range(2):
                    c0 = 512 + g * 128 + r * 64
                    CP("pool", wmix[:, :, c0:c0 + 64], stg[:, :, g * 64:(g + 1) * 64], [r_wst[i]], [r_wmix])
                    sv = swapped(stg[:, :, g * 64:(g + 1) * 64], 16)
                    dvw = swapped(wmix[:, :, c0 + 256:c0 + 256 + 64], 16)
                    CP("pool", dvw[:, :, :, 0, :], sv[:, :, :, 1, :], [r_wst[i]], [r_wmix])
                    CP("pool", dvw[:, :, :, 1, :], sv[:, :, :, 0, :], [r_wst[i]], [r_wmix])
            load_cast(l, OFF["vd"], 128, 1024)
            load_cast(l, OFF["gd"], 256, 1152)
            for c in range(NC):
                ti = load_tab(tab_gqa, c, cw)
                for fc in range(4):
                    isq = fc < 2
                    wc = fc * 128 if isq else 512 + (fc - 2) * 128
                    mb = proj_fm(S, c, wc)
                    mbs = proj_fm(S, c, wc + 256)
                    sq = tmpb_ring.next()
                    ACT(tmpb[sq][:, 0:cw], ps[mb][:, 0:cw], AF.Square, [r_ps[mb]], [r_tmpb[sq]])
                    sb_ = SB_BANKS[fc % 2]
                    MM(ps[sb_][:, 0:cw], bones_b[:], tmpb[sq][:, 0:cw], True, True, [r_const, r_tmpb[sq]], [r_ps[sb_]])
                    rs = tmpf_ring.next()
                    rstd_act(tmpf[rs][:, 0:cw], ps[sb_][:, 0:cw], 64.0, [r_ps[sb_]], [r_tmpf[rs]])
                    if isq:
                        out_ap, wr = qbuf[:, fc, c * cw:(c + 1) * cw], [r_q[fc][c]]
                        g1, g2 = pc[:, PC_GQ:PC_GQ + 1], pc[:, PC_GQS:PC_GQS + 1]
                    else:
                        out_ap, wr = kbuf[:, fc - 2, c * cw:(c + 1) * cw], [r_k[fc - 2][c]]
                        g1, g2 = pc[:, PC_GK:PC_GK + 1], pc[:, PC_GKS:PC_GKS + 1]
                    rope_evac(mb, mbs, ti, slice(0, 128), cw, out_ap, wr, gcol=g1, gscol_=g2,
                              rstd=tmpf[rs][:, 0:cw], rstd_rd=[r_tmpf[rs]])
                for qb in range(cw // 128):
                    t = c * (cw // 128) + qb
                    mb = misc.next()
                    for k in range(8):
                        MM(ps[mb][:, 0:128], h_fm[:, k, t * 128:(t + 1) * 128], wmix[:, k, 1024:1152], k == 0, k == 7,
                           [r_h[t], r_wmix], [r_ps[mb]], sig=(k == 7))
                    CP("dve", vaug[:, t, 0:130].rearrange("p (h d) -> p h d", d=65)[:, :, 0:64],
                       ps[mb][:, 0:128].rearrange("p (h d) -> p h d", d=64), [r_ps[mb]], [r_v[t]])
            groups = []
            for fc in range(2):
                units = []
                for hh in range(2):
                    def qf(c, hh=hh):
                        return qz[:, hh, 0:cw], [r_qz[hh]]

                    def kf(kt, fc=fc):
                        return kbuf[:, fc, kt * 128:(kt + 1) * 128], [r_k[fc][kt * 128 // cw]]

                    def vf(kt, fc=fc):
                        return vaug[:, kt, fc * 65:(fc + 1) * 65], [r_v[kt]]
                    units.append(dict(q=qf, k=kf, v=vf, scale=64 ** -0.5))

                def prec(c, fc=fc):
                    CP("pool", qz[0:64, 0, 0:cw], qbuf[0:64, fc, c * cw:(c + 1) * cw], [r_q[fc][c]], [r_qz[0]])
                    CP("dve", qz[64:128, 1, 0:cw], qbuf[64:128, fc, c * cw:(c + 1) * cw], [r_q[fc][c]], [r_qz[1]])
                gst0, gst1 = {}, {}
                e0 = std_epilogue(s, S, 512, 2 * fc, gst0)
                e1 = std_epilogue(s, S, 512, 2 * fc + 1, gst1)
                gh0 = gate_early(S, 1152 + 2 * fc * 64, 64, gst0)
                gh1 = gate_early(S, 1152 + (2 * fc + 1) * 64, 64, gst1)

                def cst2(c, gh0=gh0, gh1=gh1):
                    gh0(c)
                    gh1(c)

                def ep2(c, obs, e0=e0, e1=e1):
                    e0(c, (obs[0],))
                    e1(c, (obs[1],))
                groups.append(dict(units=units, epilogue=ep2, prec=prec, cstart=cst2))
            MEMSET("pool", qz[64:128, 0, :], 0.0, [r_qz[0]])
            MEMSET("pool", qz[0:64, 1, :], 0.0, [r_qz[1]])
            run_attention(S, groups)

        def mixer_B(l, s):
            S = seq_lens[s]
            cw = min(512, S)
            NC = S // cw
            P.wait_all("pool", [r_wmix])
            for nm, c0 in (("qb", 0), ("kb", 512)):
                def sw_extra(i, stg, p0, pn, c0=c0):
                    d0 = c0 + 256 + p0
                    CP("pool", wmix[:, :, d0:d0 + pn], stg, [r_wst[i]], [r_wmix])
                    sv = stg.rearrange("p k (m j) -> p k m j", j=32)
                    dvw = wmix[:, :, d0:d0 + pn].rearrange("p k (m j) -> p k m j", j=32)
                    CP("pool", dvw[:, :, :, 0:4], sv[:, :, :, 4:8], [r_wst[i]], [r_wmix])
                    CP("pool", dvw[:, :, :, 4:8], sv[:, :, :, 0:4], [r_wst[i]], [r_wmix])
                load_cast(l, OFF[nm], 256, c0, extra=sw_extra)
            load_cast(l, OFF["vb"], 256, 1024)
            load_cast(l, OFF["gb"], 256, 1280)
            for c in range(NC):
                ti = load_tab(tab_diff, c, cw)
                for fc in range(4):
                    isq = fc < 2
                    wc = fc * 128 if isq else 512 + (fc - 2) * 128
                    mb = proj_fm(S, c, wc)
                    mbs = proj_fm(S, c, wc + 256)
                    if isq:
                        out_ap, wr = qbuf[:, fc, c * cw:(c + 1) * cw], [r_q[fc][c]]
                    else:
                        out_ap, wr = kbuf[:, fc - 2, c * cw:(c + 1) * cw], [r_k[fc - 2][c]]
                    rope_evac(mb, mbs, ti, slice(0, 128), cw, out_ap, wr)
                for qb in range(cw // 128):
                    t = c * (cw // 128) + qb
                    mb = misc.next()
                    for k in range(8):
                        MM(ps[mb][:, 0:256], h_fm[:, k, t * 128:(t + 1) * 128], wmix[:, k, 1024:1280], k == 0, k == 7,
                           [r_h[t], r_wmix], [r_ps[mb]], sig=(k == 7))
                    CP("dve", vaug[:, t, 0:260].rearrange("p (h d) -> p h d", d=65)[:, :, 0:64],
                       ps[mb][:, 0:256].rearrange("p (h d) -> p h d", d=64), [r_ps[mb]], [r_v[t]])
            groups = []
            for h in range(4):
                fc = h // 2
                units = []
                for j in range(2):
                    b_ = (h % 2) * 2 + j

                    def qf(c, b_=b_):
                        return qz[:, b_, 0:cw], [r_qz[b_]]

                    def kf(kt, fc=fc):
                        return kbuf[:, fc, kt * 128:(kt + 1) * 128], [r_k[fc][kt * 128 // cw]]

                    def vf(kt, h=h):
                        return vaug[:, kt, h * 65:(h + 1) * 65], [r_v[kt]]
                    units.append(dict(q=qf, k=kf, v=vf, scale=32 ** -0.5))

                def prec(c, h=h, fc=fc):
                    for j in range(2):
                        b_ = (h % 2) * 2 + j
                        rs_ = slice(b_ * 32, (b_ + 1) * 32)
                        CP("pool" if j == 0 else "dve", qz[rs_, b_, 0:cw], qbuf[rs_, fc, c * cw:(c + 1) * cw], [r_q[fc][c]], [r_qz[b_]])

                gst = {}

                def ep(c, obs, h=h, gst=gst):
                    nqb = cw // 128
                    n = nqb * 64
                    oc1 = o_copy(obs[0], cw)
                    oc2 = o_copy(obs[1], cw)
                    mb1, ov1 = o_transposed(oc1, cw)
                    mb2, ov2 = o_transposed(oc2, cw)
                    st_ = {}

                    def partB1():
                        RCP(small[:, 24:24 + nqb], ov1[:, :, 64], [r_ps[mb1]], [r_small])
                        RCP(small[:, 28:28 + nqb], ov2[:, :, 64], [r_ps[mb2]], [r_small])
                        TS("dve", small[:, 28:28 + nqb], small[:, 28:28 + nqb], lamc[:, 3:4], None, ALU.mult, None, [r_small, r_wa], [r_small])
                        a = tmpf_ring.next()
                        b = tmpf_ring.next()
                        st_["a"], st_["b"] = a, b
                        av = tmpf[a][:, 0:n].rearrange("p (q d) -> p q d", d=64)
                        bv = tmpf[b][:, 0:n].rearrange("p (q d) -> p q d", d=64)
                        TT("dve", av, ov1[:, :, 0:64], small[:, 24:24 + nqb].unsqueeze(2).broadcast_to([128, nqb, 64]), ALU.mult,
                           [r_ps[mb1], r_small], [r_tmpf[a]])
                        TT("dve", bv, ov2[:, :, 0:64], small[:, 28:28 + nqb].unsqueeze(2).broadcast_to([128, nqb, 64]), ALU.mult,
                           [r_ps[mb2], r_small], [r_tmpf[b]])
                        TT("dve", tmpf[a][:, 0:n], tmpf[a][:, 0:n], tmpf[b][:, 0:n], ALU.add, [r_tmpf[a], r_tmpf[b]], [r_tmpf[a]])
                        TT("dve", tmpf[b][:, 0:n], tmpf[a][:, 0:n], tmpf[a][:, 0:n], ALU.mult, [r_tmpf[a]], [r_tmpf[b]])
                        P.op("dve", lambda e: e.tensor_reduce(out=small[:, 32:32 + nqb], in_=bv, axis=AX.X, op=ALU.add),
                             reads=[r_tmpf[b]], writes=[r_small])

                    def partB2():
                        a = st_["a"]
                        av = tmpf[a][:, 0:n].rearrange("p (q d) -> p q d", d=64)
                        rstd_act(small[:, 36:36 + nqb], small[:, 32:32 + nqb], 64.0, [r_small], [r_small])
                        TT("dve", av, av, small[:, 36:36 + nqb].unsqueeze(2).broadcast_to([128, nqb, 64]), ALU.mult,
                           [r_tmpf[a], r_small], [r_tmpf[a]])
                        TT("pool", av, av, subln_bc[:, 0:nqb, :], ALU.mult, [r_tmpf[a], r_wa], [r_tmpf[a]])
                        sg = gst[c]
                        oi = og_ring.next()
                        TT("pool", ogs[oi][:, 0:n], tmpf[a][:, 0:n], tmpg[sg][:, 0:n], ALU.mult, [r_tmpf[a], r_tmpg[sg]], [r_og[oi]])
                        store_o(s, c, cw, 256 + h * 64, oi)
                    DEFER[0](partB1, 1)
                    DEFER[0](partB2, 5)
                groups.append(dict(units=units, epilogue=ep, prec=prec, cstart=gate_early(S, 1280 + h * 64, 64, gst)))
            for b_ in range(4):
                MEMSET("pool", qz[:, b_, :], 0.0, [r_qz[b_]])
            run_attention(S, groups)

        def mixer_A(l, s):
            S = seq_lens[s]
            cw = min(512, S)
            NC = S // cw
            nslots = 2 if 2 * S <= Smax else 1
            P.wait_all("pool", [r_wmix])

            def kr_extra(i, stg, p0, pn):
                if p0 == 256:
                    CP("pool", wmix[:, :, 352:368], stg[:, :, 80:96], [r_wst[i]], [r_wmix])
                    CP("pool", wmix[:, :, 368:384], stg[:, :, 64:80], [r_wst[i]], [r_wmix])
            load_cast(l, 0, 352, 0, extra=kr_extra)
            load_cast(l, OFF["ga"], 256, 384)
            R64 = slice(64, 96)
            qn0 = qbuf[:, 1, :]
            qn1 = kbuf[:, 1, :]

            def qs_(slot, c):
                return qbuf[:, 0, slot * S + c * cw:slot * S + (c + 1) * cw], r_q[0][slot * NC + c]

            def ks_(slot, c):
                return kbuf[:, 0, slot * S + c * cw:slot * S + (c + 1) * cw], r_k[0][slot * NC + c]

            for c in range(NC):
                tl = [r_h[t] for t in range(c * cw // 128, (c + 1) * cw // 128)]
                csl = slice(c * cw, (c + 1) * cw)
                mbq = []
                sqs = []
                for fc, rows in ((0, 128), (1, 64)):
                    mb = proj_fm(S, c, fc * 128, rows=rows)
                    sq = tmpb_ring.next()
                    ACT(tmpb[sq][0:rows, 0:cw], ps[mb][0:rows, 0:cw], AF.Square, [r_ps[mb]], [r_tmpb[sq]])
                    mbq.append(mb)
                    sqs.append(sq)
                sb_ = SB_BANKS[0]
                MM(ps[sb_][:, 0:cw], ones_b[:, :], tmpb[sqs[0]][:, 0:cw], True, False, [r_const, r_tmpb[sqs[0]]], [r_ps[sb_]], sig=False)
                MM(ps[sb_][:, 0:cw], ones_b[0:64, :], tmpb[sqs[1]][0:64, 0:cw], False, True, [r_const, r_tmpb[sqs[1]]], [r_ps[sb_]])
                rs = tmpf_ring.next()
                rstd_act(tmpf[rs][:, 0:cw], ps[sb_][:, 0:cw], 192.0, [r_ps[sb_]], [r_tmpf[rs]])
                TT("dve", qn0[:, csl], ps[mbq[0]][:, 0:cw], tmpf[rs][:, 0:cw], ALU.mult,
                   [r_ps[mbq[0]], r_tmpf[rs]], [r_q[1][c]])
                TT("dve", qn1[0:64, csl], ps[mbq[1]][0:64, 0:cw], tmpf[rs][0:64, 0:cw], ALU.mult,
                   [r_ps[mbq[1]], r_tmpf[rs]], [r_k[1][c]])
                mb = proj_fm(S, c, 192)
                sq = tmpb_ring.next()
                ACT(tmpb[sq][:, 0:cw], ps[mb][:, 0:cw], AF.Square, [r_ps[mb]], [r_tmpb[sq]])
                sb_ = SB_BANKS[1]
                MM(ps[sb_][:, 0:cw], ones_b[:, :], tmpb[sq][:, 0:cw], True, True, [r_const, r_tmpb[sq]], [r_ps[sb_]])
                rs = tmpf_ring.next()
                rstd_act(tmpf[rs][:, 0:cw], ps[sb_][:, 0:cw], 128.0, [r_ps[sb_]], [r_tmpf[rs]])
                TT("dve", lat[:, csl], ps[mb][:, 0:cw], tmpf[rs][:, 0:cw], ALU.mult,
                   [r_ps[mb], r_tmpf[rs]], [r_lat[c]])
                for qb in range(cw // 128):
                    t = c * (cw // 128) + qb
                    vb_ = misc.next()
                    MM(ps[vb_][:, 0:256], lat[:, t * 128:(t + 1) * 128], wukv[:, 256:512], True, True, [r_lat[c], r_wa], [r_ps[vb_]])
                    CP("dve", vaug[:, t, 0:260].rearrange("p (h d) -> p h d", d=65)[:, :, 0:64],
                       ps[vb_][:, 0:256].rearrange("p (h d) -> p h d", d=64), [r_ps[vb_]], [r_v[t]])
                ti = load_tab(tab_mla, c, cw, rows=R64)
                mbk = misc.next()
                mbks = misc.next()
                for k in range(8):
                    MM(ps[mbk][64:96, 0:cw], wmix[:, k, 320:352], h_fm[:, k, c * cw:(c + 1) * cw], k == 0, k == 7,
                       [r_wmix] + tl, [r_ps[mbk]], sig=(k == 7))
                for k in range(8):
                    MM(ps[mbks][64:96, 0:cw], wmix[:, k, 352:384], h_fm[:, k, c * cw:(c + 1) * cw], k == 0, k == 7,
                       [r_wmix] + tl, [r_ps[mbks]], sig=(k == 7))
                k0, rk0 = ks_(0, c)
                rope_evac(mbk, mbks, ti, R64, cw, k0[64:96, :], [rk0])
                if nslots == 2:
                    k1, rk1 = ks_(1, c)
                    CP("pool", k1[64:96, :], k0[64:96, :], [rk0], [rk1])

            def head_proj_q(h, c):
                slot = h % nslots
                csl = slice(c * cw, (c + 1) * cw)
                qa_, rq_ = qs_(slot, c)
                mb = misc.next()
                MM(ps[mb][0:96, 0:cw], wuq[:, 0, h * 96:(h + 1) * 96], qn0[:, csl], True, False,
                   [r_wa, r_q[1][c]], [r_ps[mb]], sig=False)
                MM(ps[mb][0:96, 0:cw], wuq[0:64, 1, h * 96:(h + 1) * 96], qn1[0:64, csl], False, True,
                   [r_wa, r_k[1][c]], [r_ps[mb]])
                mbs = misc.next()
                MM(ps[mbs][0:96, 0:cw], wuqs[:, 0, h * 96:(h + 1) * 96], qn0[:, csl], True, False,
                   [r_wa, r_q[1][c]], [r_ps[mbs]], sig=False)
                MM(ps[mbs][0:96, 0:cw], wuqs[0:64, 1, h * 96:(h + 1) * 96], qn1[0:64, csl], False, True,
                   [r_wa, r_k[1][c]], [r_ps[mbs]])
                CP("dve", qa_[0:64, :], ps[mb][0:64, 0:cw], [r_ps[mb]], [rq_])
                ti = load_tab(tab_mla, c, cw, rows=R64)
                rope_evac(mb, mbs, ti, R64, cw, qa_[64:96, :], [rq_])

            def head_proj_k(h, c):
                slot = h % nslots
                csl = slice(c * cw, (c + 1) * cw)
                ka_, rk_ = ks_(slot, c)
                mbk = misc.next()
                MM(ps[mbk][0:64, 0:cw], wukv[:, h * 64:(h + 1) * 64], lat[:, csl], True, True,
                   [r_wa, r_lat[c]], [r_ps[mbk]])
                CP("dve", ka_[0:64, :], ps[mbk][0:64, 0:cw], [r_ps[mbk]], [rk_])

            def head_proj(h):
                for c in range(NC):
                    head_proj_q(h, c)
                    head_proj_k(h, c)

            def head_start(h):
                for c in range(NC):
                    head_proj_k(h, c)
                head_proj_q(h, 0)

            groups = []
            for h in range(4):
                slot = h % nslots

                def qf(c, slot=slot):
                    a_, r_ = qs_(slot, c)
                    return a_[0:96, :], [r_]

                def kf(kt, slot=slot):
                    a_, r_ = ks_(slot, kt * 128 // cw)
                    o_ = (kt * 128) % cw
                    return a_[0:96, o_:o_ + 128], [r_]

                def vf(kt, h=h):
                    return vaug[:, kt, h * 65:(h + 1) * 65], [r_v[kt]]
                u = dict(q=qf, k=kf, v=vf, scale=96 ** -0.5)
                pre = None
                prec = None
                if nslots == 2:
                    if h + 1 < 4:
                        pre = (lambda hh=h + 1: head_proj(hh))
                else:
                    if h >= 1:
                        pre = (lambda hh=h: head_start(hh))
                gst = {}
                ge_ = gate_early(S, 384 + h * 64, 64, gst)

                spc_ = (S // 128 + 1) // 2
                if nslots == 1 and spc_ < 3:
                    prec = (lambda c, hh=h: head_proj_q(hh, c + 1) if c + 1 < NC else None)

                def cst(c, hh=h, ge_=ge_):
                    ge_(c)
                    if nslots == 1 and spc_ >= 3 and c + 1 < NC:
                        DEFER[0](lambda: head_proj_q(hh, c + 1), min(4, spc_ - 2))
                groups.append(dict(units=[u], epilogue=std_epilogue(s, S, 0, h, gst), pre=pre, prec=prec, cstart=cst))
            if nslots == 2:
                head_proj(0)
            else:
                head_start(0)
            run_attention(S, groups)

        def mixer_C(l, s):
            S = seq_lens[s]
            cw = min(512, S)
            NC = S // cw
            P.wait_all("pool", [r_wmix])
            load_cast(l, OFF["xc"], 256, 0)
            load_cast(l, OFF["gc"], 256, 256)
            xcv = vaug[:].rearrange("p t d -> p (t d)").bitcast(F32)
            xconv = kbuf[:].rearrange("p a s -> p (a s)").bitcast(F32)
            hsum = qbuf[:].rearrange("p a s -> p (a s)").bitcast(F32)
            xcb = lat[:, :]
            allr = [r for rr in (r_lat, r_v) for r in rr] + [r for sl in (r_q, r_k) for rr in sl for r in rr] + r_qz + r_pt
            r_xc, r_xconv, r_hsum, r_xcb = Res("xc"), Res("xconv"), Res("hsum"), Res("xcb")
            qzf = qz[:].rearrange("p a n -> p (a n)").bitcast(F32)
            ptf = ptbuf[:].rearrange("p a n -> p (a n)").bitcast(F32)
            tmpfL = list(tmpf) + [qzf[:, 0:512], qzf[:, 512:1024], ptf[:, 0:512], ptf[:, 512:1024]]
            r_extra = [Res(f"lrux{i_}") for i_ in range(4)]
            r_tmpfL = list(r_tmpf) + r_extra
            tmpfL_ring = Ring(list(range(8)))
            for e_ in ("dve", "pool", "act", "pe"):
                P.wait_all(e_, allr)
            for cc in range(2):
                MEMSET("dve", xcv[:, 0:2], 0.0, [r_xc])
                MEMSET("dve", xcv[:, S + 2:S + 4], 0.0, [r_xc])
                for c in range(NC):
                    mb = proj_fm(S, c, cc * 128)
                    if c % 2 == 0:
                        CP("dve", xcv[:, 2 + c * cw:2 + (c + 1) * cw], ps[mb][:, 0:cw], [r_ps[mb]], [r_xc])
                    else:
                        ACT(xcv[:, 2 + c * cw:2 + (c + 1) * cw], ps[mb][:, 0:cw], AF.Copy, [r_ps[mb]], [r_xc])
                cwc = lambda j: pc[:, PC_CW + cc * 4 + j:PC_CW + cc * 4 + j + 1]
                for c in range(NC):
                    sl = slice(c * cw, (c + 1) * cw)
                    TS("dve", xconv[:, sl], xcv[:, c * cw:(c + 1) * cw], cwc(0), pc[:, PC_CB + cc:PC_CB + cc + 1], ALU.mult, ALU.add,
                       [r_xc, r_pc], [r_xconv])
                    for j in range(1, 4):
                        STT(xconv[:, sl], xcv[:, j + c * cw:j + (c + 1) * cw], cwc(j), xconv[:, sl], ALU.mult, ALU.add,
                            [r_xc, r_pc, r_xconv], [r_xconv])
                    ACT(xcb[:, sl], xconv[:, sl], AF.Copy, [r_xconv], [r_xcb])
                for d_ in range(2):
                    order = range(NC) if d_ == 0 else range(NC - 1, -1, -1)
                    prev = None
                    for c in order:
                        sl = slice(c * cw, (c + 1) * cw)
                        mba = misc.next()
                        MM(ps[mba][:, 0:cw], lruw[:, 0, d_, cc, :], xcb[:, sl], True, True, [r_wa, r_xcb], [r_ps[mba]])
                        mbx = misc.next()
                        MM(ps[mbx][:, 0:cw], lruw[:, 1, d_, cc, :], xcb[:, sl], True, True, [r_wa, r_xcb], [r_ps[mbx]])
                        ra, ri = tmpfL_ring.next(), tmpfL_ring.next()
                        ci = d_ * 2 + cc
                        ACT(tmpfL[ra][:, 0:cw], ps[mba][:, 0:cw], AF.Exp, [r_ps[mba], r_wa], [r_tmpfL[ra]], scale=-1.0, bias=lsp[:, 4 + ci:5 + ci])
                        ACT(tmpfL[ri][:, 0:cw], ps[mbx][:, 0:cw], AF.Exp, [r_ps[mbx], r_wa], [r_tmpfL[ri]], scale=-1.0, bias=lsp[:, 8 + ci:9 + ci])
                        ACT(tmpfL[ra][:, 0:cw], tmpfL[ra][:, 0:cw], AF.Ln, [r_tmpfL[ra]], [r_tmpfL[ra]], bias=1.0)
                        ACT(tmpfL[ri][:, 0:cw], tmpfL[ri][:, 0:cw], AF.Ln, [r_tmpfL[ri]], [r_tmpfL[ri]], bias=1.0)
                        ACT(tmpfL[ra][:, 0:cw], tmpfL[ra][:, 0:cw], AF.Exp, [r_tmpfL[ra]], [r_tmpfL[ra]], scale=-1.0)
                        ACT(tmpfL[ri][:, 0:cw], tmpfL[ri][:, 0:cw], AF.Exp, [r_tmpfL[ri]], [r_tmpfL[ri]], scale=-1.0)
                        ACT(tmpfL[ra][:, 0:cw], tmpfL[ra][:, 0:cw], AF.Exp, [r_tmpfL[ra], r_wa], [r_tmpfL[ra]], scale=lsp[:, ci:ci + 1])
                        rg = tmpfL_ring.next()
                        TT("dve", tmpfL[rg][:, 0:cw], tmpfL[ra][:, 0:cw], tmpfL[ra][:, 0:cw], ALU.mult, [r_tmpfL[ra]], [r_tmpfL[rg]])
                        TS("dve", tmpfL[rg][:, 0:cw], tmpfL[rg][:, 0:cw], -1.0, 1.0, ALU.mult, ALU.add, [r_tmpfL[rg]], [r_tmpfL[rg]])
                        ACT(tmpfL[rg][:, 0:cw], tmpfL[rg][:, 0:cw], AF.Ln, [r_tmpfL[rg]], [r_tmpfL[rg]])
                        ACT(tmpfL[rg][:, 0:cw], tmpfL[rg][:, 0:cw], AF.Exp, [r_tmpfL[rg]], [r_tmpfL[rg]], scale=0.5)
                        TT("pool", tmpfL[ri][:, 0:cw], tmpfL[ri][:, 0:cw], xconv[:, sl], ALU.mult, [r_tmpfL[ri], r_xconv], [r_tmpfL[ri]])
                        TT("dve", tmpfL[rg][:, 0:cw], tmpfL[rg][:, 0:cw], tmpfL[ri][:, 0:cw], ALU.mult, [r_tmpfL[rg], r_tmpfL[ri]], [r_tmpfL[rg]])
                        init = 0.0 if prev is None else prev
                        if d_ == 0:
                            P.op("dve", lambda e, ra=ra, rg=rg, init=init, ri=ri: e.tensor_tensor_scan(
                                out=tmpfL[ri][:, 0:cw], data0=tmpfL[ra][:, 0:cw], data1=tmpfL[rg][:, 0:cw], initial=init,
                                op0=ALU.mult, op1=ALU.add), reads=[r_tmpfL[ra], r_tmpfL[rg], r_small], writes=[r_tmpfL[ri]])
                            CP("dve", small[:, 40:41], tmpfL[ri][:, cw - 1:cw], [r_tmpfL[ri]], [r_small])
                            prev = small[:, 40:41]
                            CP("pool", hsum[:, sl], tmpfL[ri][:, 0:cw], [r_tmpfL[ri]], [r_hsum])
                        else:
                            P.op("dve", lambda e, ra=ra, rg=rg, init=init, ri=ri: e.tensor_tensor_scan(
                                out=tmpfL[ri][:, 0:cw][:, ::-1], data0=tmpfL[ra][:, 0:cw][:, ::-1], data1=tmpfL[rg][:, 0:cw][:, ::-1],
                                initial=init, op0=ALU.mult, op1=ALU.add), reads=[r_tmpfL[ra], r_tmpfL[rg], r_small], writes=[r_tmpfL[ri]])
                            CP("dve", small[:, 41:42], tmpfL[ri][:, 0:1], [r_tmpfL[ri]], [r_small])
                            prev = small[:, 41:42]
                            TT("pool", hsum[:, sl], hsum[:, sl], tmpfL[ri][:, 0:cw], ALU.add, [r_hsum, r_tmpfL[ri]], [r_hsum])
                for c in range(NC):
                    sl = slice(c * cw, (c + 1) * cw)
                    gb = proj_fm(S, c, 256 + cc * 128)
                    a = tmpfL_ring.next()
                    ACT(tmpfL[a][:, 0:cw], ps[gb][:, 0:cw], AF.Exp, [r_ps[gb]], [r_tmpfL[a]], scale=-1.0)
                    ACT(tmpfL[a][:, 0:cw], tmpfL[a][:, 0:cw], AF.Ln, [r_tmpfL[a]], [r_tmpfL[a]], bias=1.0)
                    ACT(tmpfL[a][:, 0:cw], tmpfL[a][:, 0:cw], AF.Exp, [r_tmpfL[a]], [r_tmpfL[a]], scale=-1.0)
                    TT("dve", tmpfL[a][:, 0:cw], ps[gb][:, 0:cw], tmpfL[a][:, 0:cw], ALU.mult, [r_ps[gb], r_tmpfL[a]], [r_tmpfL[a]])
                    b = tmpb_ring.next()
                    TT("dve", tmpb[b][:, 0:cw], tmpfL[a][:, 0:cw], hsum[:, sl], ALU.mult, [r_tmpfL[a], r_hsum], [r_tmpb[b]])
                    P.dma("pool", f"oc{b}", ocscr[s][cc * 128:(cc + 1) * 128, sl], tmpb[b][:, 0:cw], reads=[r_tmpb[b]], writes=[r_ocscr[s]])
            MEMSET("pool", vaug[:], 1.0, [r_xcb, r_xc, r_xconv, r_hsum])
            for e_ in ("dve", "pool", "act", "pe"):
                P.wait_all(e_, [r_xcb, r_xc, r_xconv, r_hsum] + r_extra)

        def phase3(l, s, last):
            S = seq_lens[s]
            P.wait_all("pool", [r_wmix])
            for k in range(8):
                i = wst_ring.next()
                LOAD(f"wst{i}", wst[i][:, 0:1024], w_out[l][k * 128:(k + 1) * 128, :], [], [r_wst[i]])
                TT("dve", wmix[:, k, 0:1024], wst[i][:, 0:1024], gate_bc[:, :], ALU.mult, [r_wst[i], r_gate], [r_wmix])
            src = x_in[s] if l == 0 else xres[s]
            misc.items = [6, 7, 0, 1, 2, 3]
            if last:
                LOAD("c3", fn_bc, fnorm.partition_broadcast(128), [], [r_mod])
            for t in range(S // 128):
                jb = t % 2
                ot = tmpf[jb][:].bitcast(BF16)[:, 0:768]
                ofm = tmpf[2 + jb][:].bitcast(BF16).rearrange("p (k n) -> p k n", n=128)
                r_ot, r_ofm = r_tmpf[jb], r_tmpf[2 + jb]
                P.dma("sp", f"otl{jb}", ot, oscr[s][t * 128:(t + 1) * 128, :], reads=[r_oscr[s][t]], writes=[r_ot])
                P.dma("sp", f"ofl{jb}", ofm[:, 4:6, :], ocscr[s][:, t * 128:(t + 1) * 128].rearrange("(c p) n -> p c n", p=128),
                      reads=[r_ocscr[s]], writes=[r_ofm])
                mb = misc.next()
                pT = ps[mb][:].bitcast(BF16)
                for j in range(6):
                    TR(pT[:, j * 128:(j + 1) * 128], ot[:, j * 128:(j + 1) * 128], ident_b[:], [r_ot, r_const], [r_ps[mb]], sig=(j == 5))
                CP("dve", ofm[:, 0:4, :], pT[:, 0:512].rearrange("p (j n) -> p j n", n=128), [r_ps[mb]], [r_ofm])
                CP("dve", ofm[:, 6:8, :], pT[:, 512:768].rearrange("p (j n) -> p j n", n=128), [r_ps[mb]], [r_ofm])
                xi = xt_ring.next()
                rd = [] if l == 0 else [r_xres[s][t]]
                LOAD(f"xt{xi}", xt[xi][:], src[t * 128:(t + 1) * 128, :], rd, [r_xt[xi]])
                for n in range(2):
                    yb = misc.next()
                    for k in range(8):
                        MM(ps[yb][:, :], ofm[:, k, :], wmix[:, k, n * 512:(n + 1) * 512], k == 0, k == 7, [r_ofm, r_wmix], [r_ps[yb]], sig=(k == 7))
                    TT("dve", xt[xi][:, n * 512:(n + 1) * 512], xt[xi][:, n * 512:(n + 1) * 512], ps[yb][:, :], ALU.add,
                       [r_ps[yb], r_xt[xi]], [r_xt[xi]])
                if not last:
                    P.dma("pool", f"xst{xi}", xres[s][t * 128:(t + 1) * 128, :], xt[xi][:], reads=[r_xt[xi]], writes=[r_xres[s][t]])
                else:
                    ACT(xn[:], xt[xi][:], AF.Square, [r_xt[xi]], [r_pt[0], r_pt[1], r_small], accum=small[:, 17:18])
                    rstd_act(small[:, 17:18], small[:, 17:18], float(D), [r_small], [r_small])
                    STT(xt[xi][:], xt[xi][:], small[:, 17:18], fn_bc, ALU.mult, ALU.mult, [r_xt[xi], r_small, r_mod], [r_xt[xi]])
                    P.dma("pool", f"xst{xi}", y_out[s][t * 128:(t + 1) * 128, :], xt[xi][:], reads=[r_xt[xi]], writes=[r_y])
            misc.items = [6, 7]

        try:
          for l in range(depth):
            layer_prep(l)
            for s in range(nseq):
                phase0(l, s)
                phase1(l, s)
                if _os.environ.get("KSTOP") == "p1":
                    raise StopIteration
                if "A" in mixers:
                    mixer_A(l, s)
                if "B" in mixers:
                    mixer_B(l, s)
                if "C" in mixers:
                    mixer_C(l, s)
                if "D" in mixers:
                    mixer_D(l, s)
                phase3(l, s, l == depth - 1)
        except StopIteration:
            pass
        allres = [r_y] + [r for rr in r_xres for r in rr] + [r for rr in r_oscr for r in rr] + r_ocscr
        P.wait_all("sp", allres)
        P.emit()
    return nc, P


def _rope_tables(Smax):
    pos = np.arange(Smax, dtype=np.float32)

    def cs(p, theta, half):
        inv = np.power(np.float32(theta), -np.arange(half, dtype=np.float32) / np.float32(half)).astype(np.float32)
        ang = (p[:, None] * inv[None, :]).astype(np.float32)
        return np.cos(ang).astype(np.float32).T, np.sin(ang).astype(np.float32).T

    c, s_ = cs(pos, 10000.0, 16)
    tab_mla = np.zeros((2, 32, Smax), np.float32)
    tab_mla[0, 0:16] = c
    tab_mla[0, 16:32] = c
    tab_mla[1, 0:16] = -s_
    tab_mla[1, 16:32] = s_
    c, s_ = cs(pos, 500000.0, 4)
    blk_c = np.ones((32, Smax), np.float32)
    blk_s = np.zeros((32, Smax), np.float32)
    blk_c[0:4] = c
    blk_c[4:8] = c
    blk_s[0:4] = -s_
    blk_s[4:8] = s_
    tab_diff = np.stack([np.tile(blk_c, (4, 1)), np.tile(blk_s, (4, 1))])
    row = np.floor(pos / 64.0).astype(np.float32)
    col = (pos - row * 64.0).astype(np.float32)
    cr, sr = cs(row, 10000.0, 16)
    cc, sc_ = cs(col, 10000.0, 16)
    blk_c = np.concatenate([cr, cr, cc, cc], 0)
    blk_s = np.concatenate([-sr, sr, -sc_, sc_], 0)
    tab_gqa = np.stack([np.tile(blk_c, (2, 1)), np.tile(blk_s, (2, 1))])
    return tab_mla, np.ascontiguousarray(tab_diff), np.ascontiguousarray(tab_gqa)


def _host_layout(inp, depth):
    f = np.float32
    pcol = np.zeros((depth, 128, NPC), f)
    prow = np.zeros((depth, NPR), f)
    p = np.arange(128)
    partner = np.where((p % 32) < 16, p + 16, p - 16) % 64
    for l in range(depth):
        pcol[l, :, PC_NG:PC_NG + 8] = inp["norm_g"][l].reshape(8, 128).T
        pcol[l, :, PC_ADAB:PC_ADAB + 24] = inp["ada_b"][l].reshape(24, 128).T
        pcol[l, :, PC_QN] = inp["mla_q_norm"][l][0:128]
        pcol[l, 0:64, PC_QN + 1] = inp["mla_q_norm"][l][128:192]
        pcol[l, :, PC_KVN] = inp["mla_kv_norm"][l]
        pcol[l, :, PC_GQ] = inp["gqa_q_norm"][l][p % 64]
        pcol[l, :, PC_GK] = inp["gqa_k_norm"][l][p % 64]
        pcol[l, :, PC_GQS] = inp["gqa_q_norm"][l][partner]
        pcol[l, :, PC_GKS] = inp["gqa_k_norm"][l][partner]
        pcol[l, :, PC_CB:PC_CB + 2] = inp["lru_conv_b"][l].reshape(2, 128).T
        pcol[l, :, PC_CW:PC_CW + 8] = inp["lru_conv_w"][l].reshape(4, 2, 128).transpose(2, 1, 0).reshape(128, 8)
        pcol[l, :, PC_BA:PC_BA + 4] = inp["lru_ba"][l].reshape(2, 2, 128).transpose(2, 0, 1).reshape(128, 4)
        pcol[l, :, PC_BX:PC_BX + 4] = inp["lru_bx"][l].reshape(2, 2, 128).transpose(2, 0, 1).reshape(128, 4)
        pcol[l, :, PC_LAM:PC_LAM + 4] = inp["lru_lambda"][l].reshape(2, 2, 128).transpose(2, 0, 1).reshape(128, 4)
        prow[l, PR_LAM:PR_LAM + 128] = inp["diff_lambda"][l].reshape(-1)
        prow[l, PR_SUB:PR_SUB + 64] = inp["diff_subln"][l]
    lru_w = np.ascontiguousarray(np.stack([inp["lru_wa"], inp["lru_wx"]], axis=1)).astype(f)
    return pcol, prow, lru_w


_CACHE = {}


def run(inputs, seqs_per_core, n_cores, mixers="ABCD"):
    depth = inputs["w_in"].shape[0]
    f = np.float32
    xs = {"p": np.asarray(inputs["x_prompt"], f), "s": np.asarray(inputs["x_sample"], f)}
    cs_ = {"p": np.asarray(inputs["c_prompt"], f), "s": np.asarray(inputs["c_sample"], f)}
    seq_lens = [xs[w].shape[1] for (w, _) in seqs_per_core[0]]
    Smax = max(seq_lens)
    key = (tuple(seq_lens), depth, mixers)
    if key not in _CACHE:
        _CACHE[key] = build(seq_lens, depth, mixers)
    nc, _ = _CACHE[key]
    pcol, prow, lru_w = _host_layout({k: np.asarray(v, f) for k, v in inputs.items()}, depth)
    tab_mla, tab_diff, tab_gqa = _rope_tables(Smax)
    shared = {
        "ada_w": np.asarray(inputs["ada_w"], f), "ada_b": np.asarray(inputs["ada_b"], f),
        "w_in": np.asarray(inputs["w_in"], f), "w_out": np.asarray(inputs["w_out"], f),
        "mla_w_uq": np.asarray(inputs["mla_w_uq"], f), "mla_w_ukv": np.asarray(inputs["mla_w_ukv"], f),
        "lru_w": lru_w, "pcol": pcol, "prow": prow, "final_norm": np.asarray(inputs["final_norm"], f).reshape(1, -1),
        "ident_b": np.eye(128, dtype=f).astype(ml_dtypes.bfloat16), "ident_f": np.eye(128, dtype=f),
        "tab_mla": tab_mla, "tab_diff": tab_diff, "tab_gqa": tab_gqa,
    }
    in_maps = []
    for core in range(n_cores):
        m = dict(shared)
        cf = np.zeros((len(seq_lens), 128, 8), f)
        for i, (w, idx) in enumerate(seqs_per_core[core]):
            m[f"x{i}"] = np.ascontiguousarray(xs[w][idx])
            cf[i] = cs_[w][idx].reshape(8, 128).T
        m["cfm"] = cf
        in_maps.append(m)
    res = run_bass_kernel_spmd(nc, in_maps, core_ids=list(range(n_cores)))
    yp = np.zeros_like(xs["p"])
    ys = np.zeros_like(xs["s"])
    out = {"p": yp, "s": ys}
    for core in range(n_cores):
        for i, (w, idx) in enumerate(seqs_per_core[core]):
            out[w][idx] = res.results[core][f"y{i}"]
    return yp, ys


def kernel(x_prompt, x_sample, c_prompt, c_sample, **weights):
    inputs = dict(x_prompt=x_prompt, x_sample=x_sample, c_prompt=c_prompt, c_sample=c_sample, **weights)
    n_cores = 8
    bp = x_prompt.shape[0] // n_cores
    bs = x_sample.shape[0] // n_cores
    spc = [[("p", c * bp + j) for j in range(bp)] + [("s", c * bs + j) for j in range(bs)] for c in range(n_cores)]
    return run(inputs, spc, n_cores)
```
